# Optimizing a Trainium2 kernel written in Bass

```python
import jax, jax.numpy as jnp
from jax import lax
import numpy as np

D_MODEL = 1024
BATCH = 16
SEQ = 2048
DEPTH = 4
DEC_BATCH = 8
DEC_SEQ = 64
PAST_LEN = 1024

CHUNK = 64
N_EVEN = (DEPTH + 1) // 2
N_ODD = DEPTH // 2
EPS = 1e-6
ATTN_DIM = D_MODEL // 2
N_HEADS_A = 8
HEAD_DIM = ATTN_DIM // N_HEADS_A
N_KV_A = 2
GQA_REP = N_HEADS_A // N_KV_A
KV_DIM = N_KV_A * HEAD_DIM
N_IDX_HEADS = 8
IDX_DIM = 64
TOPK_MAX = 256
Q_BLOCK = 128
CONV_DIM = D_MODEL // 2
CONV_W = 3
SGU_DIM = D_MODEL
SGU_CHUNK = 128
N_SGU_GROUPS = 8
SGU_GROUP_CH = SGU_DIM // N_SGU_GROUPS

EVEN_SPLITS = (ATTN_DIM, KV_DIM, KV_DIM, ATTN_DIM, N_IDX_HEADS * IDX_DIM, IDX_DIM, N_IDX_HEADS,
               CONV_DIM, CONV_DIM, CONV_DIM, CONV_DIM)
EVEN_IN = sum(EVEN_SPLITS)
ODD_IN = 3 * SGU_DIM

kernel_name = "dsa_shortconv_sgu_streaming_step"


def _split(t, sizes):
    cuts = [int(i) for i in np.cumsum(sizes)[:-1]]
    return jnp.split(t, cuts, axis=-1)


def _rmsnorm(x, g):
    xf = x.astype(jnp.float32)
    y = xf * lax.rsqrt(jnp.mean(xf * xf, axis=-1, keepdims=True) + EPS)
    return (y * g.astype(jnp.float32)).astype(x.dtype)


def _layernorm(x, g, b):
    xf = x.astype(jnp.float32)
    mu = jnp.mean(xf, axis=-1, keepdims=True)
    xc = xf - mu
    y = xc * lax.rsqrt(jnp.mean(xc * xc, axis=-1, keepdims=True) + EPS)
    return (y * g.astype(jnp.float32) + b.astype(jnp.float32)).astype(x.dtype)


def _dsa_block(q, qi, wi, qpos, k, v, ki, kpos, topk):
    rel = jax.nn.relu(jnp.einsum('bqhd,bsd->bqhs', qi, ki)).astype(jnp.float32)
    score = jnp.einsum('bqh,bqhs->bqs', wi.astype(jnp.float32), rel)
    adm = (kpos[None, :] // CHUNK) <= (qpos[:, None] // CHUNK)
    score = jnp.where(adm[None], score, -jnp.inf)
    top_val, top_idx = lax.top_k(score, topk)
    valid = jnp.isfinite(top_val)
    gather = jax.vmap(lambda a, i: a[i])
    kg = gather(k, top_idx)
    vg = gather(v, top_idx)
    logits = jnp.einsum('bqgrd,bqkgd->bqgrk', q, kg).astype(jnp.float32) * (HEAD_DIM ** -0.5)
    logits = jnp.where(valid[:, :, None, None, :], logits, -jnp.inf)
    p = jax.nn.softmax(logits, axis=-1).astype(v.dtype)
    return jnp.einsum('bqgrk,bqkgd->bqgrd', p, vg)


def _even_mixer(h, w_in, conv_w, w_out, past_k, past_v, past_ki, conv_hist):
    nb, T, _ = h.shape
    q, k, v, ga, qi, ki, wi, gb, gc, xin, gz = _split(h @ w_in, EVEN_SPLITS)
    q = q.reshape(nb, T, N_KV_A, GQA_REP, HEAD_DIM)
    k = k.reshape(nb, T, N_KV_A, HEAD_DIM)
    v = v.reshape(nb, T, N_KV_A, HEAD_DIM)
    qi = qi.reshape(nb, T, N_IDX_HEADS, IDX_DIM)
    if past_k is None:
        P = 0
        keys, vals, kidx = k, v, ki
        hist = jnp.zeros((nb, CONV_W - 1, CONV_DIM), h.dtype)
    else:
        P = past_k.shape[1]
        keys = jnp.concatenate([past_k.astype(k.dtype), k], axis=1)
        vals = jnp.concatenate([past_v.astype(v.dtype), v], axis=1)
        kidx = jnp.concatenate([past_ki.astype(ki.dtype), ki], axis=1)
        hist = conv_hist.astype(h.dtype)
    L = P + T
    kpos = jnp.arange(L, dtype=jnp.int32)
    qpos = P + jnp.arange(T, dtype=jnp.int32)
    topk = min(TOPK_MAX, L // 4)
    if T > Q_BLOCK:
        nblk = T // Q_BLOCK
        def blk(a):
            return jnp.moveaxis(a.reshape((nb, nblk, Q_BLOCK) + a.shape[2:]), 1, 0)
        o = lax.map(lambda xs: _dsa_block(xs[0], xs[1], xs[2], xs[3], keys, vals, kidx, kpos, topk),
                    (blk(q), blk(qi), blk(wi), qpos.reshape(nblk, Q_BLOCK)))
        o = jnp.moveaxis(o, 0, 1).reshape(nb, T, ATTN_DIM)
    else:
        o = _dsa_block(q, qi, wi, qpos, keys, vals, kidx, kpos, topk).reshape(nb, T, ATTN_DIM)
    a_out = o * jax.nn.silu(ga)
    z = gc * xin
    zp = jnp.concatenate([hist, z], axis=1)
    y = conv_w[CONV_W - 1] * zp[:, CONV_W - 1:]
    for j in range(CONV_W - 1):
        y = y + conv_w[j] * zp[:, j:j + T]
    b_out = gb * y * jax.nn.silu(gz)
    out = jnp.concatenate([a_out, b_out], axis=-1) @ w_out
    return out, k, v, ki, zp[:, -(CONV_W - 1):]


def _odd_mixer(h, w_in, ws, bs, ln_g, ln_b, w_out):
    nb, T, _ = h.shape
    u, v, g = _split(h @ w_in, (SGU_DIM, SGU_DIM, SGU_DIM))
    u = jax.nn.gelu(u)
    v = _layernorm(jax.nn.gelu(v), ln_g, ln_b)
    if T >= SGU_CHUNK:
        nc, lc = T // SGU_CHUNK, SGU_CHUNK
    else:
        nc, lc = 1, T
    vc = v.reshape(nb, nc, lc, N_SGU_GROUPS, SGU_GROUP_CH)
    wm = jnp.tril(ws[:, :lc, :lc])
    s = jnp.einsum('gij,bnjgc->bnigc', wm, vc) + bs[:, :lc].T[None, None, :, :, None]
    s = s.reshape(nb, T, SGU_DIM)
    out = (u * s * jax.nn.silu(g)) @ w_out
    return out, v


def _trunk(x, c, ada_w, ada_b, norm_g, ev_w_in, ev_conv_w, ev_w_out, od_w_in, od_ws, od_bs, od_ln_g, od_ln_b,
           od_w_out, final_g, cache_k=None, cache_v=None, cache_ki=None, conv_state=None):
    ks, vs, kis, convs, cvs = [], [], [], [], []
    cs = jax.nn.silu(c)
    for l in range(DEPTH):
        shift, scale, gate = jnp.split(cs @ ada_w[l] + ada_b[l], 3, axis=-1)
        h = _rmsnorm(x, norm_g[l]) * (1 + scale[:, None]) + shift[:, None]
        if l % 2 == 0:
            e = l // 2
            if cache_k is None:
                hist = (None, None, None, None)
            else:
                hist = (cache_k[e], cache_v[e], cache_ki[e], conv_state[e])
            out, k, v, ki, cst = _even_mixer(h, ev_w_in[e], ev_conv_w[e], ev_w_out[e], *hist)
            ks.append(k); vs.append(v); kis.append(ki); convs.append(cst)
        else:
            o = l // 2
            out, vrow = _odd_mixer(h, od_w_in[o], od_ws[o], od_bs[o], od_ln_g[o], od_ln_b[o], od_w_out[o])
            cvs.append(vrow)
        x = x + gate[:, None] * out
    new_c = None if cache_k is None else jnp.stack(cvs)
    return _rmsnorm(x, final_g), jnp.stack(ks), jnp.stack(vs), jnp.stack(kis), jnp.stack(convs), new_c


def setup_inputs(seed: int = 0) -> dict:
    key = jax.random.key(seed)
    k = jax.random.split(key, 21)
    def nrm(kk, shape, s):
        return jax.random.normal(kk, shape, jnp.float32) * s
    return {
        "x_prompt": nrm(k[0], (BATCH, SEQ, D_MODEL), 1.0),
        "x_sample": nrm(k[1], (DEC_BATCH, DEC_SEQ, D_MODEL), 1.0),
        "cache_a_k": nrm(k[2], (N_EVEN, DEC_BATCH, PAST_LEN, N_KV_A, HEAD_DIM), 1.0),
        "cache_a_v": nrm(k[3], (N_EVEN, DEC_BATCH, PAST_LEN, N_KV_A, HEAD_DIM), 1.0),
        "cache_a_kidx": nrm(k[4], (N_EVEN, DEC_BATCH, PAST_LEN, IDX_DIM), 1.0),
        "state_b_conv": nrm(k[5], (N_EVEN, DEC_BATCH, CONV_W - 1, CONV_DIM), 1.0),
        "c_prompt": nrm(k[6], (BATCH, D_MODEL), 1.0),
        "c_sample": nrm(k[7], (DEC_BATCH, D_MODEL), 1.0),
        "ada_w": nrm(k[8], (DEPTH, D_MODEL, 3 * D_MODEL), 0.5 * D_MODEL ** -0.5),
        "ada_b": nrm(k[9], (DEPTH, 3 * D_MODEL), 0.02),
        "norm_g": 1.0 + nrm(k[10], (DEPTH, D_MODEL), 0.02),
        "ev_w_in": nrm(k[11], (N_EVEN, D_MODEL, EVEN_IN), D_MODEL ** -0.5),
        "ev_conv_w": nrm(k[12], (N_EVEN, CONV_W, CONV_DIM), CONV_W ** -0.5),
        "ev_w_out": nrm(k[13], (N_EVEN, ATTN_DIM + CONV_DIM, D_MODEL), (ATTN_DIM + CONV_DIM) ** -0.5),
        "od_w_in": nrm(k[14], (N_ODD, D_MODEL, ODD_IN), D_MODEL ** -0.5),
        "od_ws": nrm(k[15], (N_ODD, N_SGU_GROUPS, SGU_CHUNK, SGU_CHUNK), 0.5 * SGU_CHUNK ** -0.5),
        "od_bs": 1.0 + nrm(k[16], (N_ODD, N_SGU_GROUPS, SGU_CHUNK), 0.1),
        "od_ln_g": 1.0 + nrm(k[17], (N_ODD, SGU_DIM), 0.02),
        "od_ln_b": nrm(k[18], (N_ODD, SGU_DIM), 0.02),
        "od_w_out": nrm(k[19], (N_ODD, SGU_DIM, D_MODEL), SGU_DIM ** -0.5),
        "final_g": 1.0 + nrm(k[20], (D_MODEL,), 0.02),
    }


def reference(x_prompt, x_sample, cache_a_k, cache_a_v, cache_a_kidx, state_b_conv, c_prompt, c_sample,
              ada_w, ada_b, norm_g, ev_w_in, ev_conv_w, ev_w_out, od_w_in, od_ws, od_bs, od_ln_g, od_ln_b,
              od_w_out, final_g):
    y_prompt, k_p, v_p, ki_p, conv_p, _ = _trunk(
        x_prompt, c_prompt, ada_w, ada_b, norm_g, ev_w_in, ev_conv_w, ev_w_out, od_w_in, od_ws, od_bs,
        od_ln_g, od_ln_b, od_w_out, final_g)
    y_sample, k_s, v_s, ki_s, conv_s, cv_s = _trunk(
        x_sample, c_sample, ada_w, ada_b, norm_g, ev_w_in, ev_conv_w, ev_w_out, od_w_in, od_ws, od_bs,
        od_ln_g, od_ln_b, od_w_out, final_g, cache_a_k, cache_a_v, cache_a_kidx, state_b_conv)
    return (y_prompt, y_sample, k_p, v_p, ki_p, conv_p, k_s, v_s, ki_s, conv_s, cv_s)
```

```python
import contextlib
import numpy as np
import concourse.bass as bass
import concourse.mybir as mybir
from concourse.bass_utils import run_bass_kernel_spmd

F32 = mybir.dt.float32
BF16 = mybir.dt.bfloat16
AF = mybir.ActivationFunctionType
ALU = mybir.AluOpType
AX = mybir.AxisListType

D = 1024
SEQ = 2048
TT = 256
NT = SEQ // TT
DEC = 64
PAST = 1024
EVEN_IN = 3912
NIT = 16
EPS = 1e-6
NEG = -1.0e30
_SYNC_SAME = True
COMPUTE = ("pe", "act", "dve", "pool")


class Tok:
    __slots__ = ("name", "writer", "readers")

    def __init__(self, name):
        self.name = name
        self.writer = None
        self.readers = {}


class Op:
    __slots__ = ("eng", "fn", "deps", "inc", "val", "dma_key", "is_dma", "soft")

    def __init__(self, eng, fn, dma_key=None):
        self.eng = eng
        self.fn = fn
        self.deps = []
        self.inc = False
        self.val = None
        self.dma_key = dma_key
        self.is_dma = dma_key is not None
        self.soft = ()


class Prog:
    def __init__(self, nc):
        self.nc = nc
        self.ops = {e: [] for e in ("pe", "act", "dve", "pool", "sp")}
        self.GB = Tok("GB")
        self.gap_fn = {}

    def add(self, eng, fn, reads=(), writes=(), dma_key=None, barrier=False):
        op = Op(eng, fn, dma_key)
        reads = list(reads)
        writes = list(writes)
        if barrier:
            writes.append(self.GB)
        else:
            reads.append(self.GB)
        deps = []
        soft = []
        gap_ok = (not _SYNC_SAME) and eng in ("act", "dve") and not op.is_dma
        for t in reads:
            w = t.writer
            if w is None:
                continue
            if (not w.is_dma) and w.eng == eng:
                if eng == "pe":
                    continue
                if gap_ok:
                    soft.append(w)
                    continue
            deps.append(w)
        for t in writes:
            w = t.writer
            same_ok = eng == "pe" or gap_ok
            if w is not None and not ((not w.is_dma) and (not op.is_dma) and w.eng == eng and same_ok):
                deps.append(w)
            for r in t.readers.values():
                if not ((not r.is_dma) and (not op.is_dma) and r.eng == eng and same_ok):
                    deps.append(r)
        seen = set()
        for d in deps:
            if id(d) not in seen and d is not op:
                seen.add(id(d))
                op.deps.append(d)
        op.soft = soft
        rkey = ("dma", dma_key) if op.is_dma else eng
        for t in reads:
            t.readers[rkey] = op
        for t in writes:
            t.writer = op
            t.readers = {}
        self.ops[eng].append(op)
        return op

    def emit(self):
        nc = self.nc
        self.n_gap = 0
        for lst in self.ops.values():
            for op in lst:
                for d in op.deps:
                    d.inc = True
        dma_cnt = {}
        sems = {}
        with contextlib.ExitStack() as stack:
            for e, lst in self.ops.items():
                cnt = 0
                for op in lst:
                    if op.is_dma:
                        k = ("dma", op.dma_key)
                        dma_cnt[k] = dma_cnt.get(k, 0) + 16
                        op.val = (k, dma_cnt[k])
                        op.inc = True
                    elif op.inc:
                        cnt += 1
                        op.val = (e, cnt)
            for i, k in enumerate(list(dma_cnt.keys()) + list(COMPUTE)):
                sems[k] = stack.enter_context(nc.semaphore("s%d" % i))
            engmap = {"pe": "tensor", "act": "scalar", "dve": "vector", "pool": "gpsimd", "sp": "sync"}
            with nc.Block() as block:
                def mk(ename):
                    def body(eng):
                        waited = {}
                        prev = None
                        for op in self.ops[ename]:
                            if prev is not None and any(d is prev for d in op.soft):
                                self.gap_fn[ename](eng)
                                self.n_gap += 1
                            prev = op
                            need = {}
                            for d in op.deps:
                                k, v = d.val
                                if v > need.get(k, 0):
                                    need[k] = v
                            for k, v in need.items():
                                if waited.get(k, 0) >= v:
                                    continue
                                waited[k] = v
                                eng.wait_ge(sems[k], v)
                            ins = op.fn(eng)
                            if op.inc:
                                ins.then_inc(sems[op.val[0]], 16 if op.is_dma else 1)
                        if ename == "sp":
                            for k, v in dma_cnt.items():
                                if waited.get(k, 0) < v:
                                    eng.wait_ge(sems[k], v)
                    return body
                for en, attr in engmap.items():
                    getattr(block, attr)(mk(en))


def build_nc():
    nc = bass.Bass("TRN2", target_bir_lowering=False)
    P = Prog(nc)

    def din(name, shape):
        return nc.dram_tensor(name, list(shape), F32, kind="ExternalInput").ap()

    def dout(name, shape):
        return nc.dram_tensor(name, list(shape), F32, kind="ExternalOutput").ap()

    d_xp = din("xp", [2, D, SEQ]); d_xs = din("xs", [D, DEC]); d_cT = din("cT", [D, 3])
    d_adaw = din("ada_w", [4, D, 3 * D]); d_adab = din("ada_b", [128, 4, 24])
    d_ng = din("norm_g", [128, 4, 8]); d_fg = din("final_g", [128, 8])
    d_wine = din("w_in_e", [2, D, EVEN_IN]); d_convw = din("conv_w", [128, 2, 4, 3])
    d_woute = din("w_out_e", [2, D, D]); d_wino = din("w_in_o", [2, D, 3 * D])
    d_wsT = din("wsT", [2, 128, 8, 128]); d_bs = din("bs", [2, D]); d_lng = din("ln_g", [2, D])
    d_lnb = din("ln_b", [2, D]); d_wouto = din("w_out_o", [2, D, D])
    d_ckT = din("ckT", [2, 128, PAST]); d_cv = din("cv", [2, PAST, 128]); d_ckiT = din("ckiT", [2, 64, PAST])
    d_cconv = din("cconv", [2, 128, 4, 2])
    d_ident = din("ident", [128, 128]); d_tril = din("trilT", [128, 128]); d_pow2 = din("pow2", [128, NIT + 1])

    o_yp = dout("yp", [2, D, SEQ]); o_ys = dout("ys", [D, DEC])
    o_kp = dout("kp", [2, 2, 128, SEQ]); o_vp = dout("vp", [2, 2, SEQ, 128]); o_kip = dout("kip", [2, 2, 64, SEQ])
    o_convp = dout("convp", [2, 2, 128, 4, 2])
    o_ks = dout("ks", [2, 128, DEC]); o_vs = dout("vs", [2, DEC, 128]); o_kis = dout("kis", [2, 64, DEC])
    o_convs = dout("convs", [2, 128, 4, 2]); o_cvs = dout("cvs", [2, DEC, D])

    ARENA_BYTES = 49152
    with contextlib.ExitStack() as st:
        def sb(name, shape, dt):
            return st.enter_context(nc.sbuf_tensor("sb_" + name, list(shape), dt))

        x_sb = sb("x_sb", [128, 8, SEQ], F32)
        xs_sb = sb("xs_sb", [128, 8, DEC], F32)
        w_in = sb("w_in", [128, 8, EVEN_IN], BF16)
        w_out = sb("w_out", [128, 8, D], BF16)
        kT = sb("kT", [128, SEQ], BF16)
        kiT = sb("kiT", [128, SEQ], BF16)
        v_sb = sb("v_sb", [128, 16, 128], BF16)
        ones_bf = sb("ones_bf", [128, 128], BF16)
        onesm = sb("onesm", [128, 128], BF16)
        ident_f = sb("ident_f", [128, 128], F32)
        ident_bf = sb("ident_bf", [128, 128], BF16)
        tril_f = sb("tril_f", [128, 128], F32)
        pow2 = sb("pow2", [128, NIT + 1], F32)
        adab = sb("adab", [128, 4, 24], F32)
        normg = sb("normg", [128, 4, 8], F32)
        fg = sb("fg", [128, 8], F32)
        convw = sb("convw", [128, 2, 4, 3], F32)
        cT = sb("cT", [128, 8, 3], F32)
        scT = sb("scT", [128, 8, 3], F32)
        mod = sb("mod", [128, 4, 24, 3], F32)
        Gm = sb("Gm", [128, 4, 3, 8], F32)
        arena = sb("arena", [128, ARENA_BYTES // 2], BF16)
        banks = [st.enter_context(nc.psum_tensor("bank%d" % i, [128, 512], F32)) for i in range(8)]
        btok = [Tok("bank%d" % i) for i in range(8)]

        class Arena:
            def __init__(self):
                self.off = 0

            def alloc(self, nelem, dt):
                nb = nelem * (4 if dt == F32 else 2)
                nb = (nb + 63) // 64 * 64
                a = self.off
                self.off += nb
                assert self.off <= ARENA_BYTES, ("arena overflow", self.off)
                v = arena[:, a // 2:(a + nb) // 2]
                if dt == F32:
                    v = v.bitcast(F32)
                    return v[:, 0:nelem]
                return v[:, 0:nelem]

        def v3(ap, a):
            return ap.rearrange("p (a b) -> p a b", a=a)

        A = Arena()
        stage = [A.alloc(1024, F32) for _ in range(3)]
        stage_tok = [Tok("stage%d" % i) for i in range(3)]
        wst_f = A.alloc(8 * 128, F32)
        A = Arena()
        hT = v3(A.alloc(8 * TT, BF16), 8)
        sq = v3(A.alloc(8 * TT, BF16), 8)
        sq_off = A.off - 8 * TT * 2
        rstd_t = A.alloc(TT, F32)
        rstd = A.alloc(TT, F32)
        tmpn_off = A.off
        tmpn = [A.alloc(TT, F32) for _ in range(2)]
        common_off = A.off
        xin_sb = A.alloc(TT, F32); gb_sb = A.alloc(TT, F32); sgz_sb = A.alloc(TT, F32); y_sb = A.alloc(TT, F32)
        kst = A.alloc(TT, F32); kist = A.alloc(TT, F32); vst = v3(A.alloc(2 * 128, F32), 2)
        phB_end = A.off
        A = Arena()
        score = [A.alloc(SEQ, F32) for _ in range(2)]
        _mk = A.alloc(SEQ, BF16)
        maskm1 = [_mk, _mk]
        junk_off = A.off
        _dg = v3(A.alloc(8 * 128, BF16), 8)
        dg = [_dg, _dg]
        rl = [A.alloc(512, BF16) for _ in range(2)]
        pT = [A.alloc(512, BF16) for _ in range(2)]
        rden = A.alloc(512, F32)
        junk1 = arena[:, junk_off // 2:junk_off // 2 + SEQ]
        lnd = arena[:, junk_off // 2:junk_off // 2 + 1024].bitcast(F32)
        assert A.off - junk_off >= 2 * SEQ
        bis = []
        for _i in range(2):
            bis.append(dict(dh=A.alloc(NIT + 1, F32), mid=A.alloc(NIT + 1, F32), cnt=A.alloc(NIT, F32),
                            tmp=A.alloc(NIT, F32), mx=A.alloc(1, F32), mn=A.alloc(1, F32), d0=A.alloc(1, F32)))
        A.off = max(A.off, phB_end)
        qTz = A.alloc(2 * 4 * TT, BF16).rearrange("p (g r t) -> p g r t", g=2, r=4)
        qiTz = v3(A.alloc(8 * TT, BF16), 8)
        sga = v3(A.alloc(4 * TT, BF16), 4)
        b_out = v3(A.alloc(4 * TT, BF16), 4)
        wi_sb = v3(A.alloc(2 * 8, F32), 2)
        zp_p = v3(A.alloc(4 * (TT + 2), F32), 4)
        zp_s = v3(A.alloc(4 * (DEC + 2), F32), 4)
        I4B = v3(A.alloc(4 * 128, BF16), 4)
        even_end = A.off
        A = Arena(); A.off = common_off
        u_sb = v3(A.alloc(8 * TT, BF16), 8)
        sg_sb = v3(A.alloc(8 * TT, BF16), 8)
        gated = v3(A.alloc(8 * TT, BF16), 8)
        vtm = A.alloc(D, F32)
        vn = v3(A.alloc(2 * D, BF16), 2)
        lng_b = A.alloc(D, F32); lnb_b = A.alloc(D, F32)
        bs_b = v3(A.alloc(D, F32), 8)
        wmT = v3(A.alloc(8 * 128, BF16), 8)
        st_mean = A.alloc(1, F32); st_var = A.alloc(1, F32); st_rstd = A.alloc(1, F32); st_sd = A.alloc(1, F32)
        odd_end = A.off
        vnf = arena[:, sq_off // 2:sq_off // 2 + 2 * D].bitcast(F32)
        s_tmp = arena[:, tmpn_off // 2:tmpn_off // 2 + 1024].bitcast(F32)
        A = Arena(); A.off = common_off
        yst2 = [v3(A.alloc(8 * TT, F32), 8) for _ in range(2)]
        sqF = v3(A.alloc(8 * TT, BF16), 8); rstd_tF = A.alloc(TT, F32); rstdF = A.alloc(TT, F32)
        print("arena: even", even_end, "odd", odd_end, "of", ARENA_BYTES)

        T = {}

        def tok(n):
            if n not in T:
                T[n] = Tok(n)
            return T[n]

        def dma(out, in_, reads, writes, key):
            P.add("sp", lambda e: e.dma_start(out=out, in_=in_), reads, writes, dma_key=key)

        def mm(out, lhsT, rhs, start, stop, reads, writes, tp=None):
            if tp is None:
                P.add("pe", lambda e: e.matmul(out, lhsT=lhsT, rhs=rhs, start=start, stop=stop), reads, writes)
            else:
                P.add("pe", lambda e: e.matmul(out, lhsT=lhsT, rhs=rhs, start=start, stop=stop, tile_position=tp),
                      reads, writes)

        def act(out, in_, func, reads, writes, scale=1.0, bias=None, accum=None):
            kw = {}
            if bias is not None:
                kw["bias"] = bias
            if accum is not None:
                kw["accum_out"] = accum
            P.add("act", lambda e: e.activation(out=out, in_=in_, func=func, scale=scale, **kw), reads, writes)

        def ts(eng, out, in0, s1, s2, op0, op1, reads, writes, accum=None):
            if accum is not None:
                P.add(eng, lambda e: e.tensor_scalar(out=out, in0=in0, scalar1=s1, scalar2=s2, op0=op0, op1=op1,
                                                     accum_out=accum), reads, writes)
            elif op1 is None:
                P.add(eng, lambda e: e.tensor_scalar(out=out, in0=in0, scalar1=s1, scalar2=None, op0=op0), reads, writes)
            else:
                P.add(eng, lambda e: e.tensor_scalar(out=out, in0=in0, scalar1=s1, scalar2=s2, op0=op0, op1=op1),
                      reads, writes)

        def tt(eng, out, in0, in1, op, reads, writes):
            P.add(eng, lambda e: e.tensor_tensor(out=out, in0=in0, in1=in1, op=op), reads, writes)

        def stt(eng, out, in0, scalar, in1, op0, op1, reads, writes):
            P.add(eng, lambda e: e.scalar_tensor_tensor(out=out, in0=in0, scalar=scalar, in1=in1, op0=op0, op1=op1),
                  reads, writes)

        def cp(eng, out, in_, reads, writes):
            if eng == "act":
                P.add("act", lambda e: e.activation(out=out, in_=in_, func=AF.Copy), reads, writes)
            else:
                P.add(eng, lambda e: e.tensor_copy(out=out, in_=in_), reads, writes)

        def memset(eng, ap, val, reads, writes):
            P.add(eng, lambda e: e.memset(ap, val), reads, writes)

        def recip(out, in_, reads, writes):
            P.add("dve", lambda e: e.reciprocal(out=out, in_=in_), reads, writes)

        def barrier():
            P.add("pool", lambda e: e.memset(bar_sb[:, 0:1], 0.0), [], [], barrier=True)

        bar_sb = sb("bar_sb", [128, 2], F32)
        rr = {"pp": 0, "px": 0, "pl": 0, "st": 0, "rl": 0, "pT": 0, "tmpn": 0, "pa": 0}

        def nxt(kind, n, base):
            i = rr[kind]
            rr[kind] = (i + 1) % n
            return base + i

        class _Stop(Exception):
            pass

        stage_ctr = [0]

        def stage_mark(name):
            stage_ctr[0] += 1
            if _STAGE_LIMIT is not None and stage_ctr[0] > _STAGE_LIMIT:
                print("build stopped before stage", stage_ctr[0], name)
                raise _Stop()

        def sub_mark(name):
            if _SUB_MARKS:
                stage_mark(name)

        cst = 0

        def cload(dst, src):
            nonlocal cst
            cst += 1
            dma(dst, src, [], [tok("c%d" % cst)], "const")

        cload(ident_f[:], d_ident); cload(tril_f[:], d_tril); cload(pow2[:], d_pow2)
        cload(adab[:], d_adab); cload(normg[:], d_ng); cload(fg[:], d_fg); cload(convw[:], d_convw)
        cload(cT[:], d_cT.rearrange("(kc p) j -> p kc j", p=128))
        memset("pool", ones_bf[:], 1.0, [], [tok("ones")])
        memset("pool", onesm[:], 1.0 / D, [], [tok("onesm")])
        barrier()
        cp("pool", ident_bf[:], ident_f[:], [], [tok("identbf")])
        act(scT[:], cT[:], AF.Silu, [], [tok("scT")])
        barrier()

        ada_st = [v3(arena[:, i * 8192:(i + 1) * 8192].bitcast(F32), 8) for i in range(2)]
        ada_tok = [Tok("adast0"), Tok("adast1")]
        ada_rr = 0
        modtm = arena[:, 16384:16384 + 6144].bitcast(F32)
        for l in range(4):
            for cc in range(6):
                s = ada_rr
                ada_rr = 1 - ada_rr
                bk = cc % 2
                dma(ada_st[s], d_adaw[l, :, cc * 512:(cc + 1) * 512].rearrange("(kc p) c -> p kc c", p=128), [],
                    [ada_tok[s]], "adast%d" % s)
                for kc in range(8):
                    mm(banks[bk][0:3, 0:512], scT[:, kc, :], ada_st[s][:, kc, :], kc == 0, kc == 7,
                       [ada_tok[s], tok("scT")], [btok[bk]])
                cp("dve", modtm[0:3, cc * 512:(cc + 1) * 512], banks[bk][0:3, 0:512], [btok[bk]], [tok("modtm")])
            for oc in range(24):
                mm(banks[2][:, oc * 3:(oc + 1) * 3], modtm[0:3, oc * 128:(oc + 1) * 128], ident_f[0:3, 0:3], True, True,
                   [tok("modtm")], [btok[2]])
            for j in range(3):
                tt("dve", mod[:, l, :, j], v3(banks[2][:, 0:72], 24)[:, :, j], adab[:, l, :], ALU.add,
                   [btok[2]], [tok("mod")])
            for j in range(3):
                stt("dve", Gm[:, l, j, :], mod[:, l, 8:16, j], 1.0, normg[:, l, :], ALU.add, ALU.mult,
                    [tok("mod")], [tok("Gm")])
        barrier()

        def issue_w(dram_w, dst, C, name):
            for ci in range((C + 1023) // 1024):
                c0 = ci * 1024
                n = min(1024, C - c0)
                P.add("pool", (lambda o, i_: (lambda e: e.dma_start(out=o, in_=i_)))(
                    dst[:, :, c0:c0 + n], dram_w[:, c0:c0 + n].rearrange("(kc p) c -> p kc c", p=128)),
                    [], [tok("%s_c%d" % (name, ci))], dma_key="%s_c%d" % (name, ci))

        def wtk(c0):
            return tok("w_in_c%d" % (c0 // 1024))

        HOOK = {"after_in": None, "after_out": None}

        def run_hook(k):
            f = HOOK[k]
            HOOK[k] = None
            if f is not None:
                f()

        class Seq:
            pass

        def mkseq(pi):
            S = Seq()
            S.sample = False; S.j = pi; S.TT = TT; S.nt = NT; S.pi = pi
            S.x = lambda t: x_sb[:, :, t * TT:(t + 1) * TT]
            S.xtok = [tok("x%d" % t) for t in range(NT)]
            S.zp = zp_p; S.zptok = [tok("zp%d" % c) for c in range(4)]
            return S

        SS = Seq()
        SS.sample = True; SS.j = 2; SS.TT = DEC; SS.nt = 1; SS.pi = 0
        SS.x = lambda t: xs_sb[:, :, :]
        SS.xtok = [tok("xs")]
        SS.zp = zp_s; SS.zptok = [tok("zps%d" % c) for c in range(4)]

        def rms_stats(S, t, alt=False):
            n = S.TT
            xt = S.x(t)
            sq_, rt_, rs_, sfx = (sqF, rstd_tF, rstdF, "F") if alt else (sq, rstd_t, rstd, "")
            act(sq_[:, :, 0:n], xt, AF.Square, [S.xtok[t]], [tok("sq" + sfx)])
            b = nxt("px", 2, 2)
            for kc in range(8):
                mm(banks[b][:, 0:n], onesm[:, :], sq_[:, kc, 0:n], kc == 0, kc == 7, [tok("sq" + sfx), tok("onesm")],
                   [btok[b]])
            act(rt_[:, 0:n], banks[b][:, 0:n], AF.Sqrt, [btok[b]], [tok("rstd_t" + sfx)], bias=eps_sb[:, 0:1])
            recip(rs_[:, 0:n], rt_[:, 0:n], [tok("rstd_t" + sfx)], [tok("rstd" + sfx)])
            return rs_, tok("rstd" + sfx)

        eps_sb = sb("eps_sb", [128, 1], F32)
        memset("pool", eps_sb[:], EPS, [], [tok("eps")])

        def norm_mod(S, t, l):
            n = S.TT
            xt = S.x(t)
            rms_stats(S, t)
            for kc in range(8):
                s = nxt("tmpn", 2, 0)
                stt("dve", tmpn[s][:, 0:n], xt[:, kc, :], Gm[:, l, S.j, kc:kc + 1], rstd[:, 0:n], ALU.mult, ALU.mult,
                    [S.xtok[t], tok("rstd"), tok("Gm")], [tok("tmpn%d" % s)])
                act(hT[:, kc, 0:n], tmpn[s][:, 0:n], AF.Identity, [tok("tmpn%d" % s), tok("mod")], [tok("hT")],
                    bias=mod[:, l, kc, S.j:S.j + 1])

        def proj_fm(W, wtok, c0, M, n, p0=0):
            b = nxt("pp", 2, 0)
            for kc in range(8):
                mm(banks[b][p0:p0 + M, 0:n], W[:, kc, c0:c0 + M], hT[:, kc, 0:n], kc == 0, kc == 7,
                   [wtk(c0), tok("hT")], [btok[b]], tp=(0, p0) if p0 else None)
            return b

        def out_proj_residual(S, t, l, wtok, rhs_list):
            n = S.TT
            xt = S.x(t)
            for oc in range(8):
                b = nxt("pp", 2, 0)
                for i, (rap, rtoks) in enumerate(rhs_list):
                    mm(banks[b][:, 0:n], w_out[:, i, oc * 128:(oc + 1) * 128], rap, i == 0, i == len(rhs_list) - 1,
                       [wtok] + rtoks, [btok[b]])
                stt("dve", xt[:, oc, :], banks[b][:, 0:n], mod[:, l, 16 + oc, S.j:S.j + 1], xt[:, oc, :],
                    ALU.mult, ALU.add, [btok[b], tok("mod"), S.xtok[t]], [S.xtok[t]])

        def even_tile(S, t, l, e, skip_norm=False, next_norm=None):
            n = S.TT
            wt = tok("w_in")
            key0 = PAST if S.sample else t * TT
            if not skip_norm:
                norm_mod(S, t, l)
            sub_mark('after norm')
            sub_mark('after v/wi')
            zp = S.zp
            for c in range(4):
                zt = S.zptok[c]
                b = proj_fm(w_in, wt, 2888 + c * 128, 128, n)
                cp("act", xin_sb[:, 0:n], banks[b][:, 0:n], [btok[b]], [tok("xin")])
                b = proj_fm(w_in, wt, 2376 + c * 128, 128, n)
                tt("dve", zp[:, c, 2:2 + n], banks[b][:, 0:n], xin_sb[:, 0:n], ALU.mult, [btok[b], tok("xin")], [zt])
                b = proj_fm(w_in, wt, 1864 + c * 128, 128, n)
                cp("act", gb_sb[:, 0:n], banks[b][:, 0:n], [btok[b]], [tok("gb")])
                b = proj_fm(w_in, wt, 3400 + c * 128, 128, n)
                act(sgz_sb[:, 0:n], banks[b][:, 0:n], AF.Silu, [btok[b]], [tok("sgz")])
                ts("dve", y_sb[:, 0:n], zp[:, c, 2:2 + n], convw[:, e, c, 2:3], None, ALU.mult, None, [zt], [tok("y")])
                stt("dve", y_sb[:, 0:n], zp[:, c, 1:1 + n], convw[:, e, c, 1:2], y_sb[:, 0:n], ALU.mult, ALU.add,
                    [zt, tok("y")], [tok("y")])
                stt("dve", y_sb[:, 0:n], zp[:, c, 0:n], convw[:, e, c, 0:1], y_sb[:, 0:n], ALU.mult, ALU.add,
                    [zt, tok("y")], [tok("y")])
                tt("dve", y_sb[:, 0:n], y_sb[:, 0:n], gb_sb[:, 0:n], ALU.mult, [tok("y"), tok("gb")], [tok("y")])
                tt("dve", b_out[:, c, 0:n], y_sb[:, 0:n], sgz_sb[:, 0:n], ALU.mult, [tok("y"), tok("sgz")], [tok("b_out")])
            sub_mark('after conv')
            last = (t == S.nt - 1)
            if last:
                if S.sample:
                    dma(o_convs[e], zp[:, :, n:n + 2], [S.zptok[c] for c in range(4)], [], "o_convs")
                else:
                    dma(o_convp[e, S.pi], zp[:, :, n:n + 2], [S.zptok[c] for c in range(4)], [], "o_convp")
            else:
                for c in range(4):
                    cp("dve", zp[:, c, 0:2], zp[:, c, n:n + 2], [S.zptok[c]], [S.zptok[c]])
            for r in range(4):
                b = proj_fm(w_in, wt, r * 128, 128, n)
                cp("act", qTz[0:64, 0, r, 0:n], banks[b][0:64, 0:n], [btok[b]], [tok("qT")])
                cp("act", qTz[64:128, 1, r, 0:n], banks[b][64:128, 0:n], [btok[b]], [tok("qT")])
            b = proj_fm(w_in, wt, 512, 128, n)
            cp("dve", kst[:, 0:n], banks[b][:, 0:n], [btok[b]], [tok("kst")])
            cp("act", kT[:, key0:key0 + n], kst[:, 0:n], [tok("kst")], [tok("kT%d" % (key0 // TT))])
            if S.sample:
                dma(o_ks[e], kst[:, 0:n], [tok("kst")], [], "o_k")
            else:
                dma(o_kp[e, S.pi, :, key0:key0 + n], kst[:, 0:n], [tok("kst")], [], "o_k")
            for r in range(4):
                b = proj_fm(w_in, wt, 768 + r * 128, 128, n)
                act(sga[:, r, 0:n], banks[b][:, 0:n], AF.Silu, [btok[b]], [tok("sga")])
            for jj in range(4):
                b = proj_fm(w_in, wt, 1280 + jj * 128, 128, n)
                cp("dve", qiTz[0:64, 2 * jj, 0:n], banks[b][0:64, 0:n], [btok[b]], [tok("qiT")])
                cp("dve", qiTz[64:128, 2 * jj + 1, 0:n], banks[b][64:128, 0:n], [btok[b]], [tok("qiT")])
            b = nxt("pp", 2, 0)
            for half in range(2):
                for kc in range(8):
                    mm(banks[b][half * 64:(half + 1) * 64, 0:n], w_in[:, kc, 1792:1856], hT[:, kc, 0:n], kc == 0, kc == 7,
                       [wtk(1792), tok("hT")], [btok[b]], tp=(0, 64) if half else None)
            cp("dve", kist[:, 0:n], banks[b][:, 0:n], [btok[b]], [tok("kist")])
            cp("act", kiT[:, key0:key0 + n], kist[:, 0:n], [tok("kist")], [tok("kiT%d" % (key0 // TT))])
            if S.sample:
                dma(o_kis[e], kist[0:64, 0:n], [tok("kist")], [], "o_ki")
            else:
                dma(o_kip[e, S.pi, :, key0:key0 + n], kist[0:64, 0:n], [tok("kist")], [], "o_ki")
            sub_mark('after qkgaqiki')
            ntg = (n + 127) // 128
            for tg in range(ntg):
                m = min(128, n - tg * 128)
                kb = (key0 + tg * 128) // 128
                b = nxt("px", 2, 2)
                for kc in range(8):
                    mm(banks[b][0:m, 0:128], hT[:, kc, tg * 128:tg * 128 + m], w_in[:, kc, 640:768], kc == 0, kc == 7,
                       [wtk(640), tok("hT")], [btok[b]])
                for kc in range(8):
                    mm(banks[b][0:m, 128:136], hT[:, kc, tg * 128:tg * 128 + m], w_in[:, kc, 1856:1864], kc == 0, kc == 7,
                       [wtk(1856), tok("hT")], [btok[b]])
                cp("dve", vst[0:m, tg, :], banks[b][0:m, 0:128], [btok[b]], [tok("vst")])
                cp("act", v_sb[0:m, kb, :], vst[0:m, tg, :], [tok("vst")], [tok("v%d" % (key0 // TT))])
                cp("dve", wi_sb[0:m, tg, :], banks[b][0:m, 128:136], [btok[b]], [tok("wi")])
            if S.sample:
                dma(o_vs[e], vst[0:DEC, 0, :], [tok("vst")], [], "o_v")
            else:
                dma(o_vp[e, S.pi, key0:key0 + n, :].rearrange("(tg p) c -> p tg c", p=128), vst[:, :, :],
                    [tok("vst")], [], "o_v")
            run_hook('after_in')
            nsub = (n + 127) // 128
            barrier()
            attention_tile(S, t, l)
            barrier()
            sub_mark('after attention')
            if next_norm is not None:
                norm_mod(*next_norm)
            rhs = [(sga[:, r, 0:n], [tok("sga")]) for r in range(4)] + [(b_out[:, c, 0:n], [tok("b_out")]) for c in range(4)]
            out_proj_residual(S, t, l, tok("w_out_c0"), rhs)
            run_hook("after_out")

        def attention_tile(S, t, l):
            if S.sample:
                subs = [dict(i=0, nq=DEC, q0=0, blocks=[(kb * 128, 128) for kb in range(8)] + [(PAST, DEC)],
                             corner=False, topk=True)]
            else:
                subs = []
                for sub in range(TT // 128):
                    qt = t * (TT // 128) + sub
                    subs.append(dict(i=sub, nq=128, q0=sub * 128, blocks=[(kb * 128, 128) for kb in range(qt + 1)],
                                     corner=True, topk=qt >= 2))
            for d in subs:
                d["SK"] = d["blocks"][-1][0] + d["blocks"][-1][1]
                nt_ = (d["SK"] + TT - 1) // TT
                d["ktoks"] = [tok("kT%d" % i) for i in range(nt_)]
                d["kitoks"] = [tok("kiT%d" % i) for i in range(nt_)]
                d["vtoks"] = [tok("v%d" % i) for i in range(nt_)]
            steps1 = []
            for d in subs:
                for c0 in range(0, d["SK"], 512):
                    for h in range(8):
                        steps1.append((d, c0, min(512, d["SK"] - c0), h))
            st1 = {}

            def p1_front(k):
                d, c0, m, h = steps1[k]
                i, nq, q0 = d["i"], d["nq"], d["q0"]
                if c0 == 0 and h == 0:
                    tt("dve", dg[i][0:nq, :, 0:nq], ident_bf[0:nq, 0:nq].unsqueeze(1).to_broadcast([nq, 8, nq]),
                       wi_sb[0:nq, i, :].unsqueeze(2).to_broadcast([nq, 8, nq]), ALU.mult,
                       [tok("identbf"), tok("wi")], [tok("dg")])
                b = nxt("px", 2, 2)
                mm(banks[b][0:nq, 0:m], qiTz[0:128, h, q0:q0 + nq], kiT[0:128, c0:c0 + m],
                   True, True, [tok("qiT")] + d["kitoks"], [btok[b]])
                s = nxt("rl", 2, 0)
                act(rl[s][0:nq, 0:m], banks[b][0:nq, 0:m], AF.Relu, [btok[b]], [tok("rl%d" % s)])
                st1[k] = s

            def p1_back(k):
                d, c0, m, h = steps1[k]
                i, nq = d["i"], d["nq"]
                s = st1[k]
                if h == 0:
                    st1["ab"] = nxt("pa", 2, 4)
                ab = st1["ab"]
                mm(banks[ab][0:nq, 0:m], dg[i][0:nq, h, 0:nq], rl[s][0:nq, 0:m], h == 0, h == 7,
                   [tok("dg"), tok("rl%d" % s)], [btok[ab]])
                if h == 7:
                    cp("dve", score[i][0:nq, c0:c0 + m], banks[ab][0:nq, 0:m], [btok[ab]], [tok("score%d" % i)])

            for k in range(len(steps1) + 1):
                flushed = False
                if 1 <= k < len(steps1) and steps1[k][1] == 0 and steps1[k][3] == 0:
                    p1_back(k - 1)
                    flushed = True
                if k < len(steps1):
                    p1_front(k)
                if k >= 1 and not flushed:
                    p1_back(k - 1)
            sub_mark('att p1')
            for d in subs:
                i, nq, SK = d["i"], d["nq"], d["SK"]
                B = bis[i]
                sct, bt = tok("score%d" % i), tok("bis%d" % i)
                if d["topk"]:
                    P.add("dve", (lambda o, a: (lambda e: e.tensor_reduce(out=o, in_=a, axis=AX.X, op=ALU.max)))(
                        B["mx"][0:nq, 0:1], score[i][0:nq, 0:SK]), [sct], [bt])
                    P.add("dve", (lambda o, a: (lambda e: e.tensor_reduce(out=o, in_=a, axis=AX.X, op=ALU.min)))(
                        B["mn"][0:nq, 0:1], score[i][0:nq, 0:SK]), [sct, bt], [bt])
                if d["corner"]:
                    memset("pool", score[i][0:64, SK - 64:SK], NEG, [sct, bt], [sct])
                if d["topk"]:
                    tt("dve", B["d0"][0:nq, :], B["mx"][0:nq, :], B["mn"][0:nq, :], ALU.subtract, [bt], [bt])
                    ts("dve", B["dh"][0:nq, :], pow2[0:nq, :], B["d0"][0:nq, 0:1], None, ALU.mult, None, [bt], [bt])
                    tt("dve", B["mid"][0:nq, 0:1], B["mn"][0:nq, :], B["dh"][0:nq, 0:1], ALU.add, [bt], [bt])
            tk = [d for d in subs if d["topk"]]
            on_act = tk[1]["i"] if len(tk) == 2 else None

            def cnt_op(d, k):
                i, nq, SK = d["i"], d["nq"], d["SK"]
                B = bis[i]
                jbuf = maskm1[0] if i == 0 else junk1
                jtok = [tok("mask")] if i == 0 else [tok("dg"), tok("rl0"), tok("rl1"), tok("pT0"), tok("pT1"),
                                                     tok("rden")]
                if i == on_act:
                    act(jbuf[0:nq, 0:SK], score[i][0:nq, 0:SK], AF.Sign, [tok("score%d" % i), tok("bis%d" % i)],
                        jtok + [tok("bisc%d" % i)], scale=-1.0, bias=B["mid"][0:nq, k:k + 1],
                        accum=B["cnt"][0:nq, k:k + 1])
                else:
                    ts("dve", jbuf[0:nq, 0:SK], score[i][0:nq, 0:SK], B["mid"][0:nq, k:k + 1], None, ALU.is_ge,
                       ALU.add, [tok("score%d" % i), tok("bis%d" % i)], jtok + [tok("bisc%d" % i)],
                       accum=B["cnt"][0:nq, k:k + 1])

            def upd_op(d, k):
                i, nq, SK = d["i"], d["nq"], d["SK"]
                B = bis[i]
                if i == on_act:
                    stt("dve", B["tmp"][0:nq, k:k + 1], B["cnt"][0:nq, k:k + 1], float(SK) - 511.5,
                        B["dh"][0:nq, k:k + 1], ALU.is_le, ALU.mult, [tok("bisc%d" % i), tok("bis%d" % i)],
                        [tok("bist%d" % i)])
                else:
                    stt("dve", B["tmp"][0:nq, k:k + 1], B["cnt"][0:nq, k:k + 1], 255.5, B["dh"][0:nq, k:k + 1],
                        ALU.is_ge, ALU.mult, [tok("bisc%d" % i), tok("bis%d" % i)], [tok("bist%d" % i)])
                stt("dve", B["mid"][0:nq, k + 1:k + 2], B["tmp"][0:nq, k:k + 1], B["mid"][0:nq, k:k + 1],
                    B["dh"][0:nq, k + 1:k + 2], ALU.add, ALU.subtract, [tok("bist%d" % i), tok("bis%d" % i)],
                    [tok("bis%d" % i)])

            for k in range(NIT):
                if on_act is not None:
                    cnt_op(tk[1], k)
                    cnt_op(tk[0], k)
                    upd_op(tk[1], k)
                    upd_op(tk[0], k)
                else:
                    for d in tk:
                        cnt_op(d, k)
                    for d in tk:
                        upd_op(d, k)
            sub_mark('att p2')
            for d in subs:
                i, nq, q0, SK = d["i"], d["nq"], d["q0"], d["SK"]
                thr = bis[i]["mid"][0:nq, NIT:NIT + 1] if d["topk"] else -1.0e29
                ts("dve", maskm1[i][0:nq, 0:SK], score[i][0:nq, 0:SK], thr, -1.0, ALU.is_ge, ALU.add,
                   [tok("score%d" % i), tok("bis%d" % i)], [tok("mask")])
                if nq < 128:
                    memset("dve", maskm1[i][nq:128, 0:SK], 0.0, [], [tok("mask")])
                ob = 6 if i == 0 else 0
                steps3 = [(g, kb, b0, bn) for g in range(2) for kb, (b0, bn) in enumerate(d["blocks"])]
                nb = len(d["blocks"])
                st3 = {}

                def p3_front(k):
                    g, kb, b0, bn = steps3[k]
                    b = nxt("px", 2, 2)
                    mm(v3(banks[b][0:bn, 0:4 * nq], 4), kT[0:128, b0:b0 + bn], qTz[0:128, g, 0:4, q0:q0 + nq],
                       True, False, [tok("qT")] + d["ktoks"], [btok[b]])
                    mm(v3(banks[b][0:bn, 0:4 * nq], 4), maskm1[i][0:128, b0:b0 + bn], I4B[0:128, 0:4, 0:nq],
                       False, True, [tok("mask"), tok("I4B")], [btok[b]])
                    s = nxt("pT", 2, 0)
                    act(pT[s][0:bn, 0:4 * nq], banks[b][0:bn, 0:4 * nq], AF.Exp, [btok[b]], [tok("pT%d" % s)],
                        scale=0.125)
                    st3[k] = s

                def p3_back(k):
                    g, kb, b0, bn = steps3[k]
                    s = st3[k]
                    mm(banks[ob + g][0:128, 0:4 * nq], v_sb[0:bn, b0 // 128, 0:128], pT[s][0:bn, 0:4 * nq],
                       kb == 0, kb == nb - 1, [tok("pT%d" % s)] + d["vtoks"], [btok[ob + g]])
                    mm(banks[4 + g][0:128, 0:4 * nq], ones_bf[0:bn, 0:128], pT[s][0:bn, 0:4 * nq],
                       kb == 0, kb == nb - 1, [tok("pT%d" % s), tok("ones")], [btok[4 + g]])

                def normalise(g):
                    rs = slice(g * 64, (g + 1) * 64)
                    act(lnd[rs, 0:4 * nq], banks[4 + g][rs, 0:4 * nq], AF.Ln, [btok[4 + g]], [tok("dg")])
                    act(rden[rs, 0:4 * nq], lnd[rs, 0:4 * nq], AF.Exp, [tok("dg")], [tok("rden")], scale=-1.0)
                    tt("dve", rden[rs, 0:4 * nq], banks[ob + g][rs, 0:4 * nq], rden[rs, 0:4 * nq], ALU.mult,
                       [btok[ob + g], tok("rden")], [tok("rden")])

                for k in range(len(steps3) + 1):
                    if k < len(steps3):
                        p3_front(k)
                    if k >= 1:
                        p3_back(k - 1)
                        if steps3[k - 1][1] == nb - 1:
                            normalise(steps3[k - 1][0])
                tt("dve", sga[:, 0:4, q0:q0 + nq], v3(rden[:, 0:4 * nq], 4), sga[:, 0:4, q0:q0 + nq], ALU.mult,
                   [tok("rden"), tok("sga")], [tok("sga")])

        def odd_tile(S, t, l, o, skip_norm=False, next_norm=None):
            n = S.TT
            if not skip_norm:
                norm_mod(S, t, l)
            ntg = (n + 127) // 128
            stk = tok("lnst")

            def u_chunk(c):
                b = proj_fm(w_in, None, c * 128, 128, n)
                act(u_sb[:, c, 0:n], banks[b][:, 0:n], AF.Gelu_apprx_tanh, [btok[b]], [tok("u")])

            def g_chunk(c):
                b = proj_fm(w_in, None, 2048 + c * 128, 128, n)
                act(sg_sb[:, c, 0:n], banks[b][:, 0:n], AF.Silu, [btok[b]], [tok("sg")])

            def ln1(tg):
                m = min(128, n - tg * 128)
                for half in range(2):
                    b = nxt("pp", 2, 0)
                    for kc in range(8):
                        mm(banks[b][0:m, 0:512], hT[:, kc, tg * 128:tg * 128 + m],
                           w_in[:, kc, 1024 + half * 512:1024 + (half + 1) * 512], kc == 0, kc == 7,
                           [wtk(1024 + half * 512), tok("hT")], [btok[b]])
                    act(vtm[0:m, half * 512:(half + 1) * 512], banks[b][0:m, 0:512], AF.Gelu_apprx_tanh, [btok[b]],
                        [tok("vtm")])
                ts("dve", vnf[0:m, :], vtm[0:m, :], 1.0 / D, None, ALU.mult, ALU.add, [tok("vtm")], [tok("sq"), stk],
                   accum=st_mean[0:m, 0:1])
                ts("dve", vtm[0:m, :], vtm[0:m, :], st_mean[0:m, 0:1], None, ALU.subtract, None, [tok("vtm"), stk],
                   [tok("vtm")])

            def ln2(tg):
                m = min(128, n - tg * 128)
                act(vnf[0:m, :], vtm[0:m, :], AF.Square, [tok("vtm")], [tok("sq"), stk], scale=1.0 / 32.0,
                    accum=st_var[0:m, 0:1])

            def ln3(tg):
                m = min(128, n - tg * 128)
                act(st_sd[0:m, 0:1], st_var[0:m, 0:1], AF.Sqrt, [stk], [stk], bias=eps_sb[0:m, 0:1])
                recip(st_rstd[0:m, 0:1], st_sd[0:m, 0:1], [stk], [stk])
                stt("dve", vnf[0:m, :], vtm[0:m, :], st_rstd[0:m, 0:1], lng_b[0:m, :], ALU.mult, ALU.mult,
                    [tok("vtm"), stk, tok("lnp")], [tok("sq")])
                tt("dve", vnf[0:m, :], vnf[0:m, :], lnb_b[0:m, :], ALU.add, [tok("sq"), tok("lnp")], [tok("sq")])

            def ln4(tg):
                m = min(128, n - tg * 128)
                cp("act", vn[0:m, tg, :], vnf[0:m, :], [tok("sq")], [tok("vn")])
                if S.sample:
                    dma(o_cvs[o], vnf[0:m, :], [tok("sq")], [], "o_cv")

            chunks = [("u", c) for c in range(8)] + [("g", c) for c in range(8)]
            ci = [0]

            def some_chunks(k):
                for _ in range(k):
                    if ci[0] < len(chunks):
                        kind, c = chunks[ci[0]]
                        ci[0] += 1
                        (u_chunk if kind == "u" else g_chunk)(c)

            for tg in range(ntg):
                if tg > 0:
                    some_chunks(4)
                ln1(tg)
                some_chunks(2)
                ln2(tg)
                some_chunks(2)
                ln3(tg)
                some_chunks(4 if ntg > 1 else 12)
                ln4(tg)
            some_chunks(16)
            run_hook('after_in')
            tt("dve", u_sb[:, :, 0:n], u_sb[:, :, 0:n], sg_sb[:, :, 0:n], ALU.mult, [tok("u"), tok("sg")], [tok("u")])
            for tg in range(ntg):
                m = min(128, n - tg * 128)
                for half in range(2):
                    b = nxt("px", 2, 2)
                    for gi in range(4):
                        gidx = half * 4 + gi
                        mm(banks[b][:, gi * 128:gi * 128 + m], vn[0:m, tg, gidx * 128:(gidx + 1) * 128],
                           wmT[0:m, gidx, 0:m], True, True, [tok("vn"), tok("wmT")], [btok[b]])
                    pv = v3(banks[b][:, 0:512], 4)[:, :, 0:m]
                    tt("dve", v3(s_tmp[:, 0:512], 4)[:, :, 0:m], pv, bs_b[:, half * 4:half * 4 + 4, 0:m], ALU.add,
                       [btok[b], tok("lnp")], [tok("tmpn0"), tok("tmpn1")])
                    tt("dve", gated[:, half * 4:half * 4 + 4, tg * 128:tg * 128 + m], v3(s_tmp[:, 0:512], 4)[:, :, 0:m],
                       u_sb[:, half * 4:half * 4 + 4, tg * 128:tg * 128 + m], ALU.mult,
                       [tok("tmpn0"), tok("tmpn1"), tok("u")], [tok("gated")])
            rhs = [(gated[:, kc, 0:n], [tok("gated")]) for kc in range(8)]
            if next_norm is not None:
                norm_mod(*next_norm)
            out_proj_residual(S, t, l, tok("w_out_c0"), rhs)
            run_hook("after_out")

        fin = [0]

        def final_tile(S, t):
            n = S.TT
            xt = S.x(t)
            fi = fin[0]
            fin[0] = 1 - fi
            rs_, rtok = rms_stats(S, t, alt=(fi == 1))
            yst = yst2[fi]
            for kc in range(8):
                stt("dve", yst[:, kc, 0:n], xt[:, kc, :], fg[:, kc:kc + 1], rs_[:, 0:n], ALU.mult, ALU.mult,
                    [S.xtok[t], rtok], [tok("yst%d" % fi)])
            if S.sample:
                dma(o_ys.rearrange("(kc p) t -> p kc t", p=128), yst[:, :, 0:n], [tok("yst%d" % fi)], [], "o_y%d" % fi)
            else:
                dma(o_yp[S.pi, :, t * TT:(t + 1) * TT].rearrange("(kc p) t -> p kc t", p=128), yst[:, :, 0:n],
                    [tok("yst%d" % fi)], [], "o_y%d" % fi)

        LAYERS = [(pi, l) for pi in range(2) for l in range(4)]

        def issue_in(idx):
            if idx >= len(LAYERS):
                return
            pi, l = LAYERS[idx]
            if l % 2 == 0:
                issue_w(d_wine[l // 2], w_in, EVEN_IN, "w_in")
            else:
                issue_w(d_wino[l // 2], w_in, 3 * D, "w_in")

        def issue_out(idx):
            if idx >= len(LAYERS):
                return
            pi, l = LAYERS[idx]
            issue_w((d_woute if l % 2 == 0 else d_wouto)[l // 2], w_out, D, "w_out")

        def pool_dma(out, in_, writes, key):
            P.add("pool", (lambda o, i_: (lambda e: e.dma_start(out=o, in_=i_)))(out, in_), [], writes, dma_key=key)

        def main_schedule():
            issue_in(0)
            issue_out(0)
            for pi in range(2):
                SP_ = mkseq(pi)
                seqs = [SP_] + ([SS] if pi == 1 else [])
                barrier()
                stage_mark("xload%d" % pi)
                for t in range(NT):
                    dma(x_sb[:, :, t * TT:(t + 1) * TT],
                        d_xp[pi, :, t * TT:(t + 1) * TT].rearrange("(kc p) t -> p kc t", p=128),
                        [], [SP_.xtok[t]], "xl%d" % t)
                if pi == 1:
                    dma(xs_sb[:, :, :], d_xs.rearrange("(kc p) t -> p kc t", p=128), [], [SS.xtok[0]], "xs")
                for l in range(4):
                    idx = pi * 4 + l
                    barrier()
                    stage_mark("layer p%d l%d" % (pi, l))
                    last_S = seqs[-1]
                    if l % 2 == 0:
                        e = l // 2
                        for r_ in range(4):
                            ts("dve", I4B[:, r_, :], ident_f[:, :], 30000.0, None, ALU.mult, None, [], [tok("I4B")])
                        memset("dve", qTz.rearrange("p g r t -> p (g r t)"), 0.0, [], [tok("qT")])
                        memset("dve", qiTz.rearrange("p h t -> p (h t)"), 0.0, [], [tok("qiT")])
                        barrier()
                        for S in seqs:
                            if S.sample:
                                pool_dma(kT[:, 0:PAST], d_ckT[e], [tok("kT%d" % i) for i in range(4)], "c_k")
                                pool_dma(kiT[0:64, 0:PAST], d_ckiT[e], [tok("kiT%d" % i) for i in range(4)], "c_ki")
                                pool_dma(kiT[64:128, 0:PAST], d_ckiT[e], [tok("kiT%d" % i) for i in range(4)], "c_ki2")
                                pool_dma(v_sb[:, 0:8, :], d_cv[e].rearrange("(kb p) c -> p kb c", p=128),
                                         [tok("v%d" % i) for i in range(4)], "c_v")
                                dma(zp_s[:, :, 0:2], d_cconv[e], [], SS.zptok, "cconv")
                            else:
                                for c in range(4):
                                    memset("pool", zp_p[:, c, 0:2], 0.0, [], [S.zptok[c]])
                            for t in range(S.nt):
                                stage_mark("even tile p%d l%d t%d" % (pi, l, t))
                                if S is last_S and t == S.nt - 1:
                                    HOOK["after_in"] = (lambda i_=idx: issue_in(i_ + 1))
                                    HOOK["after_out"] = (lambda i_=idx: issue_out(i_ + 1))
                                even_tile(S, t, l, e, skip_norm=(t > 0),
                                          next_norm=((S, t + 1, l) if t + 1 < S.nt else None))
                    else:
                        o = l // 2
                        dma(v3(wst_f, 8), d_wsT[o], [], [tok("wst")], "wst")
                        tt("dve", wmT[:, :, :], v3(wst_f, 8), tril_f[:, :].unsqueeze(1).to_broadcast([128, 8, 128]),
                           ALU.mult, [tok("wst")], [tok("wmT")])
                        dma(lng_b, d_lng[o:o + 1, :].partition_broadcast(128), [], [tok("lnp")], "lnp")
                        dma(lnb_b, d_lnb[o:o + 1, :].partition_broadcast(128), [], [tok("lnp2")], "lnp2")
                        dma(bs_b.rearrange("p a b -> p (a b)"), d_bs[o:o + 1, :].partition_broadcast(128), [],
                            [tok("lnp3")], "lnp3")
                        barrier()
                        for S in seqs:
                            for t in range(S.nt):
                                stage_mark("odd tile p%d l%d t%d" % (pi, l, t))
                                if S is last_S and t == S.nt - 1:
                                    HOOK["after_in"] = (lambda i_=idx: issue_in(i_ + 1))
                                    HOOK["after_out"] = (lambda i_=idx: issue_out(i_ + 1))
                                odd_tile(S, t, l, o, skip_norm=(t > 0),
                                         next_norm=((S, t + 1, l) if t + 1 < S.nt else None))
                barrier()
                for S in seqs:
                    for t in range(S.nt):
                        stage_mark("final p%d t%d" % (pi, t))
                        final_tile(S, t)

        try:
            main_schedule()
        except _Stop:
            pass
        barrier()
        P.emit()
    return nc


_STAGE_LIMIT = None
_SUB_MARKS = False
_CAST_ENG = "act"

_NC_CACHE = {}


def kernel(x_prompt, x_sample, cache_a_k, cache_a_v, cache_a_kidx, state_b_conv, c_prompt, c_sample,
           ada_w, ada_b, norm_g, ev_w_in, ev_conv_w, ev_w_out, od_w_in, od_ws, od_bs, od_ln_g, od_ln_b,
           od_w_out, final_g):
    f = lambda a: np.ascontiguousarray(np.asarray(a, dtype=np.float32))
    x_prompt = np.asarray(x_prompt, np.float32); x_sample = np.asarray(x_sample, np.float32)
    perm = np.arange(512).reshape(2, 4, 64).transpose(1, 0, 2).reshape(-1)
    cols = np.arange(EVEN_IN)
    cols[0:512] = perm
    cols[768:1280] = 768 + perm
    w_in_e = f(np.asarray(ev_w_in)[:, :, cols])
    rows = np.arange(D)
    rows[0:512] = perm
    w_out_e = f(np.asarray(ev_w_out)[:, rows, :])
    shared = {
        "ada_w": f(ada_w),
        "ada_b": f(np.asarray(ada_b).reshape(4, 24, 128).transpose(2, 0, 1)),
        "norm_g": f(np.asarray(norm_g).reshape(4, 8, 128).transpose(2, 0, 1)),
        "final_g": f(np.asarray(final_g).reshape(8, 128).T),
        "w_in_e": w_in_e,
        "conv_w": f(np.asarray(ev_conv_w).reshape(2, 3, 4, 128).transpose(3, 0, 2, 1)),
        "w_out_e": w_out_e,
        "w_in_o": f(od_w_in),
        "wsT": f(np.asarray(od_ws).transpose(0, 3, 1, 2)),
        "bs": f(np.asarray(od_bs).reshape(2, D)),
        "ln_g": f(od_ln_g), "ln_b": f(od_ln_b),
        "w_out_o": f(od_w_out),
        "ident": np.eye(128, dtype=np.float32),
        "trilT": np.triu(np.ones((128, 128), np.float32)),
        "pow2": np.tile(np.array([2.0 ** -(k + 1) for k in range(NIT)] + [2.0 ** -NIT], np.float32)[None, :], (128, 1)),
    }
    ck = np.asarray(cache_a_k, np.float32); cv = np.asarray(cache_a_v, np.float32)
    cki = np.asarray(cache_a_kidx, np.float32); cst = np.asarray(state_b_conv, np.float32)
    in_maps = []
    for i in range(8):
        m = dict(shared)
        m["xp"] = f(x_prompt[2 * i:2 * i + 2].transpose(0, 2, 1))
        m["xs"] = f(x_sample[i].T)
        m["cT"] = f(np.stack([np.asarray(c_prompt)[2 * i], np.asarray(c_prompt)[2 * i + 1], np.asarray(c_sample)[i]], 1))
        m["ckT"] = f(ck[:, i].reshape(2, PAST, 128).transpose(0, 2, 1))
        m["cv"] = f(cv[:, i].reshape(2, PAST, 128))
        m["ckiT"] = f(cki[:, i].transpose(0, 2, 1))
        m["cconv"] = f(cst[:, i].reshape(2, 2, 4, 128).transpose(0, 3, 2, 1))
        in_maps.append(m)
    if "nc" not in _NC_CACHE:
        _NC_CACHE["nc"] = build_nc()
    res = run_bass_kernel_spmd(_NC_CACHE["nc"], in_maps, core_ids=list(range(8)))
    R = res.results
    y_p = np.empty((16, SEQ, D), np.float32); y_s = np.empty((8, DEC, D), np.float32)
    k_p = np.empty((2, 16, SEQ, 2, 64), np.float32); v_p = np.empty((2, 16, SEQ, 2, 64), np.float32)
    ki_p = np.empty((2, 16, SEQ, 64), np.float32); conv_p = np.empty((2, 16, 2, 512), np.float32)
    k_s = np.empty((2, 8, DEC, 2, 64), np.float32); v_s = np.empty((2, 8, DEC, 2, 64), np.float32)
    ki_s = np.empty((2, 8, DEC, 64), np.float32); conv_s = np.empty((2, 8, 2, 512), np.float32)
    cv_s = np.empty((2, 8, DEC, D), np.float32)
    for i in range(8):
        r = R[i]
        for s in range(2):
            b = 2 * i + s
            y_p[b] = r["yp"][s].T
            k_p[:, b] = r["kp"][:, s].transpose(0, 2, 1).reshape(2, SEQ, 2, 64)
            v_p[:, b] = r["vp"][:, s].reshape(2, SEQ, 2, 64)
            ki_p[:, b] = r["kip"][:, s].transpose(0, 2, 1)
            conv_p[:, b] = r["convp"][:, s].transpose(0, 3, 2, 1).reshape(2, 2, 512)
        y_s[i] = r["ys"].T
        k_s[:, i] = r["ks"].transpose(0, 2, 1).reshape(2, DEC, 2, 64)
        v_s[:, i] = r["vs"].reshape(2, DEC, 2, 64)
        ki_s[:, i] = r["kis"].transpose(0, 2, 1)
        conv_s[:, i] = r["convs"].transpose(0, 3, 2, 1).reshape(2, 2, 512)
        cv_s[:, i] = r["cvs"]
    return (y_p, y_s, k_p, v_p, ki_p, conv_p, k_s, v_s, ki_s, conv_s, cv_s)
```

```python
import contextlib
import numpy as np
import concourse.bass as bass
import concourse.mybir as mybir
from concourse.bass_utils import run_bass_kernel_spmd

F32 = mybir.dt.float32
BF16 = mybir.dt.bfloat16
AF = mybir.ActivationFunctionType
ALU = mybir.AluOpType
AX = mybir.AxisListType

D = 1024
SEQ = 2048
TT = 256
NT = SEQ // TT
DEC = 64
PAST = 1024
EVEN_IN = 3912
NIT = 16
EPS = 1e-6
NEG = -1.0e30
_SYNC_SAME = True
COMPUTE = ("pe", "act", "dve", "pool")


class Tok:
    __slots__ = ("name", "writer", "readers")

    def __init__(self, name):
        self.name = name
        self.writer = None
        self.readers = {}


class Op:
    __slots__ = ("eng", "fn", "deps", "inc", "val", "dma_key", "is_dma", "soft")

    def __init__(self, eng, fn, dma_key=None):
        self.eng = eng
        self.fn = fn
        self.deps = []
        self.inc = False
        self.val = None
        self.dma_key = dma_key
        self.is_dma = dma_key is not None
        self.soft = ()


class Prog:
    def __init__(self, nc):
        self.nc = nc
        self.ops = {e: [] for e in ("pe", "act", "dve", "pool", "sp")}
        self.GB = Tok("GB")
        self.gap_fn = {}

    def add(self, eng, fn, reads=(), writes=(), dma_key=None, barrier=False):
        op = Op(eng, fn, dma_key)
        reads = list(reads)
        writes = list(writes)
        if barrier:
            writes.append(self.GB)
        else:
            reads.append(self.GB)
        deps = []
        soft = []
        gap_ok = (not _SYNC_SAME) and eng in ("act", "dve") and not op.is_dma
        for t in reads:
            w = t.writer
            if w is None:
                continue
            if (not w.is_dma) and w.eng == eng:
                if eng == "pe":
                    continue
                if gap_ok:
                    soft.append(w)
                    continue
            deps.append(w)
        for t in writes:
            w = t.writer
            same_ok = eng == "pe" or gap_ok
            if w is not None and not ((not w.is_dma) and (not op.is_dma) and w.eng == eng and same_ok):
                deps.append(w)
            for r in t.readers.values():
                if not ((not r.is_dma) and (not op.is_dma) and r.eng == eng and same_ok):
                    deps.append(r)
        seen = set()
        for d in deps:
            if id(d) not in seen and d is not op:
                seen.add(id(d))
                op.deps.append(d)
        op.soft = soft
        rkey = ("dma", dma_key) if op.is_dma else eng
        for t in reads:
            t.readers[rkey] = op
        for t in writes:
            t.writer = op
            t.readers = {}
        self.ops[eng].append(op)
        return op

    def emit(self):
        nc = self.nc
        self.n_gap = 0
        for lst in self.ops.values():
            for op in lst:
                for d in op.deps:
                    d.inc = True
        dma_cnt = {}
        sems = {}
        with contextlib.ExitStack() as stack:
            for e, lst in self.ops.items():
                cnt = 0
                for op in lst:
                    if op.is_dma:
                        k = ("dma", op.dma_key)
                        dma_cnt[k] = dma_cnt.get(k, 0) + 16
                        op.val = (k, dma_cnt[k])
                        op.inc = True
                    elif op.inc:
                        cnt += 1
                        op.val = (e, cnt)
            for i, k in enumerate(list(dma_cnt.keys()) + list(COMPUTE)):
                sems[k] = stack.enter_context(nc.semaphore("s%d" % i))
            engmap = {"pe": "tensor", "act": "scalar", "dve": "vector", "pool": "gpsimd", "sp": "sync"}
            with nc.Block() as block:
                def mk(ename):
                    def body(eng):
                        waited = {}
                        prev = None
                        for op in self.ops[ename]:
                            if prev is not None and any(d is prev for d in op.soft):
                                self.gap_fn[ename](eng)
                                self.n_gap += 1
                            prev = op
                            need = {}
                            for d in op.deps:
                                k, v = d.val
                                if v > need.get(k, 0):
                                    need[k] = v
                            for k, v in need.items():
                                if waited.get(k, 0) >= v:
                                    continue
                                waited[k] = v
                                eng.wait_ge(sems[k], v)
                            ins = op.fn(eng)
                            if op.inc:
                                ins.then_inc(sems[op.val[0]], 16 if op.is_dma else 1)
                        if ename == "sp":
                            for k, v in dma_cnt.items():
                                if waited.get(k, 0) < v:
                                    eng.wait_ge(sems[k], v)
                    return body
                for en, attr in engmap.items():
                    getattr(block, attr)(mk(en))


def build_nc():
    nc = bass.Bass("TRN2", target_bir_lowering=False)
    P = Prog(nc)

    def din(name, shape):
        return nc.dram_tensor(name, list(shape), F32, kind="ExternalInput").ap()

    def dout(name, shape):
        return nc.dram_tensor(name, list(shape), F32, kind="ExternalOutput").ap()

    d_xp = din("xp", [2, D, SEQ]); d_xs = din("xs", [D, DEC]); d_cT = din("cT", [D, 3])
    d_adaw = din("ada_w", [4, D, 3 * D]); d_adab = din("ada_b", [128, 4, 24])
    d_ng = din("norm_g", [128, 4, 8]); d_fg = din("final_g", [128, 8])
    d_wine = din("w_in_e", [2, D, EVEN_IN]); d_convw = din("conv_w", [128, 2, 4, 3])
    d_woute = din("w_out_e", [2, D, D]); d_wino = din("w_in_o", [2, D, 3 * D])
    d_wsT = din("wsT", [2, 128, 8, 128]); d_bs = din("bs", [2, D]); d_lng = din("ln_g", [2, D])
    d_lnb = din("ln_b", [2, D]); d_wouto = din("w_out_o", [2, D, D])
    d_ckT = din("ckT", [2, 128, PAST]); d_cv = din("cv", [2, PAST, 128]); d_ckiT = din("ckiT", [2, 64, PAST])
    d_cconv = din("cconv", [2, 128, 4, 2])
    d_ident = din("ident", [128, 128]); d_tril = din("trilT", [128, 128]); d_pow2 = din("pow2", [128, NIT + 1])

    o_yp = dout("yp", [2, D, SEQ]); o_ys = dout("ys", [D, DEC])
    o_kp = dout("kp", [2, 2, 128, SEQ]); o_vp = dout("vp", [2, 2, SEQ, 128]); o_kip = dout("kip", [2, 2, 64, SEQ])
    o_convp = dout("convp", [2, 2, 128, 4, 2])
    o_ks = dout("ks", [2, 128, DEC]); o_vs = dout("vs", [2, DEC, 128]); o_kis = dout("kis", [2, 64, DEC])
    o_convs = dout("convs", [2, 128, 4, 2]); o_cvs = dout("cvs", [2, DEC, D])

    ARENA_BYTES = 49152
    with contextlib.ExitStack() as st:
        def sb(name, shape, dt):
            return st.enter_context(nc.sbuf_tensor("sb_" + name, list(shape), dt))

        x_sb = sb("x_sb", [128, 8, SEQ], F32)
        xs_sb = sb("xs_sb", [128, 8, DEC], F32)
        w_in = sb("w_in", [128, 8, EVEN_IN], BF16)
        w_out = sb("w_out", [128, 8, D], BF16)
        kT = sb("kT", [128, SEQ], BF16)
        kiT = sb("kiT", [128, SEQ], BF16)
        v_sb = sb("v_sb", [128, 16, 128], BF16)
        ones_bf = sb("ones_bf", [128, 128], BF16)
        onesm = sb("onesm", [128, 128], BF16)
        ident_f = sb("ident_f", [128, 128], F32)
        ident_bf = sb("ident_bf", [128, 128], BF16)
        tril_f = sb("tril_f", [128, 128], F32)
        pow2 = sb("pow2", [128, NIT + 1], F32)
        adab = sb("adab", [128, 4, 24], F32)
        normg = sb("normg", [128, 4, 8], F32)
        fg = sb("fg", [128, 8], F32)
        convw = sb("convw", [128, 2, 4, 3], F32)
        cT = sb("cT", [128, 8, 3], F32)
        scT = sb("scT", [128, 8, 3], F32)
        mod = sb("mod", [128, 4, 24, 3], F32)
        Gm = sb("Gm", [128, 4, 3, 8], F32)
        arena = sb("arena", [128, ARENA_BYTES // 2], BF16)
        banks = [st.enter_context(nc.psum_tensor("bank%d" % i, [128, 512], F32)) for i in range(8)]
        btok = [Tok("bank%d" % i) for i in range(8)]

        class Arena:
            def __init__(self):
                self.off = 0

            def alloc(self, nelem, dt):
                nb = nelem * (4 if dt == F32 else 2)
                nb = (nb + 63) // 64 * 64
                a = self.off
                self.off += nb
                assert self.off <= ARENA_BYTES, ("arena overflow", self.off)
                v = arena[:, a // 2:(a + nb) // 2]
                if dt == F32:
                    v = v.bitcast(F32)
                    return v[:, 0:nelem]
                return v[:, 0:nelem]

        def v3(ap, a):
            return ap.rearrange("p (a b) -> p a b", a=a)

        A = Arena()
        stage = [A.alloc(1024, F32) for _ in range(3)]
        stage_tok = [Tok("stage%d" % i) for i in range(3)]
        wst_f = A.alloc(8 * 128, F32)
        A = Arena()
        hT = v3(A.alloc(8 * TT, BF16), 8)
        sq = v3(A.alloc(8 * TT, BF16), 8)
        sq_off = A.off - 8 * TT * 2
        rstd_t = A.alloc(TT, F32)
        rstd = A.alloc(TT, F32)
        tmpn_off = A.off
        tmpn = [A.alloc(TT, F32) for _ in range(2)]
        common_off = A.off
        xin_sb = A.alloc(TT, F32); gb_sb = A.alloc(TT, F32); sgz_sb = A.alloc(TT, F32); y_sb = A.alloc(TT, F32)
        kst = A.alloc(TT, F32); kist = A.alloc(TT, F32); vst = v3(A.alloc(2 * 128, F32), 2)
        phB_end = A.off
        A = Arena()
        score = [A.alloc(SEQ, F32) for _ in range(2)]
        _mk = A.alloc(SEQ, BF16)
        maskm1 = [_mk, _mk]
        junk_off = A.off
        _dg = v3(A.alloc(8 * 128, BF16), 8)
        dg = [_dg, _dg]
        rl = [A.alloc(512, BF16) for _ in range(2)]
        pT = [A.alloc(512, BF16) for _ in range(2)]
        rden = A.alloc(512, F32)
        junk1 = arena[:, junk_off // 2:junk_off // 2 + SEQ]
        lnd = arena[:, junk_off // 2:junk_off // 2 + 1024].bitcast(F32)
        assert A.off - junk_off >= 2 * SEQ
        bis = []
        for _i in range(2):
            bis.append(dict(dh=A.alloc(NIT + 1, F32), mid=A.alloc(NIT + 1, F32), cnt=A.alloc(NIT, F32),
                            tmp=A.alloc(NIT, F32), mx=A.alloc(1, F32), mn=A.alloc(1, F32), d0=A.alloc(1, F32)))
        A.off = max(A.off, phB_end)
        qTz = A.alloc(2 * 4 * TT, BF16).rearrange("p (g r t) -> p g r t", g=2, r=4)
        qiTz = v3(A.alloc(8 * TT, BF16), 8)
        sga = v3(A.alloc(4 * TT, BF16), 4)
        b_out = v3(A.alloc(4 * TT, BF16), 4)
        wi_sb = v3(A.alloc(2 * 8, F32), 2)
        zp_p = v3(A.alloc(4 * (TT + 2), F32), 4)
        zp_s = v3(A.alloc(4 * (DEC + 2), F32), 4)
        I4B = v3(A.alloc(4 * 128, BF16), 4)
        even_end = A.off
        A = Arena(); A.off = common_off
        u_sb = v3(A.alloc(8 * TT, BF16), 8)
        sg_sb = v3(A.alloc(8 * TT, BF16), 8)
        gated = v3(A.alloc(8 * TT, BF16), 8)
        vtm = A.alloc(D, F32)
        vn = v3(A.alloc(2 * D, BF16), 2)
        lng_b = A.alloc(D, F32); lnb_b = A.alloc(D, F32)
        bs_b = v3(A.alloc(D, F32), 8)
        wmT = v3(A.alloc(8 * 128, BF16), 8)
        st_mean = A.alloc(1, F32); st_var = A.alloc(1, F32); st_rstd = A.alloc(1, F32); st_sd = A.alloc(1, F32)
        odd_end = A.off
        vnf = arena[:, sq_off // 2:sq_off // 2 + 2 * D].bitcast(F32)
        s_tmp = arena[:, tmpn_off // 2:tmpn_off // 2 + 1024].bitcast(F32)
        A = Arena(); A.off = common_off
        yst2 = [v3(A.alloc(8 * TT, F32), 8) for _ in range(2)]
        sqF = v3(A.alloc(8 * TT, BF16), 8); rstd_tF = A.alloc(TT, F32); rstdF = A.alloc(TT, F32)
        print("arena: even", even_end, "odd", odd_end, "of", ARENA_BYTES)

        T = {}

        def tok(n):
            if n not in T:
                T[n] = Tok(n)
            return T[n]

        def dma(out, in_, reads, writes, key):
            P.add("sp", lambda e: e.dma_start(out=out, in_=in_), reads, writes, dma_key=key)

        def mm(out, lhsT, rhs, start, stop, reads, writes, tp=None):
            if tp is None:
                P.add("pe", lambda e: e.matmul(out, lhsT=lhsT, rhs=rhs, start=start, stop=stop), reads, writes)
            else:
                P.add("pe", lambda e: e.matmul(out, lhsT=lhsT, rhs=rhs, start=start, stop=stop, tile_position=tp),
                      reads, writes)

        def act(out, in_, func, reads, writes, scale=1.0, bias=None, accum=None):
            kw = {}
            if bias is not None:
                kw["bias"] = bias
            if accum is not None:
                kw["accum_out"] = accum
            P.add("act", lambda e: e.activation(out=out, in_=in_, func=func, scale=scale, **kw), reads, writes)

        def ts(eng, out, in0, s1, s2, op0, op1, reads, writes, accum=None):
            if accum is not None:
                P.add(eng, lambda e: e.tensor_scalar(out=out, in0=in0, scalar1=s1, scalar2=s2, op0=op0, op1=op1,
                                                     accum_out=accum), reads, writes)
            elif op1 is None:
                P.add(eng, lambda e: e.tensor_scalar(out=out, in0=in0, scalar1=s1, scalar2=None, op0=op0), reads, writes)
            else:
                P.add(eng, lambda e: e.tensor_scalar(out=out, in0=in0, scalar1=s1, scalar2=s2, op0=op0, op1=op1),
                      reads, writes)

        def tt(eng, out, in0, in1, op, reads, writes):
            P.add(eng, lambda e: e.tensor_tensor(out=out, in0=in0, in1=in1, op=op), reads, writes)

        def stt(eng, out, in0, scalar, in1, op0, op1, reads, writes):
            P.add(eng, lambda e: e.scalar_tensor_tensor(out=out, in0=in0, scalar=scalar, in1=in1, op0=op0, op1=op1),
                  reads, writes)

        def cp(eng, out, in_, reads, writes):
            if eng == "act":
                P.add("act", lambda e: e.activation(out=out, in_=in_, func=AF.Copy), reads, writes)
            else:
                P.add(eng, lambda e: e.tensor_copy(out=out, in_=in_), reads, writes)

        def memset(eng, ap, val, reads, writes):
            P.add(eng, lambda e: e.memset(ap, val), reads, writes)

        def recip(out, in_, reads, writes):
            P.add("dve", lambda e: e.reciprocal(out=out, in_=in_), reads, writes)

        def barrier():
            P.add("pool", lambda e: e.memset(bar_sb[:, 0:1], 0.0), [], [], barrier=True)

        bar_sb = sb("bar_sb", [128, 2], F32)
        rr = {"pp": 0, "px": 0, "pl": 0, "st": 0, "rl": 0, "pT": 0, "tmpn": 0, "pa": 0}

        def nxt(kind, n, base):
            i = rr[kind]
            rr[kind] = (i + 1) % n
            return base + i

        class _Stop(Exception):
            pass

        stage_ctr = [0]

        def stage_mark(name):
            stage_ctr[0] += 1
            if _STAGE_LIMIT is not None and stage_ctr[0] > _STAGE_LIMIT:
                print("build stopped before stage", stage_ctr[0], name)
                raise _Stop()

        def sub_mark(name):
            if _SUB_MARKS:
                stage_mark(name)

        cst = 0

        def cload(dst, src):
            nonlocal cst
            cst += 1
            dma(dst, src, [], [tok("c%d" % cst)], "const")

        cload(ident_f[:], d_ident); cload(tril_f[:], d_tril); cload(pow2[:], d_pow2)
        cload(adab[:], d_adab); cload(normg[:], d_ng); cload(fg[:], d_fg); cload(convw[:], d_convw)
        cload(cT[:], d_cT.rearrange("(kc p) j -> p kc j", p=128))
        memset("pool", ones_bf[:], 1.0, [], [tok("ones")])
        memset("pool", onesm[:], 1.0 / D, [], [tok("onesm")])
        barrier()
        cp("pool", ident_bf[:], ident_f[:], [], [tok("identbf")])
        act(scT[:], cT[:], AF.Silu, [], [tok("scT")])
        barrier()

        ada_st = [v3(arena[:, i * 8192:(i + 1) * 8192].bitcast(F32), 8) for i in range(2)]
        ada_tok = [Tok("adast0"), Tok("adast1")]
        ada_rr = 0
        modtm = arena[:, 16384:16384 + 6144].bitcast(F32)
        for l in range(4):
            for cc in range(6):
                s = ada_rr
                ada_rr = 1 - ada_rr
                bk = cc % 2
                dma(ada_st[s], d_adaw[l, :, cc * 512:(cc + 1) * 512].rearrange("(kc p) c -> p kc c", p=128), [],
                    [ada_tok[s]], "adast%d" % s)
                for kc in range(8):
                    mm(banks[bk][0:3, 0:512], scT[:, kc, :], ada_st[s][:, kc, :], kc == 0, kc == 7,
                       [ada_tok[s], tok("scT")], [btok[bk]])
                cp("dve", modtm[0:3, cc * 512:(cc + 1) * 512], banks[bk][0:3, 0:512], [btok[bk]], [tok("modtm")])
            for oc in range(24):
                mm(banks[2][:, oc * 3:(oc + 1) * 3], modtm[0:3, oc * 128:(oc + 1) * 128], ident_f[0:3, 0:3], True, True,
                   [tok("modtm")], [btok[2]])
            for j in range(3):
                tt("dve", mod[:, l, :, j], v3(banks[2][:, 0:72], 24)[:, :, j], adab[:, l, :], ALU.add,
                   [btok[2]], [tok("mod")])
            for j in range(3):
                stt("dve", Gm[:, l, j, :], mod[:, l, 8:16, j], 1.0, normg[:, l, :], ALU.add, ALU.mult,
                    [tok("mod")], [tok("Gm")])
        barrier()

        def issue_w(dram_w, dst, C, name):
            for ci in range((C + 1023) // 1024):
                c0 = ci * 1024
                n = min(1024, C - c0)
                P.add("pool", (lambda o, i_: (lambda e: e.dma_start(out=o, in_=i_)))(
                    dst[:, :, c0:c0 + n], dram_w[:, c0:c0 + n].rearrange("(kc p) c -> p kc c", p=128)),
                    [], [tok("%s_c%d" % (name, ci))], dma_key="%s_c%d" % (name, ci))

        def wtk(c0):
            return tok("w_in_c%d" % (c0 // 1024))

        HOOK = {"after_in": None, "after_out": None}

        def run_hook(k):
            f = HOOK[k]
            HOOK[k] = None
            if f is not None:
                f()

        class Seq:
            pass

        def mkseq(pi):
            S = Seq()
            S.sample = False; S.j = pi; S.TT = TT; S.nt = NT; S.pi = pi
            S.x = lambda t: x_sb[:, :, t * TT:(t + 1) * TT]
            S.xtok = [tok("x%d" % t) for t in range(NT)]
            S.zp = zp_p; S.zptok = [tok("zp%d" % c) for c in range(4)]
            return S

        SS = Seq()
        SS.sample = True; SS.j = 2; SS.TT = DEC; SS.nt = 1; SS.pi = 0
        SS.x = lambda t: xs_sb[:, :, :]
        SS.xtok = [tok("xs")]
        SS.zp = zp_s; SS.zptok = [tok("zps%d" % c) for c in range(4)]

        def rms_stats(S, t, alt=False):
            n = S.TT
            xt = S.x(t)
            sq_, rt_, rs_, sfx = (sqF, rstd_tF, rstdF, "F") if alt else (sq, rstd_t, rstd, "")
            act(sq_[:, :, 0:n], xt, AF.Square, [S.xtok[t]], [tok("sq" + sfx)])
            b = nxt("px", 2, 2)
            for kc in range(8):
                mm(banks[b][:, 0:n], onesm[:, :], sq_[:, kc, 0:n], kc == 0, kc == 7, [tok("sq" + sfx), tok("onesm")],
                   [btok[b]])
            act(rt_[:, 0:n], banks[b][:, 0:n], AF.Sqrt, [btok[b]], [tok("rstd_t" + sfx)], bias=eps_sb[:, 0:1])
            recip(rs_[:, 0:n], rt_[:, 0:n], [tok("rstd_t" + sfx)], [tok("rstd" + sfx)])
            return rs_, tok("rstd" + sfx)

        eps_sb = sb("eps_sb", [128, 1], F32)
        memset("pool", eps_sb[:], EPS, [], [tok("eps")])

        def norm_mod(S, t, l):
            n = S.TT
            xt = S.x(t)
            rms_stats(S, t)
            for kc in range(8):
                s = nxt("tmpn", 2, 0)
                stt("dve", tmpn[s][:, 0:n], xt[:, kc, :], Gm[:, l, S.j, kc:kc + 1], rstd[:, 0:n], ALU.mult, ALU.mult,
                    [S.xtok[t], tok("rstd"), tok("Gm")], [tok("tmpn%d" % s)])
                act(hT[:, kc, 0:n], tmpn[s][:, 0:n], AF.Identity, [tok("tmpn%d" % s), tok("mod")], [tok("hT")],
                    bias=mod[:, l, kc, S.j:S.j + 1])

        def proj_fm(W, wtok, c0, M, n, p0=0):
            b = nxt("pp", 2, 0)
            for kc in range(8):
                mm(banks[b][p0:p0 + M, 0:n], W[:, kc, c0:c0 + M], hT[:, kc, 0:n], kc == 0, kc == 7,
                   [wtk(c0), tok("hT")], [btok[b]], tp=(0, p0) if p0 else None)
            return b

        def out_proj_residual(S, t, l, wtok, rhs_list):
            n = S.TT
            xt = S.x(t)
            for oc in range(8):
                b = nxt("pp", 2, 0)
                for i, (rap, rtoks) in enumerate(rhs_list):
                    mm(banks[b][:, 0:n], w_out[:, i, oc * 128:(oc + 1) * 128], rap, i == 0, i == len(rhs_list) - 1,
                       [wtok] + rtoks, [btok[b]])
                stt("dve", xt[:, oc, :], banks[b][:, 0:n], mod[:, l, 16 + oc, S.j:S.j + 1], xt[:, oc, :],
                    ALU.mult, ALU.add, [btok[b], tok("mod"), S.xtok[t]], [S.xtok[t]])

        def even_tile(S, t, l, e, skip_norm=False, next_norm=None):
            n = S.TT
            wt = tok("w_in")
            key0 = PAST if S.sample else t * TT
            if not skip_norm:
                norm_mod(S, t, l)
            sub_mark('after norm')
            for r in range(4):
                b = proj_fm(w_in, wt, r * 128, 128, n)
                cp("act", qTz[0:64, 0, r, 0:n], banks[b][0:64, 0:n], [btok[b]], [tok("qT")])
                cp("act", qTz[64:128, 1, r, 0:n], banks[b][64:128, 0:n], [btok[b]], [tok("qT")])
            b = proj_fm(w_in, wt, 512, 128, n)
            cp("dve", kst[:, 0:n], banks[b][:, 0:n], [btok[b]], [tok("kst")])
            cp("act", kT[:, key0:key0 + n], kst[:, 0:n], [tok("kst")], [tok("kT%d" % (key0 // TT))])
            if S.sample:
                dma(o_ks[e], kst[:, 0:n], [tok("kst")], [], "o_k")
            else:
                dma(o_kp[e, S.pi, :, key0:key0 + n], kst[:, 0:n], [tok("kst")], [], "o_k")
            for r in range(4):
                b = proj_fm(w_in, wt, 768 + r * 128, 128, n)
                act(sga[:, r, 0:n], banks[b][:, 0:n], AF.Silu, [btok[b]], [tok("sga")])
            for jj in range(4):
                b = proj_fm(w_in, wt, 1280 + jj * 128, 128, n)
                cp("dve", qiTz[0:64, 2 * jj, 0:n], banks[b][0:64, 0:n], [btok[b]], [tok("qiT")])
                cp("dve", qiTz[64:128, 2 * jj + 1, 0:n], banks[b][64:128, 0:n], [btok[b]], [tok("qiT")])
            b = nxt("pp", 2, 0)
            for half in range(2):
                for kc in range(8):
                    mm(banks[b][half * 64:(half + 1) * 64, 0:n], w_in[:, kc, 1792:1856], hT[:, kc, 0:n], kc == 0, kc == 7,
                       [wtk(1792), tok("hT")], [btok[b]], tp=(0, 64) if half else None)
            cp("dve", kist[:, 0:n], banks[b][:, 0:n], [btok[b]], [tok("kist")])
            cp("act", kiT[:, key0:key0 + n], kist[:, 0:n], [tok("kist")], [tok("kiT%d" % (key0 // TT))])
            if S.sample:
                dma(o_kis[e], kist[0:64, 0:n], [tok("kist")], [], "o_ki")
            else:
                dma(o_kip[e, S.pi, :, key0:key0 + n], kist[0:64, 0:n], [tok("kist")], [], "o_ki")
            sub_mark('after qkgaqiki')
            ntg = (n + 127) // 128
            for tg in range(ntg):
                m = min(128, n - tg * 128)
                kb = (key0 + tg * 128) // 128
                b = nxt("px", 2, 2)
                for kc in range(8):
                    mm(banks[b][0:m, 0:128], hT[:, kc, tg * 128:tg * 128 + m], w_in[:, kc, 640:768], kc == 0, kc == 7,
                       [wtk(640), tok("hT")], [btok[b]])
                for kc in range(8):
                    mm(banks[b][0:m, 128:136], hT[:, kc, tg * 128:tg * 128 + m], w_in[:, kc, 1856:1864], kc == 0, kc == 7,
                       [wtk(1856), tok("hT")], [btok[b]])
                cp("dve", vst[0:m, tg, :], banks[b][0:m, 0:128], [btok[b]], [tok("vst")])
                cp("act", v_sb[0:m, kb, :], vst[0:m, tg, :], [tok("vst")], [tok("v%d" % (key0 // TT))])
                cp("dve", wi_sb[0:m, tg, :], banks[b][0:m, 128:136], [btok[b]], [tok("wi")])
            if S.sample:
                dma(o_vs[e], vst[0:DEC, 0, :], [tok("vst")], [], "o_v")
            else:
                dma(o_vp[e, S.pi, key0:key0 + n, :].rearrange("(tg p) c -> p tg c", p=128), vst[:, :, :],
                    [tok("vst")], [], "o_v")
            sub_mark('after v/wi')
            zp = S.zp
            for c in range(4):
                zt = S.zptok[c]
                b = proj_fm(w_in, wt, 2888 + c * 128, 128, n)
                cp("act", xin_sb[:, 0:n], banks[b][:, 0:n], [btok[b]], [tok("xin")])
                b = proj_fm(w_in, wt, 2376 + c * 128, 128, n)
                tt("dve", zp[:, c, 2:2 + n], banks[b][:, 0:n], xin_sb[:, 0:n], ALU.mult, [btok[b], tok("xin")], [zt])
                b = proj_fm(w_in, wt, 1864 + c * 128, 128, n)
                cp("act", gb_sb[:, 0:n], banks[b][:, 0:n], [btok[b]], [tok("gb")])
                b = proj_fm(w_in, wt, 3400 + c * 128, 128, n)
                act(sgz_sb[:, 0:n], banks[b][:, 0:n], AF.Silu, [btok[b]], [tok("sgz")])
                ts("dve", y_sb[:, 0:n], zp[:, c, 2:2 + n], convw[:, e, c, 2:3], None, ALU.mult, None, [zt], [tok("y")])
                stt("dve", y_sb[:, 0:n], zp[:, c, 1:1 + n], convw[:, e, c, 1:2], y_sb[:, 0:n], ALU.mult, ALU.add,
                    [zt, tok("y")], [tok("y")])
                stt("dve", y_sb[:, 0:n], zp[:, c, 0:n], convw[:, e, c, 0:1], y_sb[:, 0:n], ALU.mult, ALU.add,
                    [zt, tok("y")], [tok("y")])
                tt("dve", y_sb[:, 0:n], y_sb[:, 0:n], gb_sb[:, 0:n], ALU.mult, [tok("y"), tok("gb")], [tok("y")])
                tt("dve", b_out[:, c, 0:n], y_sb[:, 0:n], sgz_sb[:, 0:n], ALU.mult, [tok("y"), tok("sgz")], [tok("b_out")])
            sub_mark('after conv')
            run_hook('after_in')
            last = (t == S.nt - 1)
            if last:
                if S.sample:
                    dma(o_convs[e], zp[:, :, n:n + 2], [S.zptok[c] for c in range(4)], [], "o_convs")
                else:
                    dma(o_convp[e, S.pi], zp[:, :, n:n + 2], [S.zptok[c] for c in range(4)], [], "o_convp")
            else:
                for c in range(4):
                    cp("dve", zp[:, c, 0:2], zp[:, c, n:n + 2], [S.zptok[c]], [S.zptok[c]])
            nsub = (n + 127) // 128
            barrier()
            attention_tile(S, t, l)
            barrier()
            sub_mark('after attention')
            if next_norm is not None:
                norm_mod(*next_norm)
            rhs = [(sga[:, r, 0:n], [tok("sga")]) for r in range(4)] + [(b_out[:, c, 0:n], [tok("b_out")]) for c in range(4)]
            out_proj_residual(S, t, l, tok("w_out_c0"), rhs)
            run_hook("after_out")

        def attention_tile(S, t, l):
            if S.sample:
                subs = [dict(i=0, nq=DEC, q0=0, blocks=[(kb * 128, 128) for kb in range(8)] + [(PAST, DEC)],
                             corner=False, topk=True)]
            else:
                subs = []
                for sub in range(TT // 128):
                    qt = t * (TT // 128) + sub
                    subs.append(dict(i=sub, nq=128, q0=sub * 128, blocks=[(kb * 128, 128) for kb in range(qt + 1)],
                                     corner=True, topk=qt >= 2))
            for d in subs:
                d["SK"] = d["blocks"][-1][0] + d["blocks"][-1][1]
                nt_ = (d["SK"] + TT - 1) // TT
                d["ktoks"] = [tok("kT%d" % i) for i in range(nt_)]
                d["kitoks"] = [tok("kiT%d" % i) for i in range(nt_)]
                d["vtoks"] = [tok("v%d" % i) for i in range(nt_)]
            steps1 = []
            for d in subs:
                for c0 in range(0, d["SK"], 512):
                    for h in range(8):
                        steps1.append((d, c0, min(512, d["SK"] - c0), h))
            st1 = {}

            def p1_front(k):
                d, c0, m, h = steps1[k]
                i, nq, q0 = d["i"], d["nq"], d["q0"]
                if c0 == 0 and h == 0:
                    tt("dve", dg[i][0:nq, :, 0:nq], ident_bf[0:nq, 0:nq].unsqueeze(1).to_broadcast([nq, 8, nq]),
                       wi_sb[0:nq, i, :].unsqueeze(2).to_broadcast([nq, 8, nq]), ALU.mult,
                       [tok("identbf"), tok("wi")], [tok("dg")])
                b = nxt("px", 2, 2)
                mm(banks[b][0:nq, 0:m], qiTz[0:128, h, q0:q0 + nq], kiT[0:128, c0:c0 + m],
                   True, True, [tok("qiT")] + d["kitoks"], [btok[b]])
                s = nxt("rl", 2, 0)
                act(rl[s][0:nq, 0:m], banks[b][0:nq, 0:m], AF.Relu, [btok[b]], [tok("rl%d" % s)])
                st1[k] = s

            def p1_back(k):
                d, c0, m, h = steps1[k]
                i, nq = d["i"], d["nq"]
                s = st1[k]
                if h == 0:
                    st1["ab"] = nxt("pa", 2, 4)
                ab = st1["ab"]
                mm(banks[ab][0:nq, 0:m], dg[i][0:nq, h, 0:nq], rl[s][0:nq, 0:m], h == 0, h == 7,
                   [tok("dg"), tok("rl%d" % s)], [btok[ab]])
                if h == 7:
                    cp("dve", score[i][0:nq, c0:c0 + m], banks[ab][0:nq, 0:m], [btok[ab]], [tok("score%d" % i)])

            for k in range(len(steps1) + 1):
                flushed = False
                if 1 <= k < len(steps1) and steps1[k][1] == 0 and steps1[k][3] == 0:
                    p1_back(k - 1)
                    flushed = True
                if k < len(steps1):
                    p1_front(k)
                if k >= 1 and not flushed:
                    p1_back(k - 1)
            sub_mark('att p1')
            for d in subs:
                i, nq, SK = d["i"], d["nq"], d["SK"]
                B = bis[i]
                sct, bt = tok("score%d" % i), tok("bis%d" % i)
                if d["topk"]:
                    P.add("dve", (lambda o, a: (lambda e: e.tensor_reduce(out=o, in_=a, axis=AX.X, op=ALU.max)))(
                        B["mx"][0:nq, 0:1], score[i][0:nq, 0:SK]), [sct], [bt])
                    P.add("dve", (lambda o, a: (lambda e: e.tensor_reduce(out=o, in_=a, axis=AX.X, op=ALU.min)))(
                        B["mn"][0:nq, 0:1], score[i][0:nq, 0:SK]), [sct, bt], [bt])
                if d["corner"]:
                    memset("pool", score[i][0:64, SK - 64:SK], NEG, [sct, bt], [sct])
                if d["topk"]:
                    tt("dve", B["d0"][0:nq, :], B["mx"][0:nq, :], B["mn"][0:nq, :], ALU.subtract, [bt], [bt])
                    ts("dve", B["dh"][0:nq, :], pow2[0:nq, :], B["d0"][0:nq, 0:1], None, ALU.mult, None, [bt], [bt])
                    tt("dve", B["mid"][0:nq, 0:1], B["mn"][0:nq, :], B["dh"][0:nq, 0:1], ALU.add, [bt], [bt])
            tk = [d for d in subs if d["topk"]]
            on_act = tk[1]["i"] if len(tk) == 2 else None

            def cnt_op(d, k):
                i, nq, SK = d["i"], d["nq"], d["SK"]
                B = bis[i]
                jbuf = maskm1[0] if i == 0 else junk1
                jtok = [tok("mask")] if i == 0 else [tok("dg"), tok("rl0"), tok("rl1"), tok("pT0"), tok("pT1"),
                                                     tok("rden")]
                if i == on_act:
                    act(jbuf[0:nq, 0:SK], score[i][0:nq, 0:SK], AF.Sign, [tok("score%d" % i), tok("bis%d" % i)],
                        jtok + [tok("bisc%d" % i)], scale=-1.0, bias=B["mid"][0:nq, k:k + 1],
                        accum=B["cnt"][0:nq, k:k + 1])
                else:
                    ts("dve", jbuf[0:nq, 0:SK], score[i][0:nq, 0:SK], B["mid"][0:nq, k:k + 1], None, ALU.is_ge,
                       ALU.add, [tok("score%d" % i), tok("bis%d" % i)], jtok + [tok("bisc%d" % i)],
                       accum=B["cnt"][0:nq, k:k + 1])

            def upd_op(d, k):
                i, nq, SK = d["i"], d["nq"], d["SK"]
                B = bis[i]
                if i == on_act:
                    stt("dve", B["tmp"][0:nq, k:k + 1], B["cnt"][0:nq, k:k + 1], float(SK) - 511.5,
                        B["dh"][0:nq, k:k + 1], ALU.is_le, ALU.mult, [tok("bisc%d" % i), tok("bis%d" % i)],
                        [tok("bist%d" % i)])
                else:
                    stt("dve", B["tmp"][0:nq, k:k + 1], B["cnt"][0:nq, k:k + 1], 255.5, B["dh"][0:nq, k:k + 1],
                        ALU.is_ge, ALU.mult, [tok("bisc%d" % i), tok("bis%d" % i)], [tok("bist%d" % i)])
                stt("dve", B["mid"][0:nq, k + 1:k + 2], B["tmp"][0:nq, k:k + 1], B["mid"][0:nq, k:k + 1],
                    B["dh"][0:nq, k + 1:k + 2], ALU.add, ALU.subtract, [tok("bist%d" % i), tok("bis%d" % i)],
                    [tok("bis%d" % i)])

            for k in range(NIT):
                if on_act is not None:
                    cnt_op(tk[1], k)
                    cnt_op(tk[0], k)
                    upd_op(tk[1], k)
                    upd_op(tk[0], k)
                else:
                    for d in tk:
                        cnt_op(d, k)
                    for d in tk:
                        upd_op(d, k)
            sub_mark('att p2')
            for d in subs:
                i, nq, q0, SK = d["i"], d["nq"], d["q0"], d["SK"]
                thr = bis[i]["mid"][0:nq, NIT:NIT + 1] if d["topk"] else -1.0e29
                ts("dve", maskm1[i][0:nq, 0:SK], score[i][0:nq, 0:SK], thr, -1.0, ALU.is_ge, ALU.add,
                   [tok("score%d" % i), tok("bis%d" % i)], [tok("mask")])
                if nq < 128:
                    memset("dve", maskm1[i][nq:128, 0:SK], 0.0, [], [tok("mask")])
                ob = 6 if i == 0 else 0
                steps3 = [(g, kb, b0, bn) for g in range(2) for kb, (b0, bn) in enumerate(d["blocks"])]
                nb = len(d["blocks"])
                st3 = {}

                def p3_front(k):
                    g, kb, b0, bn = steps3[k]
                    b = nxt("px", 2, 2)
                    mm(v3(banks[b][0:bn, 0:4 * nq], 4), kT[0:128, b0:b0 + bn], qTz[0:128, g, 0:4, q0:q0 + nq],
                       True, False, [tok("qT")] + d["ktoks"], [btok[b]])
                    mm(v3(banks[b][0:bn, 0:4 * nq], 4), maskm1[i][0:128, b0:b0 + bn], I4B[0:128, 0:4, 0:nq],
                       False, True, [tok("mask"), tok("I4B")], [btok[b]])
                    s = nxt("pT", 2, 0)
                    act(pT[s][0:bn, 0:4 * nq], banks[b][0:bn, 0:4 * nq], AF.Exp, [btok[b]], [tok("pT%d" % s)],
                        scale=0.125)
                    st3[k] = s

                def p3_back(k):
                    g, kb, b0, bn = steps3[k]
                    s = st3[k]
                    mm(banks[ob + g][0:128, 0:4 * nq], v_sb[0:bn, b0 // 128, 0:128], pT[s][0:bn, 0:4 * nq],
                       kb == 0, kb == nb - 1, [tok("pT%d" % s)] + d["vtoks"], [btok[ob + g]])
                    mm(banks[4 + g][0:128, 0:4 * nq], ones_bf[0:bn, 0:128], pT[s][0:bn, 0:4 * nq],
                       kb == 0, kb == nb - 1, [tok("pT%d" % s), tok("ones")], [btok[4 + g]])

                def normalise(g):
                    rs = slice(g * 64, (g + 1) * 64)
                    act(lnd[rs, 0:4 * nq], banks[4 + g][rs, 0:4 * nq], AF.Ln, [btok[4 + g]], [tok("dg")])
                    act(rden[rs, 0:4 * nq], lnd[rs, 0:4 * nq], AF.Exp, [tok("dg")], [tok("rden")], scale=-1.0)
                    tt("dve", rden[rs, 0:4 * nq], banks[ob + g][rs, 0:4 * nq], rden[rs, 0:4 * nq], ALU.mult,
                       [btok[ob + g], tok("rden")], [tok("rden")])

                for k in range(len(steps3) + 1):
                    if k < len(steps3):
                        p3_front(k)
                    if k >= 1:
                        p3_back(k - 1)
                        if steps3[k - 1][1] == nb - 1:
                            normalise(steps3[k - 1][0])
                tt("dve", sga[:, 0:4, q0:q0 + nq], v3(rden[:, 0:4 * nq], 4), sga[:, 0:4, q0:q0 + nq], ALU.mult,
                   [tok("rden"), tok("sga")], [tok("sga")])

        def odd_tile(S, t, l, o, skip_norm=False, next_norm=None):
            n = S.TT
            if not skip_norm:
                norm_mod(S, t, l)
            ntg = (n + 127) // 128
            stk = tok("lnst")

            def u_chunk(c):
                b = proj_fm(w_in, None, c * 128, 128, n)
                act(u_sb[:, c, 0:n], banks[b][:, 0:n], AF.Gelu_apprx_tanh, [btok[b]], [tok("u")])

            def g_chunk(c):
                b = proj_fm(w_in, None, 2048 + c * 128, 128, n)
                act(sg_sb[:, c, 0:n], banks[b][:, 0:n], AF.Silu, [btok[b]], [tok("sg")])

            def ln1(tg):
                m = min(128, n - tg * 128)
                for half in range(2):
                    b = nxt("pp", 2, 0)
                    for kc in range(8):
                        mm(banks[b][0:m, 0:512], hT[:, kc, tg * 128:tg * 128 + m],
                           w_in[:, kc, 1024 + half * 512:1024 + (half + 1) * 512], kc == 0, kc == 7,
                           [wtk(1024 + half * 512), tok("hT")], [btok[b]])
                    act(vtm[0:m, half * 512:(half + 1) * 512], banks[b][0:m, 0:512], AF.Gelu_apprx_tanh, [btok[b]],
                        [tok("vtm")])
                ts("dve", vnf[0:m, :], vtm[0:m, :], 1.0 / D, None, ALU.mult, ALU.add, [tok("vtm")], [tok("sq"), stk],
                   accum=st_mean[0:m, 0:1])
                ts("dve", vtm[0:m, :], vtm[0:m, :], st_mean[0:m, 0:1], None, ALU.subtract, None, [tok("vtm"), stk],
                   [tok("vtm")])

            def ln2(tg):
                m = min(128, n - tg * 128)
                act(vnf[0:m, :], vtm[0:m, :], AF.Square, [tok("vtm")], [tok("sq"), stk], scale=1.0 / 32.0,
                    accum=st_var[0:m, 0:1])

            def ln3(tg):
                m = min(128, n - tg * 128)
                act(st_sd[0:m, 0:1], st_var[0:m, 0:1], AF.Sqrt, [stk], [stk], bias=eps_sb[0:m, 0:1])
                recip(st_rstd[0:m, 0:1], st_sd[0:m, 0:1], [stk], [stk])
                stt("dve", vnf[0:m, :], vtm[0:m, :], st_rstd[0:m, 0:1], lng_b[0:m, :], ALU.mult, ALU.mult,
                    [tok("vtm"), stk, tok("lnp")], [tok("sq")])
                tt("dve", vnf[0:m, :], vnf[0:m, :], lnb_b[0:m, :], ALU.add, [tok("sq"), tok("lnp")], [tok("sq")])

            def ln4(tg):
                m = min(128, n - tg * 128)
                cp("act", vn[0:m, tg, :], vnf[0:m, :], [tok("sq")], [tok("vn")])
                if S.sample:
                    dma(o_cvs[o], vnf[0:m, :], [tok("sq")], [], "o_cv")

            chunks = [("u", c) for c in range(8)] + [("g", c) for c in range(8)]
            ci = [0]

            def some_chunks(k):
                for _ in range(k):
                    if ci[0] < len(chunks):
                        kind, c = chunks[ci[0]]
                        ci[0] += 1
                        (u_chunk if kind == "u" else g_chunk)(c)

            for tg in range(ntg):
                if tg > 0:
                    some_chunks(4)
                ln1(tg)
                some_chunks(2)
                ln2(tg)
                some_chunks(2)
                ln3(tg)
                some_chunks(4 if ntg > 1 else 12)
                ln4(tg)
            some_chunks(16)
            run_hook('after_in')
            tt("dve", u_sb[:, :, 0:n], u_sb[:, :, 0:n], sg_sb[:, :, 0:n], ALU.mult, [tok("u"), tok("sg")], [tok("u")])
            for tg in range(ntg):
                m = min(128, n - tg * 128)
                for half in range(2):
                    b = nxt("px", 2, 2)
                    for gi in range(4):
                        gidx = half * 4 + gi
                        mm(banks[b][:, gi * 128:gi * 128 + m], vn[0:m, tg, gidx * 128:(gidx + 1) * 128],
                           wmT[0:m, gidx, 0:m], True, True, [tok("vn"), tok("wmT")], [btok[b]])
                    pv = v3(banks[b][:, 0:512], 4)[:, :, 0:m]
                    tt("dve", v3(s_tmp[:, 0:512], 4)[:, :, 0:m], pv, bs_b[:, half * 4:half * 4 + 4, 0:m], ALU.add,
                       [btok[b], tok("lnp")], [tok("tmpn0"), tok("tmpn1")])
                    tt("dve", gated[:, half * 4:half * 4 + 4, tg * 128:tg * 128 + m], v3(s_tmp[:, 0:512], 4)[:, :, 0:m],
                       u_sb[:, half * 4:half * 4 + 4, tg * 128:tg * 128 + m], ALU.mult,
                       [tok("tmpn0"), tok("tmpn1"), tok("u")], [tok("gated")])
            rhs = [(gated[:, kc, 0:n], [tok("gated")]) for kc in range(8)]
            if next_norm is not None:
                norm_mod(*next_norm)
            out_proj_residual(S, t, l, tok("w_out_c0"), rhs)
            run_hook("after_out")

        fin = [0]

        def final_tile(S, t):
            n = S.TT
            xt = S.x(t)
            fi = fin[0]
            fin[0] = 1 - fi
            rs_, rtok = rms_stats(S, t, alt=(fi == 1))
            yst = yst2[fi]
            for kc in range(8):
                stt("dve", yst[:, kc, 0:n], xt[:, kc, :], fg[:, kc:kc + 1], rs_[:, 0:n], ALU.mult, ALU.mult,
                    [S.xtok[t], rtok], [tok("yst%d" % fi)])
            if S.sample:
                dma(o_ys.rearrange("(kc p) t -> p kc t", p=128), yst[:, :, 0:n], [tok("yst%d" % fi)], [], "o_y%d" % fi)
            else:
                dma(o_yp[S.pi, :, t * TT:(t + 1) * TT].rearrange("(kc p) t -> p kc t", p=128), yst[:, :, 0:n],
                    [tok("yst%d" % fi)], [], "o_y%d" % fi)

        LAYERS = [(pi, l) for pi in range(2) for l in range(4)]

        def issue_in(idx):
            if idx >= len(LAYERS):
                return
            pi, l = LAYERS[idx]
            if l % 2 == 0:
                issue_w(d_wine[l // 2], w_in, EVEN_IN, "w_in")
            else:
                issue_w(d_wino[l // 2], w_in, 3 * D, "w_in")

        def issue_out(idx):
            if idx >= len(LAYERS):
                return
            pi, l = LAYERS[idx]
            issue_w((d_woute if l % 2 == 0 else d_wouto)[l // 2], w_out, D, "w_out")

        def pool_dma(out, in_, writes, key):
            P.add("pool", (lambda o, i_: (lambda e: e.dma_start(out=o, in_=i_)))(out, in_), [], writes, dma_key=key)

        def main_schedule():
            issue_in(0)
            issue_out(0)
            for pi in range(2):
                SP_ = mkseq(pi)
                seqs = [SP_] + ([SS] if pi == 1 else [])
                barrier()
                stage_mark("xload%d" % pi)
                for t in range(NT):
                    dma(x_sb[:, :, t * TT:(t + 1) * TT],
                        d_xp[pi, :, t * TT:(t + 1) * TT].rearrange("(kc p) t -> p kc t", p=128),
                        [], [SP_.xtok[t]], "xl%d" % t)
                if pi == 1:
                    dma(xs_sb[:, :, :], d_xs.rearrange("(kc p) t -> p kc t", p=128), [], [SS.xtok[0]], "xs")
                for l in range(4):
                    idx = pi * 4 + l
                    barrier()
                    stage_mark("layer p%d l%d" % (pi, l))
                    last_S = seqs[-1]
                    if l % 2 == 0:
                        e = l // 2
                        for r_ in range(4):
                            ts("dve", I4B[:, r_, :], ident_f[:, :], 30000.0, None, ALU.mult, None, [], [tok("I4B")])
                        memset("dve", qTz.rearrange("p g r t -> p (g r t)"), 0.0, [], [tok("qT")])
                        memset("dve", qiTz.rearrange("p h t -> p (h t)"), 0.0, [], [tok("qiT")])
                        for S in seqs:
                            if S.sample:
                                pool_dma(kT[:, 0:PAST], d_ckT[e], [tok("kT%d" % i) for i in range(4)], "c_k")
                                pool_dma(kiT[0:64, 0:PAST], d_ckiT[e], [tok("kiT%d" % i) for i in range(4)], "c_ki")
                                pool_dma(kiT[64:128, 0:PAST], d_ckiT[e], [tok("kiT%d" % i) for i in range(4)], "c_ki2")
                                pool_dma(v_sb[:, 0:8, :], d_cv[e].rearrange("(kb p) c -> p kb c", p=128),
                                         [tok("v%d" % i) for i in range(4)], "c_v")
                                dma(zp_s[:, :, 0:2], d_cconv[e], [], SS.zptok, "cconv")
                            else:
                                for c in range(4):
                                    memset("pool", zp_p[:, c, 0:2], 0.0, [], [S.zptok[c]])
                            for t in range(S.nt):
                                stage_mark("even tile p%d l%d t%d" % (pi, l, t))
                                if S is last_S and t == S.nt - 1:
                                    HOOK["after_in"] = (lambda i_=idx: issue_in(i_ + 1))
                                    HOOK["after_out"] = (lambda i_=idx: issue_out(i_ + 1))
                                even_tile(S, t, l, e, skip_norm=(t > 0),
                                          next_norm=((S, t + 1, l) if t + 1 < S.nt else None))
                    else:
                        o = l // 2
                        dma(v3(wst_f, 8), d_wsT[o], [], [tok("wst")], "wst")
                        tt("dve", wmT[:, :, :], v3(wst_f, 8), tril_f[:, :].unsqueeze(1).to_broadcast([128, 8, 128]),
                           ALU.mult, [tok("wst")], [tok("wmT")])
                        dma(lng_b, d_lng[o:o + 1, :].partition_broadcast(128), [], [tok("lnp")], "lnp")
                        dma(lnb_b, d_lnb[o:o + 1, :].partition_broadcast(128), [], [tok("lnp2")], "lnp2")
                        dma(bs_b.rearrange("p a b -> p (a b)"), d_bs[o:o + 1, :].partition_broadcast(128), [],
                            [tok("lnp3")], "lnp3")
                        barrier()
                        for S in seqs:
                            for t in range(S.nt):
                                stage_mark("odd tile p%d l%d t%d" % (pi, l, t))
                                if S is last_S and t == S.nt - 1:
                                    HOOK["after_in"] = (lambda i_=idx: issue_in(i_ + 1))
                                    HOOK["after_out"] = (lambda i_=idx: issue_out(i_ + 1))
                                odd_tile(S, t, l, o, skip_norm=(t > 0),
                                         next_norm=((S, t + 1, l) if t + 1 < S.nt else None))
                barrier()
                for S in seqs:
                    for t in range(S.nt):
                        stage_mark("final p%d t%d" % (pi, t))
                        final_tile(S, t)

        try:
            main_schedule()
        except _Stop:
            pass
        barrier()
        P.emit()
    return nc


_STAGE_LIMIT = None
_SUB_MARKS = False
_CAST_ENG = "act"

_NC_CACHE = {}


def kernel(x_prompt, x_sample, cache_a_k, cache_a_v, cache_a_kidx, state_b_conv, c_prompt, c_sample,
           ada_w, ada_b, norm_g, ev_w_in, ev_conv_w, ev_w_out, od_w_in, od_ws, od_bs, od_ln_g, od_ln_b,
           od_w_out, final_g):
    f = lambda a: np.ascontiguousarray(np.asarray(a, dtype=np.float32))
    x_prompt = np.asarray(x_prompt, np.float32); x_sample = np.asarray(x_sample, np.float32)
    perm = np.arange(512).reshape(2, 4, 64).transpose(1, 0, 2).reshape(-1)
    cols = np.arange(EVEN_IN)
    cols[0:512] = perm
    cols[768:1280] = 768 + perm
    w_in_e = f(np.asarray(ev_w_in)[:, :, cols])
    rows = np.arange(D)
    rows[0:512] = perm
    w_out_e = f(np.asarray(ev_w_out)[:, rows, :])
    shared = {
        "ada_w": f(ada_w),
        "ada_b": f(np.asarray(ada_b).reshape(4, 24, 128).transpose(2, 0, 1)),
        "norm_g": f(np.asarray(norm_g).reshape(4, 8, 128).transpose(2, 0, 1)),
        "final_g": f(np.asarray(final_g).reshape(8, 128).T),
        "w_in_e": w_in_e,
        "conv_w": f(np.asarray(ev_conv_w).reshape(2, 3, 4, 128).transpose(3, 0, 2, 1)),
        "w_out_e": w_out_e,
        "w_in_o": f(od_w_in),
        "wsT": f(np.asarray(od_ws).transpose(0, 3, 1, 2)),
        "bs": f(np.asarray(od_bs).reshape(2, D)),
        "ln_g": f(od_ln_g), "ln_b": f(od_ln_b),
        "w_out_o": f(od_w_out),
        "ident": np.eye(128, dtype=np.float32),
        "trilT": np.triu(np.ones((128, 128), np.float32)),
        "pow2": np.tile(np.array([2.0 ** -(k + 1) for k in range(NIT)] + [2.0 ** -NIT], np.float32)[None, :], (128, 1)),
    }
    ck = np.asarray(cache_a_k, np.float32); cv = np.asarray(cache_a_v, np.float32)
    cki = np.asarray(cache_a_kidx, np.float32); cst = np.asarray(state_b_conv, np.float32)
    in_maps = []
    for i in range(8):
        m = dict(shared)
        m["xp"] = f(x_prompt[2 * i:2 * i + 2].transpose(0, 2, 1))
        m["xs"] = f(x_sample[i].T)
        m["cT"] = f(np.stack([np.asarray(c_prompt)[2 * i], np.asarray(c_prompt)[2 * i + 1], np.asarray(c_sample)[i]], 1))
        m["ckT"] = f(ck[:, i].reshape(2, PAST, 128).transpose(0, 2, 1))
        m["cv"] = f(cv[:, i].reshape(2, PAST, 128))
        m["ckiT"] = f(cki[:, i].transpose(0, 2, 1))
        m["cconv"] = f(cst[:, i].reshape(2, 2, 4, 128).transpose(0, 3, 2, 1))
        in_maps.append(m)
    if "nc" not in _NC_CACHE:
        _NC_CACHE["nc"] = build_nc()
    res = run_bass_kernel_spmd(_NC_CACHE["nc"], in_maps, core_ids=list(range(8)))
    R = res.results
    y_p = np.empty((16, SEQ, D), np.float32); y_s = np.empty((8, DEC, D), np.float32)
    k_p = np.empty((2, 16, SEQ, 2, 64), np.float32); v_p = np.empty((2, 16, SEQ, 2, 64), np.float32)
    ki_p = np.empty((2, 16, SEQ, 64), np.float32); conv_p = np.empty((2, 16, 2, 512), np.float32)
    k_s = np.empty((2, 8, DEC, 2, 64), np.float32); v_s = np.empty((2, 8, DEC, 2, 64), np.float32)
    ki_s = np.empty((2, 8, DEC, 64), np.float32); conv_s = np.empty((2, 8, 2, 512), np.float32)
    cv_s = np.empty((2, 8, DEC, D), np.float32)
    for i in range(8):
        r = R[i]
        for s in range(2):
            b = 2 * i + s
            y_p[b] = r["yp"][s].T
            k_p[:, b] = r["kp"][:, s].transpose(0, 2, 1).reshape(2, SEQ, 2, 64)
            v_p[:, b] = r["vp"][:, s].reshape(2, SEQ, 2, 64)
            ki_p[:, b] = r["kip"][:, s].transpose(0, 2, 1)
            conv_p[:, b] = r["convp"][:, s].transpose(0, 3, 2, 1).reshape(2, 2, 512)
        y_s[i] = r["ys"].T
        k_s[:, i] = r["ks"].transpose(0, 2, 1).reshape(2, DEC, 2, 64)
        v_s[:, i] = r["vs"].reshape(2, DEC, 2, 64)
        ki_s[:, i] = r["kis"].transpose(0, 2, 1)
        conv_s[:, i] = r["convs"].transpose(0, 3, 2, 1).reshape(2, 2, 512)
        cv_s[:, i] = r["cvs"]
    return (y_p, y_s, k_p, v_p, ki_p, conv_p, k_s, v_s, ki_s, conv_s, cv_s)
```

```python
import contextlib
import numpy as np
import concourse.bass as bass
import concourse.mybir as mybir
from concourse.bass_utils import run_bass_kernel_spmd

F32 = mybir.dt.float32
BF16 = mybir.dt.bfloat16
AF = mybir.ActivationFunctionType
ALU = mybir.AluOpType
AX = mybir.AxisListType

D = 1024
SEQ = 2048
TT = 256
NT = SEQ // TT
DEC = 64
PAST = 1024
EVEN_IN = 3912
NIT = 16
EPS = 1e-6
NEG = -1.0e30
_SYNC_SAME = True
COMPUTE = ("pe", "act", "dve", "pool")


class Tok:
    __slots__ = ("name", "writer", "readers")

    def __init__(self, name):
        self.name = name
        self.writer = None
        self.readers = {}


class Op:
    __slots__ = ("eng", "fn", "deps", "inc", "val", "dma_key", "is_dma", "soft")

    def __init__(self, eng, fn, dma_key=None):
        self.eng = eng
        self.fn = fn
        self.deps = []
        self.inc = False
        self.val = None
        self.dma_key = dma_key
        self.is_dma = dma_key is not None
        self.soft = ()


class Prog:
    def __init__(self, nc):
        self.nc = nc
        self.ops = {e: [] for e in ("pe", "act", "dve", "pool", "sp")}
        self.GB = Tok("GB")
        self.gap_fn = {}

    def add(self, eng, fn, reads=(), writes=(), dma_key=None, barrier=False):
        op = Op(eng, fn, dma_key)
        reads = list(reads)
        writes = list(writes)
        if barrier:
            writes.append(self.GB)
        else:
            reads.append(self.GB)
        deps = []
        soft = []
        gap_ok = (not _SYNC_SAME) and eng in ("act", "dve") and not op.is_dma
        for t in reads:
            w = t.writer
            if w is None:
                continue
            if (not w.is_dma) and w.eng == eng:
                if eng == "pe":
                    continue
                if gap_ok:
                    soft.append(w)
                    continue
            deps.append(w)
        for t in writes:
            w = t.writer
            same_ok = eng == "pe" or gap_ok
            if w is not None and not ((not w.is_dma) and (not op.is_dma) and w.eng == eng and same_ok):
                deps.append(w)
            for r in t.readers.values():
                if not ((not r.is_dma) and (not op.is_dma) and r.eng == eng and same_ok):
                    deps.append(r)
        seen = set()
        for d in deps:
            if id(d) not in seen and d is not op:
                seen.add(id(d))
                op.deps.append(d)
        op.soft = soft
        rkey = ("dma", dma_key) if op.is_dma else eng
        for t in reads:
            t.readers[rkey] = op
        for t in writes:
            t.writer = op
            t.readers = {}
        self.ops[eng].append(op)
        return op

    def emit(self):
        nc = self.nc
        self.n_gap = 0
        for lst in self.ops.values():
            for op in lst:
                for d in op.deps:
                    d.inc = True
        dma_cnt = {}
        sems = {}
        with contextlib.ExitStack() as stack:
            for e, lst in self.ops.items():
                cnt = 0
                for op in lst:
                    if op.is_dma:
                        k = ("dma", op.dma_key)
                        dma_cnt[k] = dma_cnt.get(k, 0) + 16
                        op.val = (k, dma_cnt[k])
                        op.inc = True
                    elif op.inc:
                        cnt += 1
                        op.val = (e, cnt)
            for i, k in enumerate(list(dma_cnt.keys()) + list(COMPUTE)):
                sems[k] = stack.enter_context(nc.semaphore("s%d" % i))
            engmap = {"pe": "tensor", "act": "scalar", "dve": "vector", "pool": "gpsimd", "sp": "sync"}
            with nc.Block() as block:
                def mk(ename):
                    def body(eng):
                        waited = {}
                        prev = None
                        for op in self.ops[ename]:
                            if prev is not None and any(d is prev for d in op.soft):
                                self.gap_fn[ename](eng)
                                self.n_gap += 1
                            prev = op
                            need = {}
                            for d in op.deps:
                                k, v = d.val
                                if v > need.get(k, 0):
                                    need[k] = v
                            for k, v in need.items():
                                if waited.get(k, 0) >= v:
                                    continue
                                waited[k] = v
                                eng.wait_ge(sems[k], v)
                            ins = op.fn(eng)
                            if op.inc:
                                ins.then_inc(sems[op.val[0]], 16 if op.is_dma else 1)
                        if ename == "sp":
                            for k, v in dma_cnt.items():
                                if waited.get(k, 0) < v:
                                    eng.wait_ge(sems[k], v)
                    return body
                for en, attr in engmap.items():
                    getattr(block, attr)(mk(en))


def build_nc():
    nc = bass.Bass("TRN2", target_bir_lowering=False)
    P = Prog(nc)

    def din(name, shape):
        return nc.dram_tensor(name, list(shape), F32, kind="ExternalInput").ap()

    def dout(name, shape):
        return nc.dram_tensor(name, list(shape), F32, kind="ExternalOutput").ap()

    d_xp = din("xp", [2, D, SEQ]); d_xs = din("xs", [D, DEC]); d_cT = din("cT", [D, 3])
    d_adaw = din("ada_w", [4, D, 3 * D]); d_adab = din("ada_b", [128, 4, 24])
    d_ng = din("norm_g", [128, 4, 8]); d_fg = din("final_g", [128, 8])
    d_wine = din("w_in_e", [2, D, EVEN_IN]); d_convw = din("conv_w", [128, 2, 4, 3])
    d_woute = din("w_out_e", [2, D, D]); d_wino = din("w_in_o", [2, D, 3 * D])
    d_wsT = din("wsT", [2, 128, 8, 128]); d_bs = din("bs", [2, D]); d_lng = din("ln_g", [2, D])
    d_lnb = din("ln_b", [2, D]); d_wouto = din("w_out_o", [2, D, D])
    d_ckT = din("ckT", [2, 128, PAST]); d_cv = din("cv", [2, PAST, 128]); d_ckiT = din("ckiT", [2, 64, PAST])
    d_cconv = din("cconv", [2, 128, 4, 2])
    d_ident = din("ident", [128, 128]); d_tril = din("trilT", [128, 128]); d_pow2 = din("pow2", [128, NIT + 1])

    o_yp = dout("yp", [2, D, SEQ]); o_ys = dout("ys", [D, DEC])
    o_kp = dout("kp", [2, 2, 128, SEQ]); o_vp = dout("vp", [2, 2, SEQ, 128]); o_kip = dout("kip", [2, 2, 64, SEQ])
    o_convp = dout("convp", [2, 2, 128, 4, 2])
    o_ks = dout("ks", [2, 128, DEC]); o_vs = dout("vs", [2, DEC, 128]); o_kis = dout("kis", [2, 64, DEC])
    o_convs = dout("convs", [2, 128, 4, 2]); o_cvs = dout("cvs", [2, DEC, D])

    ARENA_BYTES = 49152
    with contextlib.ExitStack() as st:
        def sb(name, shape, dt):
            return st.enter_context(nc.sbuf_tensor("sb_" + name, list(shape), dt))

        x_sb = sb("x_sb", [128, 8, SEQ], F32)
        xs_sb = sb("xs_sb", [128, 8, DEC], F32)
        w_in = sb("w_in", [128, 8, EVEN_IN], BF16)
        w_out = sb("w_out", [128, 8, D], BF16)
        kT = sb("kT", [128, SEQ], BF16)
        kiT = sb("kiT", [128, SEQ], BF16)
        v_sb = sb("v_sb", [128, 16, 128], BF16)
        ones_bf = sb("ones_bf", [128, 128], BF16)
        onesm = sb("onesm", [128, 128], BF16)
        ident_f = sb("ident_f", [128, 128], F32)
        ident_bf = sb("ident_bf", [128, 128], BF16)
        tril_f = sb("tril_f", [128, 128], F32)
        pow2 = sb("pow2", [128, NIT + 1], F32)
        adab = sb("adab", [128, 4, 24], F32)
        normg = sb("normg", [128, 4, 8], F32)
        fg = sb("fg", [128, 8], F32)
        convw = sb("convw", [128, 2, 4, 3], F32)
        cT = sb("cT", [128, 8, 3], F32)
        scT = sb("scT", [128, 8, 3], F32)
        mod = sb("mod", [128, 4, 24, 3], F32)
        Gm = sb("Gm", [128, 4, 3, 8], F32)
        arena = sb("arena", [128, ARENA_BYTES // 2], BF16)
        banks = [st.enter_context(nc.psum_tensor("bank%d" % i, [128, 512], F32)) for i in range(8)]
        btok = [Tok("bank%d" % i) for i in range(8)]

        class Arena:
            def __init__(self):
                self.off = 0

            def alloc(self, nelem, dt):
                nb = nelem * (4 if dt == F32 else 2)
                nb = (nb + 63) // 64 * 64
                a = self.off
                self.off += nb
                assert self.off <= ARENA_BYTES, ("arena overflow", self.off)
                v = arena[:, a // 2:(a + nb) // 2]
                if dt == F32:
                    v = v.bitcast(F32)
                    return v[:, 0:nelem]
                return v[:, 0:nelem]

        def v3(ap, a):
            return ap.rearrange("p (a b) -> p a b", a=a)

        A = Arena()
        stage = [A.alloc(1024, F32) for _ in range(3)]
        stage_tok = [Tok("stage%d" % i) for i in range(3)]
        wst_f = A.alloc(8 * 128, F32)
        A = Arena()
        hT = v3(A.alloc(8 * TT, BF16), 8)
        sq = v3(A.alloc(8 * TT, BF16), 8)
        sq_off = A.off - 8 * TT * 2
        rstd_t = A.alloc(TT, F32)
        rstd = A.alloc(TT, F32)
        tmpn_off = A.off
        tmpn = [A.alloc(TT, F32) for _ in range(2)]
        common_off = A.off
        xin_sb = A.alloc(TT, F32); gb_sb = A.alloc(TT, F32); sgz_sb = A.alloc(TT, F32); y_sb = A.alloc(TT, F32)
        kst = A.alloc(TT, F32); kist = A.alloc(TT, F32); vst = v3(A.alloc(2 * 128, F32), 2)
        phB_end = A.off
        A = Arena()
        score = [A.alloc(SEQ, F32) for _ in range(2)]
        _mk = A.alloc(SEQ, BF16)
        maskm1 = [_mk, _mk]
        junk_off = A.off
        _dg = v3(A.alloc(8 * 128, BF16), 8)
        dg = [_dg, _dg]
        rl = [A.alloc(512, BF16) for _ in range(2)]
        pT = [A.alloc(512, BF16) for _ in range(2)]
        rden = A.alloc(512, F32)
        junk1 = arena[:, junk_off // 2:junk_off // 2 + SEQ]
        lnd = arena[:, junk_off // 2:junk_off // 2 + 1024].bitcast(F32)
        assert A.off - junk_off >= 2 * SEQ
        bis = []
        for _i in range(2):
            bis.append(dict(dh=A.alloc(NIT + 1, F32), mid=A.alloc(NIT + 1, F32), cnt=A.alloc(NIT, F32),
                            tmp=A.alloc(NIT, F32), mx=A.alloc(1, F32), mn=A.alloc(1, F32), d0=A.alloc(1, F32)))
        A.off = max(A.off, phB_end)
        qTz = A.alloc(2 * 4 * TT, BF16).rearrange("p (g r t) -> p g r t", g=2, r=4)
        qiTz = v3(A.alloc(8 * TT, BF16), 8)
        sga = v3(A.alloc(4 * TT, BF16), 4)
        b_out = v3(A.alloc(4 * TT, BF16), 4)
        wi_sb = v3(A.alloc(2 * 8, F32), 2)
        zp_p = v3(A.alloc(4 * (TT + 2), F32), 4)
        zp_s = v3(A.alloc(4 * (DEC + 2), F32), 4)
        I4B = v3(A.alloc(4 * 128, BF16), 4)
        even_end = A.off
        A = Arena(); A.off = common_off
        u_sb = v3(A.alloc(8 * TT, BF16), 8)
        sg_sb = v3(A.alloc(8 * TT, BF16), 8)
        gated = v3(A.alloc(8 * TT, BF16), 8)
        vtm = A.alloc(D, F32)
        vn = v3(A.alloc(2 * D, BF16), 2)
        lng_b = A.alloc(D, F32); lnb_b = A.alloc(D, F32)
        bs_b = v3(A.alloc(D, F32), 8)
        wmT = v3(A.alloc(8 * 128, BF16), 8)
        st_mean = A.alloc(1, F32); st_var = A.alloc(1, F32); st_rstd = A.alloc(1, F32); st_sd = A.alloc(1, F32)
        odd_end = A.off
        vnf = arena[:, sq_off // 2:sq_off // 2 + 2 * D].bitcast(F32)
        s_tmp = arena[:, tmpn_off // 2:tmpn_off // 2 + 1024].bitcast(F32)
        A = Arena(); A.off = common_off
        yst2 = [v3(A.alloc(8 * TT, F32), 8) for _ in range(2)]
        sqF = v3(A.alloc(8 * TT, BF16), 8); rstd_tF = A.alloc(TT, F32); rstdF = A.alloc(TT, F32)
        print("arena: even", even_end, "odd", odd_end, "of", ARENA_BYTES)

        T = {}

        def tok(n):
            if n not in T:
                T[n] = Tok(n)
            return T[n]

        def dma(out, in_, reads, writes, key):
            P.add("sp", lambda e: e.dma_start(out=out, in_=in_), reads, writes, dma_key=key)

        def mm(out, lhsT, rhs, start, stop, reads, writes, tp=None):
            if tp is None:
                P.add("pe", lambda e: e.matmul(out, lhsT=lhsT, rhs=rhs, start=start, stop=stop), reads, writes)
            else:
                P.add("pe", lambda e: e.matmul(out, lhsT=lhsT, rhs=rhs, start=start, stop=stop, tile_position=tp),
                      reads, writes)

        def act(out, in_, func, reads, writes, scale=1.0, bias=None, accum=None):
            kw = {}
            if bias is not None:
                kw["bias"] = bias
            if accum is not None:
                kw["accum_out"] = accum
            P.add("act", lambda e: e.activation(out=out, in_=in_, func=func, scale=scale, **kw), reads, writes)

        def ts(eng, out, in0, s1, s2, op0, op1, reads, writes, accum=None):
            if accum is not None:
                P.add(eng, lambda e: e.tensor_scalar(out=out, in0=in0, scalar1=s1, scalar2=s2, op0=op0, op1=op1,
                                                     accum_out=accum), reads, writes)
            elif op1 is None:
                P.add(eng, lambda e: e.tensor_scalar(out=out, in0=in0, scalar1=s1, scalar2=None, op0=op0), reads, writes)
            else:
                P.add(eng, lambda e: e.tensor_scalar(out=out, in0=in0, scalar1=s1, scalar2=s2, op0=op0, op1=op1),
                      reads, writes)

        def tt(eng, out, in0, in1, op, reads, writes):
            P.add(eng, lambda e: e.tensor_tensor(out=out, in0=in0, in1=in1, op=op), reads, writes)

        def stt(eng, out, in0, scalar, in1, op0, op1, reads, writes):
            P.add(eng, lambda e: e.scalar_tensor_tensor(out=out, in0=in0, scalar=scalar, in1=in1, op0=op0, op1=op1),
                  reads, writes)

        def cp(eng, out, in_, reads, writes):
            if eng == "act":
                P.add("act", lambda e: e.activation(out=out, in_=in_, func=AF.Copy), reads, writes)
            else:
                P.add(eng, lambda e: e.tensor_copy(out=out, in_=in_), reads, writes)

        def memset(eng, ap, val, reads, writes):
            P.add(eng, lambda e: e.memset(ap, val), reads, writes)

        def recip(out, in_, reads, writes):
            P.add("dve", lambda e: e.reciprocal(out=out, in_=in_), reads, writes)

        def barrier():
            P.add("pool", lambda e: e.memset(bar_sb[:, 0:1], 0.0), [], [], barrier=True)

        bar_sb = sb("bar_sb", [128, 2], F32)
        rr = {"pp": 0, "px": 0, "pl": 0, "st": 0, "rl": 0, "pT": 0, "tmpn": 0, "pa": 0}

        PPB = [0, 1, 6, 7]

        def nxt(kind, n, base):
            i = rr[kind]
            rr[kind] = (i + 1) % n
            return base + i

        class _Stop(Exception):
            pass

        stage_ctr = [0]

        def stage_mark(name):
            stage_ctr[0] += 1
            if _STAGE_LIMIT is not None and stage_ctr[0] > _STAGE_LIMIT:
                print("build stopped before stage", stage_ctr[0], name)
                raise _Stop()

        def sub_mark(name):
            if _SUB_MARKS:
                stage_mark(name)

        cst = 0

        def cload(dst, src):
            nonlocal cst
            cst += 1
            dma(dst, src, [], [tok("c%d" % cst)], "const")

        cload(ident_f[:], d_ident); cload(tril_f[:], d_tril); cload(pow2[:], d_pow2)
        cload(adab[:], d_adab); cload(normg[:], d_ng); cload(fg[:], d_fg); cload(convw[:], d_convw)
        cload(cT[:], d_cT.rearrange("(kc p) j -> p kc j", p=128))
        memset("pool", ones_bf[:], 1.0, [], [tok("ones")])
        memset("pool", onesm[:], 1.0 / D, [], [tok("onesm")])
        barrier()
        cp("pool", ident_bf[:], ident_f[:], [], [tok("identbf")])
        act(scT[:], cT[:], AF.Silu, [], [tok("scT")])
        barrier()

        ada_st = [v3(arena[:, i * 8192:(i + 1) * 8192].bitcast(F32), 8) for i in range(2)]
        ada_tok = [Tok("adast0"), Tok("adast1")]
        ada_rr = 0
        modtm = arena[:, 16384:16384 + 6144].bitcast(F32)
        for l in range(4):
            for cc in range(6):
                s = ada_rr
                ada_rr = 1 - ada_rr
                bk = cc % 2
                dma(ada_st[s], d_adaw[l, :, cc * 512:(cc + 1) * 512].rearrange("(kc p) c -> p kc c", p=128), [],
                    [ada_tok[s]], "adast%d" % s)
                for kc in range(8):
                    mm(banks[bk][0:3, 0:512], scT[:, kc, :], ada_st[s][:, kc, :], kc == 0, kc == 7,
                       [ada_tok[s], tok("scT")], [btok[bk]])
                cp("dve", modtm[0:3, cc * 512:(cc + 1) * 512], banks[bk][0:3, 0:512], [btok[bk]], [tok("modtm")])
            for oc in range(24):
                mm(banks[2][:, oc * 3:(oc + 1) * 3], modtm[0:3, oc * 128:(oc + 1) * 128], ident_f[0:3, 0:3], True, True,
                   [tok("modtm")], [btok[2]])
            for j in range(3):
                tt("dve", mod[:, l, :, j], v3(banks[2][:, 0:72], 24)[:, :, j], adab[:, l, :], ALU.add,
                   [btok[2]], [tok("mod")])
            for j in range(3):
                stt("dve", Gm[:, l, j, :], mod[:, l, 8:16, j], 1.0, normg[:, l, :], ALU.add, ALU.mult,
                    [tok("mod")], [tok("Gm")])
        barrier()

        def issue_w(dram_w, dst, C, name):
            for ci in range((C + 1023) // 1024):
                c0 = ci * 1024
                n = min(1024, C - c0)
                P.add("pool", (lambda o, i_: (lambda e: e.dma_start(out=o, in_=i_)))(
                    dst[:, :, c0:c0 + n], dram_w[:, c0:c0 + n].rearrange("(kc p) c -> p kc c", p=128)),
                    [], [tok("%s_c%d" % (name, ci))], dma_key="%s_c%d" % (name, ci))

        def wtk(c0):
            return tok("w_in_c%d" % (c0 // 1024))

        HOOK = {"after_in": None, "after_out": None}

        def run_hook(k):
            f = HOOK[k]
            HOOK[k] = None
            if f is not None:
                f()

        class Seq:
            pass

        def mkseq(pi):
            S = Seq()
            S.sample = False; S.j = pi; S.TT = TT; S.nt = NT; S.pi = pi
            S.x = lambda t: x_sb[:, :, t * TT:(t + 1) * TT]
            S.xtok = [tok("x%d" % t) for t in range(NT)]
            S.zp = zp_p; S.zptok = [tok("zp%d" % c) for c in range(4)]
            return S

        SS = Seq()
        SS.sample = True; SS.j = 2; SS.TT = DEC; SS.nt = 1; SS.pi = 0
        SS.x = lambda t: xs_sb[:, :, :]
        SS.xtok = [tok("xs")]
        SS.zp = zp_s; SS.zptok = [tok("zps%d" % c) for c in range(4)]

        def rms_stats(S, t, alt=False):
            n = S.TT
            xt = S.x(t)
            sq_, rt_, rs_, sfx = (sqF, rstd_tF, rstdF, "F") if alt else (sq, rstd_t, rstd, "")
            act(sq_[:, :, 0:n], xt, AF.Square, [S.xtok[t]], [tok("sq" + sfx)])
            b = nxt("px", 2, 2)
            for kc in range(8):
                mm(banks[b][:, 0:n], onesm[:, :], sq_[:, kc, 0:n], kc == 0, kc == 7, [tok("sq" + sfx), tok("onesm")],
                   [btok[b]])
            act(rt_[:, 0:n], banks[b][:, 0:n], AF.Sqrt, [btok[b]], [tok("rstd_t" + sfx)], bias=eps_sb[:, 0:1])
            recip(rs_[:, 0:n], rt_[:, 0:n], [tok("rstd_t" + sfx)], [tok("rstd" + sfx)])
            return rs_, tok("rstd" + sfx)

        eps_sb = sb("eps_sb", [128, 1], F32)
        memset("pool", eps_sb[:], EPS, [], [tok("eps")])

        def norm_mod(S, t, l):
            n = S.TT
            xt = S.x(t)
            rms_stats(S, t)
            for kc in range(8):
                s = nxt("tmpn", 2, 0)
                stt("dve", tmpn[s][:, 0:n], xt[:, kc, :], Gm[:, l, S.j, kc:kc + 1], rstd[:, 0:n], ALU.mult, ALU.mult,
                    [S.xtok[t], tok("rstd"), tok("Gm")], [tok("tmpn%d" % s)])
                act(hT[:, kc, 0:n], tmpn[s][:, 0:n], AF.Identity, [tok("tmpn%d" % s), tok("mod")], [tok("hT")],
                    bias=mod[:, l, kc, S.j:S.j + 1])

        def proj_fm(W, wtok, c0, M, n, p0=0):
            b = PPB[nxt("pp", 4, 0)]
            for kc in range(8):
                mm(banks[b][p0:p0 + M, 0:n], W[:, kc, c0:c0 + M], hT[:, kc, 0:n], kc == 0, kc == 7,
                   [wtk(c0), tok("hT")], [btok[b]], tp=(0, p0) if p0 else None)
            return b

        def out_proj_residual(S, t, l, wtok, rhs_list):
            n = S.TT
            xt = S.x(t)
            for oc in range(8):
                b = PPB[nxt("pp", 4, 0)]
                for i, (rap, rtoks) in enumerate(rhs_list):
                    mm(banks[b][:, 0:n], w_out[:, i, oc * 128:(oc + 1) * 128], rap, i == 0, i == len(rhs_list) - 1,
                       [wtok] + rtoks, [btok[b]])
                stt("dve", xt[:, oc, :], banks[b][:, 0:n], mod[:, l, 16 + oc, S.j:S.j + 1], xt[:, oc, :],
                    ALU.mult, ALU.add, [btok[b], tok("mod"), S.xtok[t]], [S.xtok[t]])

        def even_tile(S, t, l, e, skip_norm=False, next_norm=None):
            n = S.TT
            wt = tok("w_in")
            key0 = PAST if S.sample else t * TT
            if not skip_norm:
                norm_mod(S, t, l)
            sub_mark('after norm')
            for r in range(4):
                b = proj_fm(w_in, wt, r * 128, 128, n)
                cp("act", qTz[0:64, 0, r, 0:n], banks[b][0:64, 0:n], [btok[b]], [tok("qT")])
                cp("act", qTz[64:128, 1, r, 0:n], banks[b][64:128, 0:n], [btok[b]], [tok("qT")])
            b = proj_fm(w_in, wt, 512, 128, n)
            cp("dve", kst[:, 0:n], banks[b][:, 0:n], [btok[b]], [tok("kst")])
            cp("act", kT[:, key0:key0 + n], kst[:, 0:n], [tok("kst")], [tok("kT%d" % (key0 // TT))])
            if S.sample:
                dma(o_ks[e], kst[:, 0:n], [tok("kst")], [], "o_k")
            else:
                dma(o_kp[e, S.pi, :, key0:key0 + n], kst[:, 0:n], [tok("kst")], [], "o_k")
            for r in range(4):
                b = proj_fm(w_in, wt, 768 + r * 128, 128, n)
                act(sga[:, r, 0:n], banks[b][:, 0:n], AF.Silu, [btok[b]], [tok("sga")])
            for jj in range(4):
                b = proj_fm(w_in, wt, 1280 + jj * 128, 128, n)
                cp("dve", qiTz[0:64, 2 * jj, 0:n], banks[b][0:64, 0:n], [btok[b]], [tok("qiT")])
                cp("dve", qiTz[64:128, 2 * jj + 1, 0:n], banks[b][64:128, 0:n], [btok[b]], [tok("qiT")])
            b = PPB[nxt("pp", 4, 0)]
            for half in range(2):
                for kc in range(8):
                    mm(banks[b][half * 64:(half + 1) * 64, 0:n], w_in[:, kc, 1792:1856], hT[:, kc, 0:n], kc == 0, kc == 7,
                       [wtk(1792), tok("hT")], [btok[b]], tp=(0, 64) if half else None)
            cp("dve", kist[:, 0:n], banks[b][:, 0:n], [btok[b]], [tok("kist")])
            cp("act", kiT[:, key0:key0 + n], kist[:, 0:n], [tok("kist")], [tok("kiT%d" % (key0 // TT))])
            if S.sample:
                dma(o_kis[e], kist[0:64, 0:n], [tok("kist")], [], "o_ki")
            else:
                dma(o_kip[e, S.pi, :, key0:key0 + n], kist[0:64, 0:n], [tok("kist")], [], "o_ki")
            sub_mark('after qkgaqiki')
            ntg = (n + 127) // 128
            for tg in range(ntg):
                m = min(128, n - tg * 128)
                kb = (key0 + tg * 128) // 128
                b = nxt("px", 2, 2)
                for kc in range(8):
                    mm(banks[b][0:m, 0:128], hT[:, kc, tg * 128:tg * 128 + m], w_in[:, kc, 640:768], kc == 0, kc == 7,
                       [wtk(640), tok("hT")], [btok[b]])
                for kc in range(8):
                    mm(banks[b][0:m, 128:136], hT[:, kc, tg * 128:tg * 128 + m], w_in[:, kc, 1856:1864], kc == 0, kc == 7,
                       [wtk(1856), tok("hT")], [btok[b]])
                cp("dve", vst[0:m, tg, :], banks[b][0:m, 0:128], [btok[b]], [tok("vst")])
                cp("act", v_sb[0:m, kb, :], vst[0:m, tg, :], [tok("vst")], [tok("v%d" % (key0 // TT))])
                cp("dve", wi_sb[0:m, tg, :], banks[b][0:m, 128:136], [btok[b]], [tok("wi")])
            if S.sample:
                dma(o_vs[e], vst[0:DEC, 0, :], [tok("vst")], [], "o_v")
            else:
                dma(o_vp[e, S.pi, key0:key0 + n, :].rearrange("(tg p) c -> p tg c", p=128), vst[:, :, :],
                    [tok("vst")], [], "o_v")
            sub_mark('after v/wi')
            zp = S.zp
            for c in range(4):
                zt = S.zptok[c]
                b = proj_fm(w_in, wt, 2888 + c * 128, 128, n)
                cp("act", xin_sb[:, 0:n], banks[b][:, 0:n], [btok[b]], [tok("xin")])
                b = proj_fm(w_in, wt, 2376 + c * 128, 128, n)
                tt("dve", zp[:, c, 2:2 + n], banks[b][:, 0:n], xin_sb[:, 0:n], ALU.mult, [btok[b], tok("xin")], [zt])
                b = proj_fm(w_in, wt, 1864 + c * 128, 128, n)
                cp("act", gb_sb[:, 0:n], banks[b][:, 0:n], [btok[b]], [tok("gb")])
                b = proj_fm(w_in, wt, 3400 + c * 128, 128, n)
                act(sgz_sb[:, 0:n], banks[b][:, 0:n], AF.Silu, [btok[b]], [tok("sgz")])
                ts("dve", y_sb[:, 0:n], zp[:, c, 2:2 + n], convw[:, e, c, 2:3], None, ALU.mult, None, [zt], [tok("y")])
                stt("dve", y_sb[:, 0:n], zp[:, c, 1:1 + n], convw[:, e, c, 1:2], y_sb[:, 0:n], ALU.mult, ALU.add,
                    [zt, tok("y")], [tok("y")])
                stt("dve", y_sb[:, 0:n], zp[:, c, 0:n], convw[:, e, c, 0:1], y_sb[:, 0:n], ALU.mult, ALU.add,
                    [zt, tok("y")], [tok("y")])
                tt("dve", y_sb[:, 0:n], y_sb[:, 0:n], gb_sb[:, 0:n], ALU.mult, [tok("y"), tok("gb")], [tok("y")])
                tt("dve", b_out[:, c, 0:n], y_sb[:, 0:n], sgz_sb[:, 0:n], ALU.mult, [tok("y"), tok("sgz")], [tok("b_out")])
            sub_mark('after conv')
            run_hook('after_in')
            last = (t == S.nt - 1)
            if last:
                if S.sample:
                    dma(o_convs[e], zp[:, :, n:n + 2], [S.zptok[c] for c in range(4)], [], "o_convs")
                else:
                    dma(o_convp[e, S.pi], zp[:, :, n:n + 2], [S.zptok[c] for c in range(4)], [], "o_convp")
            else:
                for c in range(4):
                    cp("dve", zp[:, c, 0:2], zp[:, c, n:n + 2], [S.zptok[c]], [S.zptok[c]])
            nsub = (n + 127) // 128
            barrier()
            attention_tile(S, t, l)
            barrier()
            sub_mark('after attention')
            if next_norm is not None:
                norm_mod(*next_norm)
            rhs = [(sga[:, r, 0:n], [tok("sga")]) for r in range(4)] + [(b_out[:, c, 0:n], [tok("b_out")]) for c in range(4)]
            out_proj_residual(S, t, l, tok("w_out_c0"), rhs)
            run_hook("after_out")

        def attention_tile(S, t, l):
            if S.sample:
                subs = [dict(i=0, nq=DEC, q0=0, blocks=[(kb * 128, 128) for kb in range(8)] + [(PAST, DEC)],
                             corner=False, topk=True)]
            else:
                subs = []
                for sub in range(TT // 128):
                    qt = t * (TT // 128) + sub
                    subs.append(dict(i=sub, nq=128, q0=sub * 128, blocks=[(kb * 128, 128) for kb in range(qt + 1)],
                                     corner=True, topk=qt >= 2))
            for d in subs:
                d["SK"] = d["blocks"][-1][0] + d["blocks"][-1][1]
                nt_ = (d["SK"] + TT - 1) // TT
                d["ktoks"] = [tok("kT%d" % i) for i in range(nt_)]
                d["kitoks"] = [tok("kiT%d" % i) for i in range(nt_)]
                d["vtoks"] = [tok("v%d" % i) for i in range(nt_)]
            steps1 = []
            for d in subs:
                for c0 in range(0, d["SK"], 512):
                    for h in range(8):
                        steps1.append((d, c0, min(512, d["SK"] - c0), h))
            st1 = {}

            def p1_front(k):
                d, c0, m, h = steps1[k]
                i, nq, q0 = d["i"], d["nq"], d["q0"]
                if c0 == 0 and h == 0:
                    tt("dve", dg[i][0:nq, :, 0:nq], ident_bf[0:nq, 0:nq].unsqueeze(1).to_broadcast([nq, 8, nq]),
                       wi_sb[0:nq, i, :].unsqueeze(2).to_broadcast([nq, 8, nq]), ALU.mult,
                       [tok("identbf"), tok("wi")], [tok("dg")])
                b = nxt("px", 2, 2)
                mm(banks[b][0:nq, 0:m], qiTz[0:128, h, q0:q0 + nq], kiT[0:128, c0:c0 + m],
                   True, True, [tok("qiT")] + d["kitoks"], [btok[b]])
                s = nxt("rl", 2, 0)
                act(rl[s][0:nq, 0:m], banks[b][0:nq, 0:m], AF.Relu, [btok[b]], [tok("rl%d" % s)])
                st1[k] = s

            def p1_back(k):
                d, c0, m, h = steps1[k]
                i, nq = d["i"], d["nq"]
                s = st1[k]
                if h == 0:
                    st1["ab"] = nxt("pa", 2, 4)
                ab = st1["ab"]
                mm(banks[ab][0:nq, 0:m], dg[i][0:nq, h, 0:nq], rl[s][0:nq, 0:m], h == 0, h == 7,
                   [tok("dg"), tok("rl%d" % s)], [btok[ab]])
                if h == 7:
                    cp("dve", score[i][0:nq, c0:c0 + m], banks[ab][0:nq, 0:m], [btok[ab]], [tok("score%d" % i)])

            for k in range(len(steps1) + 1):
                flushed = False
                if 1 <= k < len(steps1) and steps1[k][1] == 0 and steps1[k][3] == 0:
                    p1_back(k - 1)
                    flushed = True
                if k < len(steps1):
                    p1_front(k)
                if k >= 1 and not flushed:
                    p1_back(k - 1)
            sub_mark('att p1')
            for d in subs:
                i, nq, SK = d["i"], d["nq"], d["SK"]
                B = bis[i]
                sct, bt = tok("score%d" % i), tok("bis%d" % i)
                if d["topk"]:
                    P.add("dve", (lambda o, a: (lambda e: e.tensor_reduce(out=o, in_=a, axis=AX.X, op=ALU.max)))(
                        B["mx"][0:nq, 0:1], score[i][0:nq, 0:SK]), [sct], [bt])
                    P.add("dve", (lambda o, a: (lambda e: e.tensor_reduce(out=o, in_=a, axis=AX.X, op=ALU.min)))(
                        B["mn"][0:nq, 0:1], score[i][0:nq, 0:SK]), [sct, bt], [bt])
                if d["corner"]:
                    memset("pool", score[i][0:64, SK - 64:SK], NEG, [sct, bt], [sct])
                if d["topk"]:
                    tt("dve", B["d0"][0:nq, :], B["mx"][0:nq, :], B["mn"][0:nq, :], ALU.subtract, [bt], [bt])
                    ts("dve", B["dh"][0:nq, :], pow2[0:nq, :], B["d0"][0:nq, 0:1], None, ALU.mult, None, [bt], [bt])
                    tt("dve", B["mid"][0:nq, 0:1], B["mn"][0:nq, :], B["dh"][0:nq, 0:1], ALU.add, [bt], [bt])
            tk = [d for d in subs if d["topk"]]
            on_act = tk[1]["i"] if len(tk) == 2 else None

            def cnt_op(d, k):
                i, nq, SK = d["i"], d["nq"], d["SK"]
                B = bis[i]
                jbuf = maskm1[0] if i == 0 else junk1
                jtok = [tok("mask")] if i == 0 else [tok("dg"), tok("rl0"), tok("rl1"), tok("pT0"), tok("pT1"),
                                                     tok("rden")]
                if i == on_act:
                    act(jbuf[0:nq, 0:SK], score[i][0:nq, 0:SK], AF.Sign, [tok("score%d" % i), tok("bis%d" % i)],
                        jtok + [tok("bisc%d" % i)], scale=-1.0, bias=B["mid"][0:nq, k:k + 1],
                        accum=B["cnt"][0:nq, k:k + 1])
                else:
                    ts("dve", jbuf[0:nq, 0:SK], score[i][0:nq, 0:SK], B["mid"][0:nq, k:k + 1], None, ALU.is_ge,
                       ALU.add, [tok("score%d" % i), tok("bis%d" % i)], jtok + [tok("bisc%d" % i)],
                       accum=B["cnt"][0:nq, k:k + 1])

            def upd_op(d, k):
                i, nq, SK = d["i"], d["nq"], d["SK"]
                B = bis[i]
                if i == on_act:
                    stt("dve", B["tmp"][0:nq, k:k + 1], B["cnt"][0:nq, k:k + 1], float(SK) - 511.5,
                        B["dh"][0:nq, k:k + 1], ALU.is_le, ALU.mult, [tok("bisc%d" % i), tok("bis%d" % i)],
                        [tok("bist%d" % i)])
                else:
                    stt("dve", B["tmp"][0:nq, k:k + 1], B["cnt"][0:nq, k:k + 1], 255.5, B["dh"][0:nq, k:k + 1],
                        ALU.is_ge, ALU.mult, [tok("bisc%d" % i), tok("bis%d" % i)], [tok("bist%d" % i)])
                stt("dve", B["mid"][0:nq, k + 1:k + 2], B["tmp"][0:nq, k:k + 1], B["mid"][0:nq, k:k + 1],
                    B["dh"][0:nq, k + 1:k + 2], ALU.add, ALU.subtract, [tok("bist%d" % i), tok("bis%d" % i)],
                    [tok("bis%d" % i)])

            for k in range(NIT):
                if on_act is not None:
                    cnt_op(tk[1], k)
                    cnt_op(tk[0], k)
                    upd_op(tk[1], k)
                    upd_op(tk[0], k)
                else:
                    for d in tk:
                        cnt_op(d, k)
                    for d in tk:
                        upd_op(d, k)
            sub_mark('att p2')
            for d in subs:
                i, nq, q0, SK = d["i"], d["nq"], d["q0"], d["SK"]
                thr = bis[i]["mid"][0:nq, NIT:NIT + 1] if d["topk"] else -1.0e29
                ts("dve", maskm1[i][0:nq, 0:SK], score[i][0:nq, 0:SK], thr, -1.0, ALU.is_ge, ALU.add,
                   [tok("score%d" % i), tok("bis%d" % i)], [tok("mask")])
                if nq < 128:
                    memset("dve", maskm1[i][nq:128, 0:SK], 0.0, [], [tok("mask")])
                ob = 6 if i == 0 else 0
                steps3 = [(g, kb, b0, bn) for g in range(2) for kb, (b0, bn) in enumerate(d["blocks"])]
                nb = len(d["blocks"])
                st3 = {}

                def p3_front(k):
                    g, kb, b0, bn = steps3[k]
                    b = nxt("px", 2, 2)
                    mm(v3(banks[b][0:bn, 0:4 * nq], 4), kT[0:128, b0:b0 + bn], qTz[0:128, g, 0:4, q0:q0 + nq],
                       True, False, [tok("qT")] + d["ktoks"], [btok[b]])
                    mm(v3(banks[b][0:bn, 0:4 * nq], 4), maskm1[i][0:128, b0:b0 + bn], I4B[0:128, 0:4, 0:nq],
                       False, True, [tok("mask"), tok("I4B")], [btok[b]])
                    s = nxt("pT", 2, 0)
                    act(pT[s][0:bn, 0:4 * nq], banks[b][0:bn, 0:4 * nq], AF.Exp, [btok[b]], [tok("pT%d" % s)],
                        scale=0.125)
                    st3[k] = s

                def p3_back(k):
                    g, kb, b0, bn = steps3[k]
                    s = st3[k]
                    mm(banks[ob + g][0:128, 0:4 * nq], v_sb[0:bn, b0 // 128, 0:128], pT[s][0:bn, 0:4 * nq],
                       kb == 0, kb == nb - 1, [tok("pT%d" % s)] + d["vtoks"], [btok[ob + g]])
                    mm(banks[4 + g][0:128, 0:4 * nq], ones_bf[0:bn, 0:128], pT[s][0:bn, 0:4 * nq],
                       kb == 0, kb == nb - 1, [tok("pT%d" % s), tok("ones")], [btok[4 + g]])

                def normalise(g):
                    rs = slice(g * 64, (g + 1) * 64)
                    act(lnd[rs, 0:4 * nq], banks[4 + g][rs, 0:4 * nq], AF.Ln, [btok[4 + g]], [tok("dg")])
                    act(rden[rs, 0:4 * nq], lnd[rs, 0:4 * nq], AF.Exp, [tok("dg")], [tok("rden")], scale=-1.0)
                    tt("dve", rden[rs, 0:4 * nq], banks[ob + g][rs, 0:4 * nq], rden[rs, 0:4 * nq], ALU.mult,
                       [btok[ob + g], tok("rden")], [tok("rden")])

                for k in range(len(steps3) + 1):
                    if k < len(steps3):
                        p3_front(k)
                    if k >= 1:
                        p3_back(k - 1)
                        if steps3[k - 1][1] == nb - 1:
                            normalise(steps3[k - 1][0])
                tt("dve", sga[:, 0:4, q0:q0 + nq], v3(rden[:, 0:4 * nq], 4), sga[:, 0:4, q0:q0 + nq], ALU.mult,
                   [tok("rden"), tok("sga")], [tok("sga")])

        def odd_tile(S, t, l, o, skip_norm=False, next_norm=None):
            n = S.TT
            if not skip_norm:
                norm_mod(S, t, l)
            ntg = (n + 127) // 128
            stk = tok("lnst")

            def u_chunk(c):
                b = proj_fm(w_in, None, c * 128, 128, n)
                act(u_sb[:, c, 0:n], banks[b][:, 0:n], AF.Gelu_apprx_tanh, [btok[b]], [tok("u")])

            def g_chunk(c):
                b = proj_fm(w_in, None, 2048 + c * 128, 128, n)
                act(sg_sb[:, c, 0:n], banks[b][:, 0:n], AF.Silu, [btok[b]], [tok("sg")])

            def ln1(tg):
                m = min(128, n - tg * 128)
                for half in range(2):
                    b = PPB[nxt("pp", 4, 0)]
                    for kc in range(8):
                        mm(banks[b][0:m, 0:512], hT[:, kc, tg * 128:tg * 128 + m],
                           w_in[:, kc, 1024 + half * 512:1024 + (half + 1) * 512], kc == 0, kc == 7,
                           [wtk(1024 + half * 512), tok("hT")], [btok[b]])
                    act(vtm[0:m, half * 512:(half + 1) * 512], banks[b][0:m, 0:512], AF.Gelu_apprx_tanh, [btok[b]],
                        [tok("vtm")])
                ts("dve", vnf[0:m, :], vtm[0:m, :], 1.0 / D, None, ALU.mult, ALU.add, [tok("vtm")], [tok("sq"), stk],
                   accum=st_mean[0:m, 0:1])
                ts("dve", vtm[0:m, :], vtm[0:m, :], st_mean[0:m, 0:1], None, ALU.subtract, None, [tok("vtm"), stk],
                   [tok("vtm")])

            def ln2(tg):
                m = min(128, n - tg * 128)
                act(vnf[0:m, :], vtm[0:m, :], AF.Square, [tok("vtm")], [tok("sq"), stk], scale=1.0 / 32.0,
                    accum=st_var[0:m, 0:1])

            def ln3(tg):
                m = min(128, n - tg * 128)
                act(st_sd[0:m, 0:1], st_var[0:m, 0:1], AF.Sqrt, [stk], [stk], bias=eps_sb[0:m, 0:1])
                recip(st_rstd[0:m, 0:1], st_sd[0:m, 0:1], [stk], [stk])
                stt("dve", vnf[0:m, :], vtm[0:m, :], st_rstd[0:m, 0:1], lng_b[0:m, :], ALU.mult, ALU.mult,
                    [tok("vtm"), stk, tok("lnp")], [tok("sq")])
                tt("dve", vnf[0:m, :], vnf[0:m, :], lnb_b[0:m, :], ALU.add, [tok("sq"), tok("lnp")], [tok("sq")])

            def ln4(tg):
                m = min(128, n - tg * 128)
                cp("act", vn[0:m, tg, :], vnf[0:m, :], [tok("sq")], [tok("vn")])
                if S.sample:
                    dma(o_cvs[o], vnf[0:m, :], [tok("sq")], [], "o_cv")

            chunks = [("u", c) for c in range(8)] + [("g", c) for c in range(8)]
            ci = [0]

            def some_chunks(k):
                for _ in range(k):
                    if ci[0] < len(chunks):
                        kind, c = chunks[ci[0]]
                        ci[0] += 1
                        (u_chunk if kind == "u" else g_chunk)(c)

            for tg in range(ntg):
                if tg > 0:
                    some_chunks(4)
                ln1(tg)
                some_chunks(2)
                ln2(tg)
                some_chunks(2)
                ln3(tg)
                some_chunks(4 if ntg > 1 else 12)
                ln4(tg)
            some_chunks(16)
            run_hook('after_in')
            tt("dve", u_sb[:, :, 0:n], u_sb[:, :, 0:n], sg_sb[:, :, 0:n], ALU.mult, [tok("u"), tok("sg")], [tok("u")])
            for tg in range(ntg):
                m = min(128, n - tg * 128)
                for half in range(2):
                    b = nxt("px", 2, 2)
                    for gi in range(4):
                        gidx = half * 4 + gi
                        mm(banks[b][:, gi * 128:gi * 128 + m], vn[0:m, tg, gidx * 128:(gidx + 1) * 128],
                           wmT[0:m, gidx, 0:m], True, True, [tok("vn"), tok("wmT")], [btok[b]])
                    pv = v3(banks[b][:, 0:512], 4)[:, :, 0:m]
                    tt("dve", v3(s_tmp[:, 0:512], 4)[:, :, 0:m], pv, bs_b[:, half * 4:half * 4 + 4, 0:m], ALU.add,
                       [btok[b], tok("lnp")], [tok("tmpn0"), tok("tmpn1")])
                    tt("dve", gated[:, half * 4:half * 4 + 4, tg * 128:tg * 128 + m], v3(s_tmp[:, 0:512], 4)[:, :, 0:m],
                       u_sb[:, half * 4:half * 4 + 4, tg * 128:tg * 128 + m], ALU.mult,
                       [tok("tmpn0"), tok("tmpn1"), tok("u")], [tok("gated")])
            rhs = [(gated[:, kc, 0:n], [tok("gated")]) for kc in range(8)]
            if next_norm is not None:
                norm_mod(*next_norm)
            out_proj_residual(S, t, l, tok("w_out_c0"), rhs)
            run_hook("after_out")

        fin = [0]

        def final_tile(S, t):
            n = S.TT
            xt = S.x(t)
            fi = fin[0]
            fin[0] = 1 - fi
            rs_, rtok = rms_stats(S, t, alt=(fi == 1))
            yst = yst2[fi]
            for kc in range(8):
                stt("dve", yst[:, kc, 0:n], xt[:, kc, :], fg[:, kc:kc + 1], rs_[:, 0:n], ALU.mult, ALU.mult,
                    [S.xtok[t], rtok], [tok("yst%d" % fi)])
            if S.sample:
                dma(o_ys.rearrange("(kc p) t -> p kc t", p=128), yst[:, :, 0:n], [tok("yst%d" % fi)], [], "o_y%d" % fi)
            else:
                dma(o_yp[S.pi, :, t * TT:(t + 1) * TT].rearrange("(kc p) t -> p kc t", p=128), yst[:, :, 0:n],
                    [tok("yst%d" % fi)], [], "o_y%d" % fi)

        LAYERS = [(pi, l) for pi in range(2) for l in range(4)]

        def issue_in(idx):
            if idx >= len(LAYERS):
                return
            pi, l = LAYERS[idx]
            if l % 2 == 0:
                issue_w(d_wine[l // 2], w_in, EVEN_IN, "w_in")
            else:
                issue_w(d_wino[l // 2], w_in, 3 * D, "w_in")

        def issue_out(idx):
            if idx >= len(LAYERS):
                return
            pi, l = LAYERS[idx]
            issue_w((d_woute if l % 2 == 0 else d_wouto)[l // 2], w_out, D, "w_out")

        def pool_dma(out, in_, writes, key):
            P.add("pool", (lambda o, i_: (lambda e: e.dma_start(out=o, in_=i_)))(out, in_), [], writes, dma_key=key)

        def main_schedule():
            issue_in(0)
            issue_out(0)
            for pi in range(2):
                SP_ = mkseq(pi)
                seqs = [SP_] + ([SS] if pi == 1 else [])
                barrier()
                stage_mark("xload%d" % pi)
                for t in range(NT):
                    dma(x_sb[:, :, t * TT:(t + 1) * TT],
                        d_xp[pi, :, t * TT:(t + 1) * TT].rearrange("(kc p) t -> p kc t", p=128),
                        [], [SP_.xtok[t]], "xl%d" % t)
                if pi == 1:
                    dma(xs_sb[:, :, :], d_xs.rearrange("(kc p) t -> p kc t", p=128), [], [SS.xtok[0]], "xs")
                for l in range(4):
                    idx = pi * 4 + l
                    barrier()
                    stage_mark("layer p%d l%d" % (pi, l))
                    last_S = seqs[-1]
                    if l % 2 == 0:
                        e = l // 2
                        for r_ in range(4):
                            ts("dve", I4B[:, r_, :], ident_f[:, :], 30000.0, None, ALU.mult, None, [], [tok("I4B")])
                        memset("dve", qTz.rearrange("p g r t -> p (g r t)"), 0.0, [], [tok("qT")])
                        memset("dve", qiTz.rearrange("p h t -> p (h t)"), 0.0, [], [tok("qiT")])
                        for S in seqs:
                            if S.sample:
                                pool_dma(kT[:, 0:PAST], d_ckT[e], [tok("kT%d" % i) for i in range(4)], "c_k")
                                pool_dma(kiT[0:64, 0:PAST], d_ckiT[e], [tok("kiT%d" % i) for i in range(4)], "c_ki")
                                pool_dma(kiT[64:128, 0:PAST], d_ckiT[e], [tok("kiT%d" % i) for i in range(4)], "c_ki2")
                                pool_dma(v_sb[:, 0:8, :], d_cv[e].rearrange("(kb p) c -> p kb c", p=128),
                                         [tok("v%d" % i) for i in range(4)], "c_v")
                                dma(zp_s[:, :, 0:2], d_cconv[e], [], SS.zptok, "cconv")
                            else:
                                for c in range(4):
                                    memset("pool", zp_p[:, c, 0:2], 0.0, [], [S.zptok[c]])
                            for t in range(S.nt):
                                stage_mark("even tile p%d l%d t%d" % (pi, l, t))
                                if S is last_S and t == S.nt - 1:
                                    HOOK["after_in"] = (lambda i_=idx: issue_in(i_ + 1))
                                    HOOK["after_out"] = (lambda i_=idx: issue_out(i_ + 1))
                                even_tile(S, t, l, e, skip_norm=(t > 0),
                                          next_norm=((S, t + 1, l) if t + 1 < S.nt else None))
                    else:
                        o = l // 2
                        dma(v3(wst_f, 8), d_wsT[o], [], [tok("wst")], "wst")
                        tt("dve", wmT[:, :, :], v3(wst_f, 8), tril_f[:, :].unsqueeze(1).to_broadcast([128, 8, 128]),
                           ALU.mult, [tok("wst")], [tok("wmT")])
                        dma(lng_b, d_lng[o:o + 1, :].partition_broadcast(128), [], [tok("lnp")], "lnp")
                        dma(lnb_b, d_lnb[o:o + 1, :].partition_broadcast(128), [], [tok("lnp2")], "lnp2")
                        dma(bs_b.rearrange("p a b -> p (a b)"), d_bs[o:o + 1, :].partition_broadcast(128), [],
                            [tok("lnp3")], "lnp3")
                        barrier()
                        for S in seqs:
                            for t in range(S.nt):
                                stage_mark("odd tile p%d l%d t%d" % (pi, l, t))
                                if S is last_S and t == S.nt - 1:
                                    HOOK["after_in"] = (lambda i_=idx: issue_in(i_ + 1))
                                    HOOK["after_out"] = (lambda i_=idx: issue_out(i_ + 1))
                                odd_tile(S, t, l, o, skip_norm=(t > 0),
                                         next_norm=((S, t + 1, l) if t + 1 < S.nt else None))
                barrier()
                for S in seqs:
                    for t in range(S.nt):
                        stage_mark("final p%d t%d" % (pi, t))
                        final_tile(S, t)

        try:
            main_schedule()
        except _Stop:
            pass
        barrier()
        P.emit()
    return nc


_STAGE_LIMIT = None
_SUB_MARKS = False
_CAST_ENG = "act"

_NC_CACHE = {}


def kernel(x_prompt, x_sample, cache_a_k, cache_a_v, cache_a_kidx, state_b_conv, c_prompt, c_sample,
           ada_w, ada_b, norm_g, ev_w_in, ev_conv_w, ev_w_out, od_w_in, od_ws, od_bs, od_ln_g, od_ln_b,
           od_w_out, final_g):
    f = lambda a: np.ascontiguousarray(np.asarray(a, dtype=np.float32))
    x_prompt = np.asarray(x_prompt, np.float32); x_sample = np.asarray(x_sample, np.float32)
    perm = np.arange(512).reshape(2, 4, 64).transpose(1, 0, 2).reshape(-1)
    cols = np.arange(EVEN_IN)
    cols[0:512] = perm
    cols[768:1280] = 768 + perm
    w_in_e = f(np.asarray(ev_w_in)[:, :, cols])
    rows = np.arange(D)
    rows[0:512] = perm
    w_out_e = f(np.asarray(ev_w_out)[:, rows, :])
    shared = {
        "ada_w": f(ada_w),
        "ada_b": f(np.asarray(ada_b).reshape(4, 24, 128).transpose(2, 0, 1)),
        "norm_g": f(np.asarray(norm_g).reshape(4, 8, 128).transpose(2, 0, 1)),
        "final_g": f(np.asarray(final_g).reshape(8, 128).T),
        "w_in_e": w_in_e,
        "conv_w": f(np.asarray(ev_conv_w).reshape(2, 3, 4, 128).transpose(3, 0, 2, 1)),
        "w_out_e": w_out_e,
        "w_in_o": f(od_w_in),
        "wsT": f(np.asarray(od_ws).transpose(0, 3, 1, 2)),
        "bs": f(np.asarray(od_bs).reshape(2, D)),
        "ln_g": f(od_ln_g), "ln_b": f(od_ln_b),
        "w_out_o": f(od_w_out),
        "ident": np.eye(128, dtype=np.float32),
        "trilT": np.triu(np.ones((128, 128), np.float32)),
        "pow2": np.tile(np.array([2.0 ** -(k + 1) for k in range(NIT)] + [2.0 ** -NIT], np.float32)[None, :], (128, 1)),
    }
    ck = np.asarray(cache_a_k, np.float32); cv = np.asarray(cache_a_v, np.float32)
    cki = np.asarray(cache_a_kidx, np.float32); cst = np.asarray(state_b_conv, np.float32)
    in_maps = []
    for i in range(8):
        m = dict(shared)
        m["xp"] = f(x_prompt[2 * i:2 * i + 2].transpose(0, 2, 1))
        m["xs"] = f(x_sample[i].T)
        m["cT"] = f(np.stack([np.asarray(c_prompt)[2 * i], np.asarray(c_prompt)[2 * i + 1], np.asarray(c_sample)[i]], 1))
        m["ckT"] = f(ck[:, i].reshape(2, PAST, 128).transpose(0, 2, 1))
        m["cv"] = f(cv[:, i].reshape(2, PAST, 128))
        m["ckiT"] = f(cki[:, i].transpose(0, 2, 1))
        m["cconv"] = f(cst[:, i].reshape(2, 2, 4, 128).transpose(0, 3, 2, 1))
        in_maps.append(m)
    if "nc" not in _NC_CACHE:
        _NC_CACHE["nc"] = build_nc()
    res = run_bass_kernel_spmd(_NC_CACHE["nc"], in_maps, core_ids=list(range(8)))
    R = res.results
    y_p = np.empty((16, SEQ, D), np.float32); y_s = np.empty((8, DEC, D), np.float32)
    k_p = np.empty((2, 16, SEQ, 2, 64), np.float32); v_p = np.empty((2, 16, SEQ, 2, 64), np.float32)
    ki_p = np.empty((2, 16, SEQ, 64), np.float32); conv_p = np.empty((2, 16, 2, 512), np.float32)
    k_s = np.empty((2, 8, DEC, 2, 64), np.float32); v_s = np.empty((2, 8, DEC, 2, 64), np.float32)
    ki_s = np.empty((2, 8, DEC, 64), np.float32); conv_s = np.empty((2, 8, 2, 512), np.float32)
    cv_s = np.empty((2, 8, DEC, D), np.float32)
    for i in range(8):
        r = R[i]
        for s in range(2):
            b = 2 * i + s
            y_p[b] = r["yp"][s].T
            k_p[:, b] = r["kp"][:, s].transpose(0, 2, 1).reshape(2, SEQ, 2, 64)
            v_p[:, b] = r["vp"][:, s].reshape(2, SEQ, 2, 64)
            ki_p[:, b] = r["kip"][:, s].transpose(0, 2, 1)
            conv_p[:, b] = r["convp"][:, s].transpose(0, 3, 2, 1).reshape(2, 2, 512)
        y_s[i] = r["ys"].T
        k_s[:, i] = r["ks"].transpose(0, 2, 1).reshape(2, DEC, 2, 64)
        v_s[:, i] = r["vs"].reshape(2, DEC, 2, 64)
        ki_s[:, i] = r["kis"].transpose(0, 2, 1)
        conv_s[:, i] = r["convs"].transpose(0, 3, 2, 1).reshape(2, 2, 512)
        cv_s[:, i] = r["cvs"]
    return (y_p, y_s, k_p, v_p, ki_p, conv_p, k_s, v_s, ki_s, conv_s, cv_s)
```

```python
import contextlib
import numpy as np
import concourse.bass as bass
import concourse.mybir as mybir
from concourse.bass_utils import run_bass_kernel_spmd

F32 = mybir.dt.float32
BF16 = mybir.dt.bfloat16
AF = mybir.ActivationFunctionType
ALU = mybir.AluOpType
AX = mybir.AxisListType

D = 1024
SEQ = 2048
TT = 256
NT = SEQ // TT
DEC = 64
PAST = 1024
EVEN_IN = 3912
NIT = 16
EPS = 1e-6
NEG = -1.0e30
_SYNC_SAME = True
COMPUTE = ("pe", "act", "dve", "pool")


class Tok:
    __slots__ = ("name", "writer", "readers")

    def __init__(self, name):
        self.name = name
        self.writer = None
        self.readers = {}


class Op:
    __slots__ = ("eng", "fn", "deps", "inc", "val", "dma_key", "is_dma", "soft")

    def __init__(self, eng, fn, dma_key=None):
        self.eng = eng
        self.fn = fn
        self.deps = []
        self.inc = False
        self.val = None
        self.dma_key = dma_key
        self.is_dma = dma_key is not None
        self.soft = ()


class Prog:
    def __init__(self, nc):
        self.nc = nc
        self.ops = {e: [] for e in ("pe", "act", "dve", "pool", "sp")}
        self.GB = Tok("GB")
        self.gap_fn = {}

    def add(self, eng, fn, reads=(), writes=(), dma_key=None, barrier=False):
        op = Op(eng, fn, dma_key)
        reads = list(reads)
        writes = list(writes)
        if barrier:
            writes.append(self.GB)
        else:
            reads.append(self.GB)
        deps = []
        soft = []
        gap_ok = (not _SYNC_SAME) and eng in ("act", "dve") and not op.is_dma
        for t in reads:
            w = t.writer
            if w is None:
                continue
            if (not w.is_dma) and w.eng == eng:
                if eng == "pe":
                    continue
                if gap_ok:
                    soft.append(w)
                    continue
            deps.append(w)
        for t in writes:
            w = t.writer
            same_ok = eng == "pe" or gap_ok
            if w is not None and not ((not w.is_dma) and (not op.is_dma) and w.eng == eng and same_ok):
                deps.append(w)
            for r in t.readers.values():
                if not ((not r.is_dma) and (not op.is_dma) and r.eng == eng and same_ok):
                    deps.append(r)
        seen = set()
        for d in deps:
            if id(d) not in seen and d is not op:
                seen.add(id(d))
                op.deps.append(d)
        op.soft = soft
        rkey = ("dma", dma_key) if op.is_dma else eng
        for t in reads:
            t.readers[rkey] = op
        for t in writes:
            t.writer = op
            t.readers = {}
        self.ops[eng].append(op)
        return op

    def emit(self):
        nc = self.nc
        self.n_gap = 0
        for lst in self.ops.values():
            for op in lst:
                for d in op.deps:
                    d.inc = True
        dma_cnt = {}
        sems = {}
        with contextlib.ExitStack() as stack:
            for e, lst in self.ops.items():
                cnt = 0
                for op in lst:
                    if op.is_dma:
                        k = ("dma", op.dma_key)
                        dma_cnt[k] = dma_cnt.get(k, 0) + 16
                        op.val = (k, dma_cnt[k])
                        op.inc = True
                    elif op.inc:
                        cnt += 1
                        op.val = (e, cnt)
            for i, k in enumerate(list(dma_cnt.keys()) + list(COMPUTE)):
                sems[k] = stack.enter_context(nc.semaphore("s%d" % i))
            engmap = {"pe": "tensor", "act": "scalar", "dve": "vector", "pool": "gpsimd", "sp": "sync"}
            with nc.Block() as block:
                def mk(ename):
                    def body(eng):
                        waited = {}
                        prev = None
                        for op in self.ops[ename]:
                            if prev is not None and any(d is prev for d in op.soft):
                                self.gap_fn[ename](eng)
                                self.n_gap += 1
                            prev = op
                            need = {}
                            for d in op.deps:
                                k, v = d.val
                                if v > need.get(k, 0):
                                    need[k] = v
                            for k, v in need.items():
                                if waited.get(k, 0) >= v:
                                    continue
                                waited[k] = v
                                eng.wait_ge(sems[k], v)
                            ins = op.fn(eng)
                            if op.inc:
                                ins.then_inc(sems[op.val[0]], 16 if op.is_dma else 1)
                        if ename == "sp":
                            for k, v in dma_cnt.items():
                                if waited.get(k, 0) < v:
                                    eng.wait_ge(sems[k], v)
                    return body
                for en, attr in engmap.items():
                    getattr(block, attr)(mk(en))


def build_nc():
    nc = bass.Bass("TRN2", target_bir_lowering=False)
    P = Prog(nc)

    def din(name, shape):
        return nc.dram_tensor(name, list(shape), F32, kind="ExternalInput").ap()

    def dout(name, shape):
        return nc.dram_tensor(name, list(shape), F32, kind="ExternalOutput").ap()

    d_xp = din("xp", [2, D, SEQ]); d_xs = din("xs", [D, DEC]); d_cT = din("cT", [D, 3])
    d_adaw = din("ada_w", [4, D, 3 * D]); d_adab = din("ada_b", [128, 4, 24])
    d_ng = din("norm_g", [128, 4, 8]); d_fg = din("final_g", [128, 8])
    d_wine = din("w_in_e", [2, D, EVEN_IN]); d_convw = din("conv_w", [128, 2, 4, 3])
    d_woute = din("w_out_e", [2, D, D]); d_wino = din("w_in_o", [2, D, 3 * D])
    d_wsT = din("wsT", [2, 128, 8, 128]); d_bs = din("bs", [2, D]); d_lng = din("ln_g", [2, D])
    d_lnb = din("ln_b", [2, D]); d_wouto = din("w_out_o", [2, D, D])
    d_ckT = din("ckT", [2, 128, PAST]); d_cv = din("cv", [2, PAST, 128]); d_ckiT = din("ckiT", [2, 64, PAST])
    d_cconv = din("cconv", [2, 128, 4, 2])
    d_ident = din("ident", [128, 128]); d_tril = din("trilT", [128, 128]); d_pow2 = din("pow2", [128, NIT + 1])

    o_yp = dout("yp", [2, D, SEQ]); o_ys = dout("ys", [D, DEC])
    o_kp = dout("kp", [2, 2, 128, SEQ]); o_vp = dout("vp", [2, 2, SEQ, 128]); o_kip = dout("kip", [2, 2, 64, SEQ])
    o_convp = dout("convp", [2, 2, 128, 4, 2])
    o_ks = dout("ks", [2, 128, DEC]); o_vs = dout("vs", [2, DEC, 128]); o_kis = dout("kis", [2, 64, DEC])
    o_convs = dout("convs", [2, 128, 4, 2]); o_cvs = dout("cvs", [2, DEC, D])

    ARENA_BYTES = 49152
    with contextlib.ExitStack() as st:
        def sb(name, shape, dt):
            return st.enter_context(nc.sbuf_tensor("sb_" + name, list(shape), dt))

        x_sb = sb("x_sb", [128, 8, SEQ], F32)
        xs_sb = sb("xs_sb", [128, 8, DEC], F32)
        w_in = sb("w_in", [128, 8, EVEN_IN], BF16)
        w_out = sb("w_out", [128, 8, D], BF16)
        kT = sb("kT", [128, SEQ], BF16)
        kiT = sb("kiT", [128, SEQ], BF16)
        v_sb = sb("v_sb", [128, 16, 128], BF16)
        ones_bf = sb("ones_bf", [128, 128], BF16)
        onesm = sb("onesm", [128, 128], BF16)
        ident_f = sb("ident_f", [128, 128], F32)
        ident_bf = sb("ident_bf", [128, 128], BF16)
        tril_f = sb("tril_f", [128, 128], F32)
        pow2 = sb("pow2", [128, NIT + 1], F32)
        adab = sb("adab", [128, 4, 24], F32)
        normg = sb("normg", [128, 4, 8], F32)
        fg = sb("fg", [128, 8], F32)
        convw = sb("convw", [128, 2, 4, 3], F32)
        cT = sb("cT", [128, 8, 3], F32)
        scT = sb("scT", [128, 8, 3], F32)
        mod = sb("mod", [128, 4, 24, 3], F32)
        Gm = sb("Gm", [128, 4, 3, 8], F32)
        arena = sb("arena", [128, ARENA_BYTES // 2], BF16)
        banks = [st.enter_context(nc.psum_tensor("bank%d" % i, [128, 512], F32)) for i in range(8)]
        btok = [Tok("bank%d" % i) for i in range(8)]

        class Arena:
            def __init__(self):
                self.off = 0

            def alloc(self, nelem, dt):
                nb = nelem * (4 if dt == F32 else 2)
                nb = (nb + 63) // 64 * 64
                a = self.off
                self.off += nb
                assert self.off <= ARENA_BYTES, ("arena overflow", self.off)
                v = arena[:, a // 2:(a + nb) // 2]
                if dt == F32:
                    v = v.bitcast(F32)
                    return v[:, 0:nelem]
                return v[:, 0:nelem]

        def v3(ap, a):
            return ap.rearrange("p (a b) -> p a b", a=a)

        A = Arena()
        stage = [A.alloc(1024, F32) for _ in range(3)]
        stage_tok = [Tok("stage%d" % i) for i in range(3)]
        wst_f = A.alloc(8 * 128, F32)
        A = Arena()
        hT = v3(A.alloc(8 * TT, BF16), 8)
        sq = v3(A.alloc(8 * TT, BF16), 8)
        sq_off = A.off - 8 * TT * 2
        rstd_t = A.alloc(TT, F32)
        rstd = A.alloc(TT, F32)
        tmpn_off = A.off
        tmpn = [A.alloc(TT, F32) for _ in range(2)]
        common_off = A.off
        xin_sb = A.alloc(TT, F32); gb_sb = A.alloc(TT, F32); sgz_sb = A.alloc(TT, F32); y_sb = A.alloc(TT, F32)
        kst = A.alloc(TT, F32); kist = A.alloc(TT, F32); vst = v3(A.alloc(2 * 128, F32), 2)
        phB_end = A.off
        A = Arena()
        score = [A.alloc(SEQ, F32) for _ in range(2)]
        _mk = A.alloc(SEQ, BF16)
        maskm1 = [_mk, _mk]
        junk_off = A.off
        _dg = v3(A.alloc(8 * 128, BF16), 8)
        dg = [_dg, _dg]
        rl = [A.alloc(512, BF16) for _ in range(2)]
        pT = [A.alloc(512, BF16) for _ in range(2)]
        rden = A.alloc(512, F32)
        junk1 = arena[:, junk_off // 2:junk_off // 2 + SEQ]
        lnd = arena[:, junk_off // 2:junk_off // 2 + 1024].bitcast(F32)
        assert A.off - junk_off >= 2 * SEQ
        bis = []
        for _i in range(2):
            bis.append(dict(dh=A.alloc(NIT + 1, F32), mid=A.alloc(NIT + 1, F32), cnt=A.alloc(NIT, F32),
                            tmp=A.alloc(NIT, F32), mx=A.alloc(1, F32), mn=A.alloc(1, F32), d0=A.alloc(1, F32)))
        A.off = max(A.off, phB_end)
        qTz = A.alloc(2 * 4 * TT, BF16).rearrange("p (g r t) -> p g r t", g=2, r=4)
        qiTz = v3(A.alloc(8 * TT, BF16), 8)
        sga = v3(A.alloc(4 * TT, BF16), 4)
        b_out = v3(A.alloc(4 * TT, BF16), 4)
        wi_sb = v3(A.alloc(2 * 8, F32), 2)
        zp_p = v3(A.alloc(4 * (TT + 2), F32), 4)
        zp_s = v3(A.alloc(4 * (DEC + 2), F32), 4)
        I4B = v3(A.alloc(4 * 128, BF16), 4)
        even_end = A.off
        A = Arena(); A.off = common_off
        u_sb = v3(A.alloc(8 * TT, BF16), 8)
        sg_sb = v3(A.alloc(8 * TT, BF16), 8)
        gated = v3(A.alloc(8 * TT, BF16), 8)
        vtm = A.alloc(D, F32)
        vn = v3(A.alloc(2 * D, BF16), 2)
        lng_b = A.alloc(D, F32); lnb_b = A.alloc(D, F32)
        bs_b = v3(A.alloc(D, F32), 8)
        wmT = v3(A.alloc(8 * 128, BF16), 8)
        st_mean = A.alloc(1, F32); st_var = A.alloc(1, F32); st_rstd = A.alloc(1, F32); st_sd = A.alloc(1, F32)
        odd_end = A.off
        vnf = arena[:, sq_off // 2:sq_off // 2 + 2 * D].bitcast(F32)
        s_tmp = arena[:, tmpn_off // 2:tmpn_off // 2 + 1024].bitcast(F32)
        A = Arena(); A.off = common_off
        yst2 = [v3(A.alloc(8 * TT, F32), 8) for _ in range(2)]
        sqF = v3(A.alloc(8 * TT, BF16), 8); rstd_tF = A.alloc(TT, F32); rstdF = A.alloc(TT, F32)
        print("arena: even", even_end, "odd", odd_end, "of", ARENA_BYTES)

        T = {}

        def tok(n):
            if n not in T:
                T[n] = Tok(n)
            return T[n]

        def dma(out, in_, reads, writes, key):
            P.add("sp", lambda e: e.dma_start(out=out, in_=in_), reads, writes, dma_key=key)

        def mm(out, lhsT, rhs, start, stop, reads, writes, tp=None):
            if tp is None:
                P.add("pe", lambda e: e.matmul(out, lhsT=lhsT, rhs=rhs, start=start, stop=stop), reads, writes)
            else:
                P.add("pe", lambda e: e.matmul(out, lhsT=lhsT, rhs=rhs, start=start, stop=stop, tile_position=tp),
                      reads, writes)

        def act(out, in_, func, reads, writes, scale=1.0, bias=None, accum=None):
            kw = {}
            if bias is not None:
                kw["bias"] = bias
            if accum is not None:
                kw["accum_out"] = accum
            P.add("act", lambda e: e.activation(out=out, in_=in_, func=func, scale=scale, **kw), reads, writes)

        def ts(eng, out, in0, s1, s2, op0, op1, reads, writes, accum=None):
            if accum is not None:
                P.add(eng, lambda e: e.tensor_scalar(out=out, in0=in0, scalar1=s1, scalar2=s2, op0=op0, op1=op1,
                                                     accum_out=accum), reads, writes)
            elif op1 is None:
                P.add(eng, lambda e: e.tensor_scalar(out=out, in0=in0, scalar1=s1, scalar2=None, op0=op0), reads, writes)
            else:
                P.add(eng, lambda e: e.tensor_scalar(out=out, in0=in0, scalar1=s1, scalar2=s2, op0=op0, op1=op1),
                      reads, writes)

        def tt(eng, out, in0, in1, op, reads, writes):
            P.add(eng, lambda e: e.tensor_tensor(out=out, in0=in0, in1=in1, op=op), reads, writes)

        def stt(eng, out, in0, scalar, in1, op0, op1, reads, writes):
            P.add(eng, lambda e: e.scalar_tensor_tensor(out=out, in0=in0, scalar=scalar, in1=in1, op0=op0, op1=op1),
                  reads, writes)

        def cp(eng, out, in_, reads, writes):
            if eng == "act":
                P.add("act", lambda e: e.activation(out=out, in_=in_, func=AF.Copy), reads, writes)
            else:
                P.add(eng, lambda e: e.tensor_copy(out=out, in_=in_), reads, writes)

        def memset(eng, ap, val, reads, writes):
            P.add(eng, lambda e: e.memset(ap, val), reads, writes)

        def recip(out, in_, reads, writes):
            P.add("dve", lambda e: e.reciprocal(out=out, in_=in_), reads, writes)

        def barrier():
            P.add("pool", lambda e: e.memset(bar_sb[:, 0:1], 0.0), [], [], barrier=True)

        bar_sb = sb("bar_sb", [128, 2], F32)
        rr = {"pp": 0, "px": 0, "pl": 0, "st": 0, "rl": 0, "pT": 0, "tmpn": 0, "pa": 0}

        PPB = [0, 1, 6, 7, 4, 5]

        def nxt(kind, n, base):
            i = rr[kind]
            rr[kind] = (i + 1) % n
            return base + i

        class _Stop(Exception):
            pass

        stage_ctr = [0]

        def stage_mark(name):
            stage_ctr[0] += 1
            if _STAGE_LIMIT is not None and stage_ctr[0] > _STAGE_LIMIT:
                print("build stopped before stage", stage_ctr[0], name)
                raise _Stop()

        def sub_mark(name):
            if _SUB_MARKS:
                stage_mark(name)

        cst = 0

        def cload(dst, src):
            nonlocal cst
            cst += 1
            dma(dst, src, [], [tok("c%d" % cst)], "const")

        cload(ident_f[:], d_ident); cload(tril_f[:], d_tril); cload(pow2[:], d_pow2)
        cload(adab[:], d_adab); cload(normg[:], d_ng); cload(fg[:], d_fg); cload(convw[:], d_convw)
        cload(cT[:], d_cT.rearrange("(kc p) j -> p kc j", p=128))
        memset("pool", ones_bf[:], 1.0, [], [tok("ones")])
        memset("pool", onesm[:], 1.0 / D, [], [tok("onesm")])
        barrier()
        cp("pool", ident_bf[:], ident_f[:], [], [tok("identbf")])
        act(scT[:], cT[:], AF.Silu, [], [tok("scT")])
        barrier()

        ada_st = [v3(arena[:, i * 8192:(i + 1) * 8192].bitcast(F32), 8) for i in range(2)]
        ada_tok = [Tok("adast0"), Tok("adast1")]
        ada_rr = 0
        modtm = arena[:, 16384:16384 + 6144].bitcast(F32)
        for l in range(4):
            for cc in range(6):
                s = ada_rr
                ada_rr = 1 - ada_rr
                bk = cc % 2
                dma(ada_st[s], d_adaw[l, :, cc * 512:(cc + 1) * 512].rearrange("(kc p) c -> p kc c", p=128), [],
                    [ada_tok[s]], "adast%d" % s)
                for kc in range(8):
                    mm(banks[bk][0:3, 0:512], scT[:, kc, :], ada_st[s][:, kc, :], kc == 0, kc == 7,
                       [ada_tok[s], tok("scT")], [btok[bk]])
                cp("dve", modtm[0:3, cc * 512:(cc + 1) * 512], banks[bk][0:3, 0:512], [btok[bk]], [tok("modtm")])
            for oc in range(24):
                mm(banks[2][:, oc * 3:(oc + 1) * 3], modtm[0:3, oc * 128:(oc + 1) * 128], ident_f[0:3, 0:3], True, True,
                   [tok("modtm")], [btok[2]])
            for j in range(3):
                tt("dve", mod[:, l, :, j], v3(banks[2][:, 0:72], 24)[:, :, j], adab[:, l, :], ALU.add,
                   [btok[2]], [tok("mod")])
            for j in range(3):
                stt("dve", Gm[:, l, j, :], mod[:, l, 8:16, j], 1.0, normg[:, l, :], ALU.add, ALU.mult,
                    [tok("mod")], [tok("Gm")])
        barrier()

        def issue_w(dram_w, dst, C, name):
            for ci in range((C + 1023) // 1024):
                c0 = ci * 1024
                n = min(1024, C - c0)
                P.add("pool", (lambda o, i_: (lambda e: e.dma_start(out=o, in_=i_)))(
                    dst[:, :, c0:c0 + n], dram_w[:, c0:c0 + n].rearrange("(kc p) c -> p kc c", p=128)),
                    [], [tok("%s_c%d" % (name, ci))], dma_key="%s_c%d" % (name, ci))

        def wtk(c0):
            return tok("w_in_c%d" % (c0 // 1024))

        HOOK = {"after_in": None, "after_out": None}

        def run_hook(k):
            f = HOOK[k]
            HOOK[k] = None
            if f is not None:
                f()

        class Seq:
            pass

        def mkseq(pi):
            S = Seq()
            S.sample = False; S.j = pi; S.TT = TT; S.nt = NT; S.pi = pi
            S.x = lambda t: x_sb[:, :, t * TT:(t + 1) * TT]
            S.xtok = [tok("x%d" % t) for t in range(NT)]
            S.zp = zp_p; S.zptok = [tok("zp%d" % c) for c in range(4)]
            return S

        SS = Seq()
        SS.sample = True; SS.j = 2; SS.TT = DEC; SS.nt = 1; SS.pi = 0
        SS.x = lambda t: xs_sb[:, :, :]
        SS.xtok = [tok("xs")]
        SS.zp = zp_s; SS.zptok = [tok("zps%d" % c) for c in range(4)]

        def rms_stats(S, t, alt=False):
            n = S.TT
            xt = S.x(t)
            sq_, rt_, rs_, sfx = (sqF, rstd_tF, rstdF, "F") if alt else (sq, rstd_t, rstd, "")
            act(sq_[:, :, 0:n], xt, AF.Square, [S.xtok[t]], [tok("sq" + sfx)])
            b = nxt("px", 2, 2)
            for kc in range(8):
                mm(banks[b][:, 0:n], onesm[:, :], sq_[:, kc, 0:n], kc == 0, kc == 7, [tok("sq" + sfx), tok("onesm")],
                   [btok[b]])
            act(rt_[:, 0:n], banks[b][:, 0:n], AF.Sqrt, [btok[b]], [tok("rstd_t" + sfx)], bias=eps_sb[:, 0:1])
            recip(rs_[:, 0:n], rt_[:, 0:n], [tok("rstd_t" + sfx)], [tok("rstd" + sfx)])
            return rs_, tok("rstd" + sfx)

        eps_sb = sb("eps_sb", [128, 1], F32)
        memset("pool", eps_sb[:], EPS, [], [tok("eps")])

        def norm_mod(S, t, l):
            n = S.TT
            xt = S.x(t)
            rms_stats(S, t)
            for kc in range(8):
                s = nxt("tmpn", 2, 0)
                stt("dve", tmpn[s][:, 0:n], xt[:, kc, :], Gm[:, l, S.j, kc:kc + 1], rstd[:, 0:n], ALU.mult, ALU.mult,
                    [S.xtok[t], tok("rstd"), tok("Gm")], [tok("tmpn%d" % s)])
                act(hT[:, kc, 0:n], tmpn[s][:, 0:n], AF.Identity, [tok("tmpn%d" % s), tok("mod")], [tok("hT")],
                    bias=mod[:, l, kc, S.j:S.j + 1])

        def proj_fm(W, wtok, c0, M, n, p0=0):
            b = PPB[nxt("pp", 6, 0)]
            for kc in range(8):
                mm(banks[b][p0:p0 + M, 0:n], W[:, kc, c0:c0 + M], hT[:, kc, 0:n], kc == 0, kc == 7,
                   [wtk(c0), tok("hT")], [btok[b]], tp=(0, p0) if p0 else None)
            return b

        def out_proj_residual(S, t, l, wtok, rhs_list):
            n = S.TT
            xt = S.x(t)
            for oc in range(8):
                b = PPB[nxt("pp", 6, 0)]
                for i, (rap, rtoks) in enumerate(rhs_list):
                    mm(banks[b][:, 0:n], w_out[:, i, oc * 128:(oc + 1) * 128], rap, i == 0, i == len(rhs_list) - 1,
                       [wtok] + rtoks, [btok[b]])
                stt("dve", xt[:, oc, :], banks[b][:, 0:n], mod[:, l, 16 + oc, S.j:S.j + 1], xt[:, oc, :],
                    ALU.mult, ALU.add, [btok[b], tok("mod"), S.xtok[t]], [S.xtok[t]])

        def even_tile(S, t, l, e, skip_norm=False, next_norm=None):
            n = S.TT
            wt = tok("w_in")
            key0 = PAST if S.sample else t * TT
            if not skip_norm:
                norm_mod(S, t, l)
            sub_mark('after norm')
            for r in range(4):
                b = proj_fm(w_in, wt, r * 128, 128, n)
                cp("act", qTz[0:64, 0, r, 0:n], banks[b][0:64, 0:n], [btok[b]], [tok("qT")])
                cp("act", qTz[64:128, 1, r, 0:n], banks[b][64:128, 0:n], [btok[b]], [tok("qT")])
            b = proj_fm(w_in, wt, 512, 128, n)
            cp("dve", kst[:, 0:n], banks[b][:, 0:n], [btok[b]], [tok("kst")])
            cp("act", kT[:, key0:key0 + n], kst[:, 0:n], [tok("kst")], [tok("kT%d" % (key0 // TT))])
            if S.sample:
                dma(o_ks[e], kst[:, 0:n], [tok("kst")], [], "o_k")
            else:
                dma(o_kp[e, S.pi, :, key0:key0 + n], kst[:, 0:n], [tok("kst")], [], "o_k")
            for r in range(4):
                b = proj_fm(w_in, wt, 768 + r * 128, 128, n)
                act(sga[:, r, 0:n], banks[b][:, 0:n], AF.Silu, [btok[b]], [tok("sga")])
            for jj in range(4):
                b = proj_fm(w_in, wt, 1280 + jj * 128, 128, n)
                cp("dve", qiTz[0:64, 2 * jj, 0:n], banks[b][0:64, 0:n], [btok[b]], [tok("qiT")])
                cp("dve", qiTz[64:128, 2 * jj + 1, 0:n], banks[b][64:128, 0:n], [btok[b]], [tok("qiT")])
            b = PPB[nxt("pp", 6, 0)]
            for half in range(2):
                for kc in range(8):
                    mm(banks[b][half * 64:(half + 1) * 64, 0:n], w_in[:, kc, 1792:1856], hT[:, kc, 0:n], kc == 0, kc == 7,
                       [wtk(1792), tok("hT")], [btok[b]], tp=(0, 64) if half else None)
            cp("dve", kist[:, 0:n], banks[b][:, 0:n], [btok[b]], [tok("kist")])
            cp("act", kiT[:, key0:key0 + n], kist[:, 0:n], [tok("kist")], [tok("kiT%d" % (key0 // TT))])
            if S.sample:
                dma(o_kis[e], kist[0:64, 0:n], [tok("kist")], [], "o_ki")
            else:
                dma(o_kip[e, S.pi, :, key0:key0 + n], kist[0:64, 0:n], [tok("kist")], [], "o_ki")
            sub_mark('after qkgaqiki')
            ntg = (n + 127) // 128
            for tg in range(ntg):
                m = min(128, n - tg * 128)
                kb = (key0 + tg * 128) // 128
                b = nxt("px", 2, 2)
                for kc in range(8):
                    mm(banks[b][0:m, 0:128], hT[:, kc, tg * 128:tg * 128 + m], w_in[:, kc, 640:768], kc == 0, kc == 7,
                       [wtk(640), tok("hT")], [btok[b]])
                for kc in range(8):
                    mm(banks[b][0:m, 128:136], hT[:, kc, tg * 128:tg * 128 + m], w_in[:, kc, 1856:1864], kc == 0, kc == 7,
                       [wtk(1856), tok("hT")], [btok[b]])
                cp("dve", vst[0:m, tg, :], banks[b][0:m, 0:128], [btok[b]], [tok("vst")])
                cp("act", v_sb[0:m, kb, :], vst[0:m, tg, :], [tok("vst")], [tok("v%d" % (key0 // TT))])
                cp("dve", wi_sb[0:m, tg, :], banks[b][0:m, 128:136], [btok[b]], [tok("wi")])
            if S.sample:
                dma(o_vs[e], vst[0:DEC, 0, :], [tok("vst")], [], "o_v")
            else:
                dma(o_vp[e, S.pi, key0:key0 + n, :].rearrange("(tg p) c -> p tg c", p=128), vst[:, :, :],
                    [tok("vst")], [], "o_v")
            sub_mark('after v/wi')
            zp = S.zp
            for c in range(4):
                zt = S.zptok[c]
                b = proj_fm(w_in, wt, 2888 + c * 128, 128, n)
                cp("act", xin_sb[:, 0:n], banks[b][:, 0:n], [btok[b]], [tok("xin")])
                b = proj_fm(w_in, wt, 2376 + c * 128, 128, n)
                tt("dve", zp[:, c, 2:2 + n], banks[b][:, 0:n], xin_sb[:, 0:n], ALU.mult, [btok[b], tok("xin")], [zt])
                b = proj_fm(w_in, wt, 1864 + c * 128, 128, n)
                cp("act", gb_sb[:, 0:n], banks[b][:, 0:n], [btok[b]], [tok("gb")])
                b = proj_fm(w_in, wt, 3400 + c * 128, 128, n)
                act(sgz_sb[:, 0:n], banks[b][:, 0:n], AF.Silu, [btok[b]], [tok("sgz")])
                ts("dve", y_sb[:, 0:n], zp[:, c, 2:2 + n], convw[:, e, c, 2:3], None, ALU.mult, None, [zt], [tok("y")])
                stt("dve", y_sb[:, 0:n], zp[:, c, 1:1 + n], convw[:, e, c, 1:2], y_sb[:, 0:n], ALU.mult, ALU.add,
                    [zt, tok("y")], [tok("y")])
                stt("dve", y_sb[:, 0:n], zp[:, c, 0:n], convw[:, e, c, 0:1], y_sb[:, 0:n], ALU.mult, ALU.add,
                    [zt, tok("y")], [tok("y")])
                tt("dve", y_sb[:, 0:n], y_sb[:, 0:n], gb_sb[:, 0:n], ALU.mult, [tok("y"), tok("gb")], [tok("y")])
                tt("dve", b_out[:, c, 0:n], y_sb[:, 0:n], sgz_sb[:, 0:n], ALU.mult, [tok("y"), tok("sgz")], [tok("b_out")])
            sub_mark('after conv')
            run_hook('after_in')
            last = (t == S.nt - 1)
            if last:
                if S.sample:
                    dma(o_convs[e], zp[:, :, n:n + 2], [S.zptok[c] for c in range(4)], [], "o_convs")
                else:
                    dma(o_convp[e, S.pi], zp[:, :, n:n + 2], [S.zptok[c] for c in range(4)], [], "o_convp")
            else:
                for c in range(4):
                    cp("dve", zp[:, c, 0:2], zp[:, c, n:n + 2], [S.zptok[c]], [S.zptok[c]])
            nsub = (n + 127) // 128
            barrier()
            attention_tile(S, t, l)
            barrier()
            sub_mark('after attention')
            if next_norm is not None:
                norm_mod(*next_norm)
            rhs = [(sga[:, r, 0:n], [tok("sga")]) for r in range(4)] + [(b_out[:, c, 0:n], [tok("b_out")]) for c in range(4)]
            out_proj_residual(S, t, l, tok("w_out_c0"), rhs)
            run_hook("after_out")

        def attention_tile(S, t, l):
            if S.sample:
                subs = [dict(i=0, nq=DEC, q0=0, blocks=[(kb * 128, 128) for kb in range(8)] + [(PAST, DEC)],
                             corner=False, topk=True)]
            else:
                subs = []
                for sub in range(TT // 128):
                    qt = t * (TT // 128) + sub
                    subs.append(dict(i=sub, nq=128, q0=sub * 128, blocks=[(kb * 128, 128) for kb in range(qt + 1)],
                                     corner=True, topk=qt >= 2))
            for d in subs:
                d["SK"] = d["blocks"][-1][0] + d["blocks"][-1][1]
                nt_ = (d["SK"] + TT - 1) // TT
                d["ktoks"] = [tok("kT%d" % i) for i in range(nt_)]
                d["kitoks"] = [tok("kiT%d" % i) for i in range(nt_)]
                d["vtoks"] = [tok("v%d" % i) for i in range(nt_)]
            steps1 = []
            for d in subs:
                for c0 in range(0, d["SK"], 512):
                    for h in range(8):
                        steps1.append((d, c0, min(512, d["SK"] - c0), h))
            st1 = {}

            def p1_front(k):
                d, c0, m, h = steps1[k]
                i, nq, q0 = d["i"], d["nq"], d["q0"]
                if c0 == 0 and h == 0:
                    tt("dve", dg[i][0:nq, :, 0:nq], ident_bf[0:nq, 0:nq].unsqueeze(1).to_broadcast([nq, 8, nq]),
                       wi_sb[0:nq, i, :].unsqueeze(2).to_broadcast([nq, 8, nq]), ALU.mult,
                       [tok("identbf"), tok("wi")], [tok("dg")])
                b = nxt("px", 2, 2)
                mm(banks[b][0:nq, 0:m], qiTz[0:128, h, q0:q0 + nq], kiT[0:128, c0:c0 + m],
                   True, True, [tok("qiT")] + d["kitoks"], [btok[b]])
                s = nxt("rl", 2, 0)
                act(rl[s][0:nq, 0:m], banks[b][0:nq, 0:m], AF.Relu, [btok[b]], [tok("rl%d" % s)])
                st1[k] = s

            def p1_back(k):
                d, c0, m, h = steps1[k]
                i, nq = d["i"], d["nq"]
                s = st1[k]
                if h == 0:
                    st1["ab"] = nxt("pa", 2, 4)
                ab = st1["ab"]
                mm(banks[ab][0:nq, 0:m], dg[i][0:nq, h, 0:nq], rl[s][0:nq, 0:m], h == 0, h == 7,
                   [tok("dg"), tok("rl%d" % s)], [btok[ab]])
                if h == 7:
                    cp("dve", score[i][0:nq, c0:c0 + m], banks[ab][0:nq, 0:m], [btok[ab]], [tok("score%d" % i)])

            for k in range(len(steps1) + 1):
                flushed = False
                if 1 <= k < len(steps1) and steps1[k][1] == 0 and steps1[k][3] == 0:
                    p1_back(k - 1)
                    flushed = True
                if k < len(steps1):
                    p1_front(k)
                if k >= 1 and not flushed:
                    p1_back(k - 1)
            sub_mark('att p1')
            for d in subs:
                i, nq, SK = d["i"], d["nq"], d["SK"]
                B = bis[i]
                sct, bt = tok("score%d" % i), tok("bis%d" % i)
                if d["topk"]:
                    P.add("dve", (lambda o, a: (lambda e: e.tensor_reduce(out=o, in_=a, axis=AX.X, op=ALU.max)))(
                        B["mx"][0:nq, 0:1], score[i][0:nq, 0:SK]), [sct], [bt])
                    P.add("dve", (lambda o, a: (lambda e: e.tensor_reduce(out=o, in_=a, axis=AX.X, op=ALU.min)))(
                        B["mn"][0:nq, 0:1], score[i][0:nq, 0:SK]), [sct, bt], [bt])
                if d["corner"]:
                    memset("pool", score[i][0:64, SK - 64:SK], NEG, [sct, bt], [sct])
                if d["topk"]:
                    tt("dve", B["d0"][0:nq, :], B["mx"][0:nq, :], B["mn"][0:nq, :], ALU.subtract, [bt], [bt])
                    ts("dve", B["dh"][0:nq, :], pow2[0:nq, :], B["d0"][0:nq, 0:1], None, ALU.mult, None, [bt], [bt])
                    tt("dve", B["mid"][0:nq, 0:1], B["mn"][0:nq, :], B["dh"][0:nq, 0:1], ALU.add, [bt], [bt])
            tk = [d for d in subs if d["topk"]]
            on_act = tk[1]["i"] if len(tk) == 2 else None

            def cnt_op(d, k):
                i, nq, SK = d["i"], d["nq"], d["SK"]
                B = bis[i]
                jbuf = maskm1[0] if i == 0 else junk1
                jtok = [tok("mask")] if i == 0 else [tok("dg"), tok("rl0"), tok("rl1"), tok("pT0"), tok("pT1"),
                                                     tok("rden")]
                if i == on_act:
                    act(jbuf[0:nq, 0:SK], score[i][0:nq, 0:SK], AF.Sign, [tok("score%d" % i), tok("bis%d" % i)],
                        jtok + [tok("bisc%d" % i)], scale=-1.0, bias=B["mid"][0:nq, k:k + 1],
                        accum=B["cnt"][0:nq, k:k + 1])
                else:
                    ts("dve", jbuf[0:nq, 0:SK], score[i][0:nq, 0:SK], B["mid"][0:nq, k:k + 1], None, ALU.is_ge,
                       ALU.add, [tok("score%d" % i), tok("bis%d" % i)], jtok + [tok("bisc%d" % i)],
                       accum=B["cnt"][0:nq, k:k + 1])

            def upd_op(d, k):
                i, nq, SK = d["i"], d["nq"], d["SK"]
                B = bis[i]
                if i == on_act:
                    stt("dve", B["tmp"][0:nq, k:k + 1], B["cnt"][0:nq, k:k + 1], float(SK) - 511.5,
                        B["dh"][0:nq, k:k + 1], ALU.is_le, ALU.mult, [tok("bisc%d" % i), tok("bis%d" % i)],
                        [tok("bist%d" % i)])
                else:
                    stt("dve", B["tmp"][0:nq, k:k + 1], B["cnt"][0:nq, k:k + 1], 255.5, B["dh"][0:nq, k:k + 1],
                        ALU.is_ge, ALU.mult, [tok("bisc%d" % i), tok("bis%d" % i)], [tok("bist%d" % i)])
                stt("dve", B["mid"][0:nq, k + 1:k + 2], B["tmp"][0:nq, k:k + 1], B["mid"][0:nq, k:k + 1],
                    B["dh"][0:nq, k + 1:k + 2], ALU.add, ALU.subtract, [tok("bist%d" % i), tok("bis%d" % i)],
                    [tok("bis%d" % i)])

            for k in range(NIT):
                if on_act is not None:
                    cnt_op(tk[1], k)
                    cnt_op(tk[0], k)
                    upd_op(tk[1], k)
                    upd_op(tk[0], k)
                else:
                    for d in tk:
                        cnt_op(d, k)
                    for d in tk:
                        upd_op(d, k)
            sub_mark('att p2')
            for d in subs:
                i, nq, q0, SK = d["i"], d["nq"], d["q0"], d["SK"]
                thr = bis[i]["mid"][0:nq, NIT:NIT + 1] if d["topk"] else -1.0e29
                ts("dve", maskm1[i][0:nq, 0:SK], score[i][0:nq, 0:SK], thr, -1.0, ALU.is_ge, ALU.add,
                   [tok("score%d" % i), tok("bis%d" % i)], [tok("mask")])
                if nq < 128:
                    memset("dve", maskm1[i][nq:128, 0:SK], 0.0, [], [tok("mask")])
                ob = 6 if i == 0 else 0
                steps3 = [(g, kb, b0, bn) for g in range(2) for kb, (b0, bn) in enumerate(d["blocks"])]
                nb = len(d["blocks"])
                st3 = {}

                def p3_front(k):
                    g, kb, b0, bn = steps3[k]
                    b = nxt("px", 2, 2)
                    mm(v3(banks[b][0:bn, 0:4 * nq], 4), kT[0:128, b0:b0 + bn], qTz[0:128, g, 0:4, q0:q0 + nq],
                       True, False, [tok("qT")] + d["ktoks"], [btok[b]])
                    mm(v3(banks[b][0:bn, 0:4 * nq], 4), maskm1[i][0:128, b0:b0 + bn], I4B[0:128, 0:4, 0:nq],
                       False, True, [tok("mask"), tok("I4B")], [btok[b]])
                    s = nxt("pT", 2, 0)
                    act(pT[s][0:bn, 0:4 * nq], banks[b][0:bn, 0:4 * nq], AF.Exp, [btok[b]], [tok("pT%d" % s)],
                        scale=0.125)
                    st3[k] = s

                def p3_back(k):
                    g, kb, b0, bn = steps3[k]
                    s = st3[k]
                    mm(banks[ob + g][0:128, 0:4 * nq], v_sb[0:bn, b0 // 128, 0:128], pT[s][0:bn, 0:4 * nq],
                       kb == 0, kb == nb - 1, [tok("pT%d" % s)] + d["vtoks"], [btok[ob + g]])
                    mm(banks[4 + g][0:128, 0:4 * nq], ones_bf[0:bn, 0:128], pT[s][0:bn, 0:4 * nq],
                       kb == 0, kb == nb - 1, [tok("pT%d" % s), tok("ones")], [btok[4 + g]])

                def normalise(g):
                    rs = slice(g * 64, (g + 1) * 64)
                    act(lnd[rs, 0:4 * nq], banks[4 + g][rs, 0:4 * nq], AF.Ln, [btok[4 + g]], [tok("dg")])
                    act(rden[rs, 0:4 * nq], lnd[rs, 0:4 * nq], AF.Exp, [tok("dg")], [tok("rden")], scale=-1.0)
                    tt("dve", rden[rs, 0:4 * nq], banks[ob + g][rs, 0:4 * nq], rden[rs, 0:4 * nq], ALU.mult,
                       [btok[ob + g], tok("rden")], [tok("rden")])

                for k in range(len(steps3) + 1):
                    if k < len(steps3):
                        p3_front(k)
                    if k >= 1:
                        p3_back(k - 1)
                        if steps3[k - 1][1] == nb - 1:
                            normalise(steps3[k - 1][0])
                tt("dve", sga[:, 0:4, q0:q0 + nq], v3(rden[:, 0:4 * nq], 4), sga[:, 0:4, q0:q0 + nq], ALU.mult,
                   [tok("rden"), tok("sga")], [tok("sga")])

        def odd_tile(S, t, l, o, skip_norm=False, next_norm=None):
            n = S.TT
            if not skip_norm:
                norm_mod(S, t, l)
            ntg = (n + 127) // 128
            stk = tok("lnst")

            def u_chunk(c):
                b = proj_fm(w_in, None, c * 128, 128, n)
                act(u_sb[:, c, 0:n], banks[b][:, 0:n], AF.Gelu_apprx_tanh, [btok[b]], [tok("u")])

            def g_chunk(c):
                b = proj_fm(w_in, None, 2048 + c * 128, 128, n)
                act(sg_sb[:, c, 0:n], banks[b][:, 0:n], AF.Silu, [btok[b]], [tok("sg")])

            def ln1(tg):
                m = min(128, n - tg * 128)
                for half in range(2):
                    b = PPB[nxt("pp", 6, 0)]
                    for kc in range(8):
                        mm(banks[b][0:m, 0:512], hT[:, kc, tg * 128:tg * 128 + m],
                           w_in[:, kc, 1024 + half * 512:1024 + (half + 1) * 512], kc == 0, kc == 7,
                           [wtk(1024 + half * 512), tok("hT")], [btok[b]])
                    act(vtm[0:m, half * 512:(half + 1) * 512], banks[b][0:m, 0:512], AF.Gelu_apprx_tanh, [btok[b]],
                        [tok("vtm")])
                ts("dve", vnf[0:m, :], vtm[0:m, :], 1.0 / D, None, ALU.mult, ALU.add, [tok("vtm")], [tok("sq"), stk],
                   accum=st_mean[0:m, 0:1])
                ts("dve", vtm[0:m, :], vtm[0:m, :], st_mean[0:m, 0:1], None, ALU.subtract, None, [tok("vtm"), stk],
                   [tok("vtm")])

            def ln2(tg):
                m = min(128, n - tg * 128)
                act(vnf[0:m, :], vtm[0:m, :], AF.Square, [tok("vtm")], [tok("sq"), stk], scale=1.0 / 32.0,
                    accum=st_var[0:m, 0:1])

            def ln3(tg):
                m = min(128, n - tg * 128)
                act(st_sd[0:m, 0:1], st_var[0:m, 0:1], AF.Sqrt, [stk], [stk], bias=eps_sb[0:m, 0:1])
                recip(st_rstd[0:m, 0:1], st_sd[0:m, 0:1], [stk], [stk])
                stt("dve", vnf[0:m, :], vtm[0:m, :], st_rstd[0:m, 0:1], lng_b[0:m, :], ALU.mult, ALU.mult,
                    [tok("vtm"), stk, tok("lnp")], [tok("sq")])
                tt("dve", vnf[0:m, :], vnf[0:m, :], lnb_b[0:m, :], ALU.add, [tok("sq"), tok("lnp")], [tok("sq")])

            def ln4(tg):
                m = min(128, n - tg * 128)
                cp("act", vn[0:m, tg, :], vnf[0:m, :], [tok("sq")], [tok("vn")])
                if S.sample:
                    dma(o_cvs[o], vnf[0:m, :], [tok("sq")], [], "o_cv")

            chunks = [("u", c) for c in range(8)] + [("g", c) for c in range(8)]
            ci = [0]

            def some_chunks(k):
                for _ in range(k):
                    if ci[0] < len(chunks):
                        kind, c = chunks[ci[0]]
                        ci[0] += 1
                        (u_chunk if kind == "u" else g_chunk)(c)

            for tg in range(ntg):
                if tg > 0:
                    some_chunks(4)
                ln1(tg)
                some_chunks(2)
                ln2(tg)
                some_chunks(2)
                ln3(tg)
                some_chunks(4 if ntg > 1 else 12)
                ln4(tg)
            some_chunks(16)
            run_hook('after_in')
            tt("dve", u_sb[:, :, 0:n], u_sb[:, :, 0:n], sg_sb[:, :, 0:n], ALU.mult, [tok("u"), tok("sg")], [tok("u")])
            for tg in range(ntg):
                m = min(128, n - tg * 128)
                for half in range(2):
                    b = nxt("px", 2, 2)
                    for gi in range(4):
                        gidx = half * 4 + gi
                        mm(banks[b][:, gi * 128:gi * 128 + m], vn[0:m, tg, gidx * 128:(gidx + 1) * 128],
                           wmT[0:m, gidx, 0:m], True, True, [tok("vn"), tok("wmT")], [btok[b]])
                    pv = v3(banks[b][:, 0:512], 4)[:, :, 0:m]
                    tt("dve", v3(s_tmp[:, 0:512], 4)[:, :, 0:m], pv, bs_b[:, half * 4:half * 4 + 4, 0:m], ALU.add,
                       [btok[b], tok("lnp")], [tok("tmpn0"), tok("tmpn1")])
                    tt("dve", gated[:, half * 4:half * 4 + 4, tg * 128:tg * 128 + m], v3(s_tmp[:, 0:512], 4)[:, :, 0:m],
                       u_sb[:, half * 4:half * 4 + 4, tg * 128:tg * 128 + m], ALU.mult,
                       [tok("tmpn0"), tok("tmpn1"), tok("u")], [tok("gated")])
            rhs = [(gated[:, kc, 0:n], [tok("gated")]) for kc in range(8)]
            if next_norm is not None:
                norm_mod(*next_norm)
            out_proj_residual(S, t, l, tok("w_out_c0"), rhs)
            run_hook("after_out")

        fin = [0]

        def final_tile(S, t):
            n = S.TT
            xt = S.x(t)
            fi = fin[0]
            fin[0] = 1 - fi
            rs_, rtok = rms_stats(S, t, alt=(fi == 1))
            yst = yst2[fi]
            for kc in range(8):
                stt("dve", yst[:, kc, 0:n], xt[:, kc, :], fg[:, kc:kc + 1], rs_[:, 0:n], ALU.mult, ALU.mult,
                    [S.xtok[t], rtok], [tok("yst%d" % fi)])
            if S.sample:
                dma(o_ys.rearrange("(kc p) t -> p kc t", p=128), yst[:, :, 0:n], [tok("yst%d" % fi)], [], "o_y%d" % fi)
            else:
                dma(o_yp[S.pi, :, t * TT:(t + 1) * TT].rearrange("(kc p) t -> p kc t", p=128), yst[:, :, 0:n],
                    [tok("yst%d" % fi)], [], "o_y%d" % fi)

        LAYERS = [(pi, l) for pi in range(2) for l in range(4)]

        def issue_in(idx):
            if idx >= len(LAYERS):
                return
            pi, l = LAYERS[idx]
            if l % 2 == 0:
                issue_w(d_wine[l // 2], w_in, EVEN_IN, "w_in")
            else:
                issue_w(d_wino[l // 2], w_in, 3 * D, "w_in")

        def issue_out(idx):
            if idx >= len(LAYERS):
                return
            pi, l = LAYERS[idx]
            issue_w((d_woute if l % 2 == 0 else d_wouto)[l // 2], w_out, D, "w_out")

        def pool_dma(out, in_, writes, key):
            P.add("pool", (lambda o, i_: (lambda e: e.dma_start(out=o, in_=i_)))(out, in_), [], writes, dma_key=key)

        def main_schedule():
            issue_in(0)
            issue_out(0)
            for pi in range(2):
                SP_ = mkseq(pi)
                seqs = [SP_] + ([SS] if pi == 1 else [])
                barrier()
                stage_mark("xload%d" % pi)
                for t in range(NT):
                    dma(x_sb[:, :, t * TT:(t + 1) * TT],
                        d_xp[pi, :, t * TT:(t + 1) * TT].rearrange("(kc p) t -> p kc t", p=128),
                        [], [SP_.xtok[t]], "xl%d" % t)
                if pi == 1:
                    dma(xs_sb[:, :, :], d_xs.rearrange("(kc p) t -> p kc t", p=128), [], [SS.xtok[0]], "xs")
                for l in range(4):
                    idx = pi * 4 + l
                    barrier()
                    stage_mark("layer p%d l%d" % (pi, l))
                    last_S = seqs[-1]
                    if l % 2 == 0:
                        e = l // 2
                        for r_ in range(4):
                            ts("dve", I4B[:, r_, :], ident_f[:, :], 30000.0, None, ALU.mult, None, [], [tok("I4B")])
                        memset("dve", qTz.rearrange("p g r t -> p (g r t)"), 0.0, [], [tok("qT")])
                        memset("dve", qiTz.rearrange("p h t -> p (h t)"), 0.0, [], [tok("qiT")])
                        for S in seqs:
                            if S.sample:
                                pool_dma(kT[:, 0:PAST], d_ckT[e], [tok("kT%d" % i) for i in range(4)], "c_k")
                                pool_dma(kiT[0:64, 0:PAST], d_ckiT[e], [tok("kiT%d" % i) for i in range(4)], "c_ki")
                                pool_dma(kiT[64:128, 0:PAST], d_ckiT[e], [tok("kiT%d" % i) for i in range(4)], "c_ki2")
                                pool_dma(v_sb[:, 0:8, :], d_cv[e].rearrange("(kb p) c -> p kb c", p=128),
                                         [tok("v%d" % i) for i in range(4)], "c_v")
                                dma(zp_s[:, :, 0:2], d_cconv[e], [], SS.zptok, "cconv")
                            else:
                                for c in range(4):
                                    memset("pool", zp_p[:, c, 0:2], 0.0, [], [S.zptok[c]])
                            for t in range(S.nt):
                                stage_mark("even tile p%d l%d t%d" % (pi, l, t))
                                if S is last_S and t == S.nt - 1:
                                    HOOK["after_in"] = (lambda i_=idx: issue_in(i_ + 1))
                                    HOOK["after_out"] = (lambda i_=idx: issue_out(i_ + 1))
                                even_tile(S, t, l, e, skip_norm=(t > 0),
                                          next_norm=((S, t + 1, l) if t + 1 < S.nt else None))
                    else:
                        o = l // 2
                        dma(v3(wst_f, 8), d_wsT[o], [], [tok("wst")], "wst")
                        tt("dve", wmT[:, :, :], v3(wst_f, 8), tril_f[:, :].unsqueeze(1).to_broadcast([128, 8, 128]),
                           ALU.mult, [tok("wst")], [tok("wmT")])
                        dma(lng_b, d_lng[o:o + 1, :].partition_broadcast(128), [], [tok("lnp")], "lnp")
                        dma(lnb_b, d_lnb[o:o + 1, :].partition_broadcast(128), [], [tok("lnp2")], "lnp2")
                        dma(bs_b.rearrange("p a b -> p (a b)"), d_bs[o:o + 1, :].partition_broadcast(128), [],
                            [tok("lnp3")], "lnp3")
                        barrier()
                        for S in seqs:
                            for t in range(S.nt):
                                stage_mark("odd tile p%d l%d t%d" % (pi, l, t))
                                if S is last_S and t == S.nt - 1:
                                    HOOK["after_in"] = (lambda i_=idx: issue_in(i_ + 1))
                                    HOOK["after_out"] = (lambda i_=idx: issue_out(i_ + 1))
                                odd_tile(S, t, l, o, skip_norm=(t > 0),
                                         next_norm=((S, t + 1, l) if t + 1 < S.nt else None))
                barrier()
                for S in seqs:
                    for t in range(S.nt):
                        stage_mark("final p%d t%d" % (pi, t))
                        final_tile(S, t)

        try:
            main_schedule()
        except _Stop:
            pass
        barrier()
        P.emit()
    return nc


_STAGE_LIMIT = None
_SUB_MARKS = False
_CAST_ENG = "act"

_NC_CACHE = {}


def kernel(x_prompt, x_sample, cache_a_k, cache_a_v, cache_a_kidx, state_b_conv, c_prompt, c_sample,
           ada_w, ada_b, norm_g, ev_w_in, ev_conv_w, ev_w_out, od_w_in, od_ws, od_bs, od_ln_g, od_ln_b,
           od_w_out, final_g):
    f = lambda a: np.ascontiguousarray(np.asarray(a, dtype=np.float32))
    x_prompt = np.asarray(x_prompt, np.float32); x_sample = np.asarray(x_sample, np.float32)
    perm = np.arange(512).reshape(2, 4, 64).transpose(1, 0, 2).reshape(-1)
    cols = np.arange(EVEN_IN)
    cols[0:512] = perm
    cols[768:1280] = 768 + perm
    w_in_e = f(np.asarray(ev_w_in)[:, :, cols])
    rows = np.arange(D)
    rows[0:512] = perm
    w_out_e = f(np.asarray(ev_w_out)[:, rows, :])
    shared = {
        "ada_w": f(ada_w),
        "ada_b": f(np.asarray(ada_b).reshape(4, 24, 128).transpose(2, 0, 1)),
        "norm_g": f(np.asarray(norm_g).reshape(4, 8, 128).transpose(2, 0, 1)),
        "final_g": f(np.asarray(final_g).reshape(8, 128).T),
        "w_in_e": w_in_e,
        "conv_w": f(np.asarray(ev_conv_w).reshape(2, 3, 4, 128).transpose(3, 0, 2, 1)),
        "w_out_e": w_out_e,
        "w_in_o": f(od_w_in),
        "wsT": f(np.asarray(od_ws).transpose(0, 3, 1, 2)),
        "bs": f(np.asarray(od_bs).reshape(2, D)),
        "ln_g": f(od_ln_g), "ln_b": f(od_ln_b),
        "w_out_o": f(od_w_out),
        "ident": np.eye(128, dtype=np.float32),
        "trilT": np.triu(np.ones((128, 128), np.float32)),
        "pow2": np.tile(np.array([2.0 ** -(k + 1) for k in range(NIT)] + [2.0 ** -NIT], np.float32)[None, :], (128, 1)),
    }
    ck = np.asarray(cache_a_k, np.float32); cv = np.asarray(cache_a_v, np.float32)
    cki = np.asarray(cache_a_kidx, np.float32); cst = np.asarray(state_b_conv, np.float32)
    in_maps = []
    for i in range(8):
        m = dict(shared)
        m["xp"] = f(x_prompt[2 * i:2 * i + 2].transpose(0, 2, 1))
        m["xs"] = f(x_sample[i].T)
        m["cT"] = f(np.stack([np.asarray(c_prompt)[2 * i], np.asarray(c_prompt)[2 * i + 1], np.asarray(c_sample)[i]], 1))
        m["ckT"] = f(ck[:, i].reshape(2, PAST, 128).transpose(0, 2, 1))
        m["cv"] = f(cv[:, i].reshape(2, PAST, 128))
        m["ckiT"] = f(cki[:, i].transpose(0, 2, 1))
        m["cconv"] = f(cst[:, i].reshape(2, 2, 4, 128).transpose(0, 3, 2, 1))
        in_maps.append(m)
    if "nc" not in _NC_CACHE:
        _NC_CACHE["nc"] = build_nc()
    res = run_bass_kernel_spmd(_NC_CACHE["nc"], in_maps, core_ids=list(range(8)))
    R = res.results
    y_p = np.empty((16, SEQ, D), np.float32); y_s = np.empty((8, DEC, D), np.float32)
    k_p = np.empty((2, 16, SEQ, 2, 64), np.float32); v_p = np.empty((2, 16, SEQ, 2, 64), np.float32)
    ki_p = np.empty((2, 16, SEQ, 64), np.float32); conv_p = np.empty((2, 16, 2, 512), np.float32)
    k_s = np.empty((2, 8, DEC, 2, 64), np.float32); v_s = np.empty((2, 8, DEC, 2, 64), np.float32)
    ki_s = np.empty((2, 8, DEC, 64), np.float32); conv_s = np.empty((2, 8, 2, 512), np.float32)
    cv_s = np.empty((2, 8, DEC, D), np.float32)
    for i in range(8):
        r = R[i]
        for s in range(2):
            b = 2 * i + s
            y_p[b] = r["yp"][s].T
            k_p[:, b] = r["kp"][:, s].transpose(0, 2, 1).reshape(2, SEQ, 2, 64)
            v_p[:, b] = r["vp"][:, s].reshape(2, SEQ, 2, 64)
            ki_p[:, b] = r["kip"][:, s].transpose(0, 2, 1)
            conv_p[:, b] = r["convp"][:, s].transpose(0, 3, 2, 1).reshape(2, 2, 512)
        y_s[i] = r["ys"].T
        k_s[:, i] = r["ks"].transpose(0, 2, 1).reshape(2, DEC, 2, 64)
        v_s[:, i] = r["vs"].reshape(2, DEC, 2, 64)
        ki_s[:, i] = r["kis"].transpose(0, 2, 1)
        conv_s[:, i] = r["convs"].transpose(0, 3, 2, 1).reshape(2, 2, 512)
        cv_s[:, i] = r["cvs"]
    return (y_p, y_s, k_p, v_p, ki_p, conv_p, k_s, v_s, ki_s, conv_s, cv_s)
```
